# Optimizing a Trainium2 kernel written in Bass

```python
import math
import jax, jax.numpy as jnp
from jax import lax
import numpy as np

D_MODEL = 1024
BATCH = 4
SEQ = 4096
DEPTH = 2
DEC_BATCH = 128
DEC_SEQ = 8
PAST_LEN = 2048
PAGE_SIZE = 128

HEAD_DIM = 64
CONV_CH = D_MODEL // 2
CONV_W = 31
SB_HEADS = (D_MODEL // 2) // HEAD_DIM
SB_DIM = SB_HEADS * HEAD_DIM
NSA_HEADS = D_MODEL // HEAD_DIM
NSA_GQA = 4
NSA_KVH = NSA_HEADS // NSA_GQA
NSA_KV_DIM = NSA_KVH * HEAD_DIM
CMP_BLK = 64
CMP_HID = 256
SEL_BLK = CMP_BLK
SEL_TOPN = 16
FORCE_SCORE = 1.0e4
WINDOW = 512
MEM_LEN = 256
MEM_HEADS = 4
MEM_HD = D_MODEL // MEM_HEADS
D_FF = -(-8 * D_MODEL // (3 * 256)) * 256
ROPE_THETA = 10000.0
NORM_EPS = 1e-6
Q_BLK = 128
SEL_Q_ROWS = 128
NEG_INF = -1e30
N_EVEN = (DEPTH + 1) // 2
N_ODD = DEPTH // 2
IN_EVEN = 2 * CONV_CH + 3 * SB_DIM
IN_ODD = NSA_HEADS * HEAD_DIM + 6 * NSA_KV_DIM + 3 * NSA_HEADS

kernel_name = 'hybrid_conv_stickbreak_nsa_memory_step'


def rmsnorm(x, g):
    xf = x.astype(jnp.float32)
    y = xf * lax.rsqrt(jnp.mean(xf * xf, axis=-1, keepdims=True) + NORM_EPS)
    return (y * g.astype(jnp.float32)).astype(x.dtype)


def layernorm(x, g, b):
    xf = x.astype(jnp.float32)
    mu = jnp.mean(xf, axis=-1, keepdims=True)
    var = jnp.mean(jnp.square(xf - mu), axis=-1, keepdims=True)
    y = (xf - mu) * lax.rsqrt(var + NORM_EPS)
    return (y * g.astype(jnp.float32) + b.astype(jnp.float32)).astype(x.dtype)


def rope(x, pos):
    half = HEAD_DIM // 2
    inv = ROPE_THETA ** (-jnp.arange(half, dtype=jnp.float32) / half)
    ang = pos.astype(jnp.float32)[:, None] * inv[None, :]
    cos = jnp.cos(ang)[:, None, :]
    sin = jnp.sin(ang)[:, None, :]
    xf = x.astype(jnp.float32)
    x1, x2 = xf[..., :half], xf[..., half:]
    return jnp.concatenate([x1 * cos - x2 * sin, x2 * cos + x1 * sin], axis=-1).astype(x.dtype)


def masked_softmax(s, mask):
    s = jnp.where(mask, s, NEG_INF)
    m = jnp.max(s, axis=-1, keepdims=True)
    p = jnp.where(mask, jnp.exp(s - m), 0.0)
    return p / jnp.maximum(jnp.sum(p, axis=-1, keepdims=True), 1e-30)


def sweep_queries(fn, q, q_pos):
    B, T = q.shape[:2]
    if T <= Q_BLK or T % Q_BLK:
        return fn(q, q_pos)
    nb = T // Q_BLK
    qb = jnp.moveaxis(q.reshape(B, nb, Q_BLK, *q.shape[2:]), 1, 0)
    out = lax.map(lambda a: fn(a[0], a[1]), (qb, q_pos.reshape(nb, Q_BLK)))
    return jnp.moveaxis(out, 0, 1).reshape(B, T, *out.shape[3:])


def gather_pages(pool, page_table):
    g = pool[page_table]
    return g.reshape(g.shape[0], g.shape[1] * g.shape[2], *pool.shape[2:])


def conformer_conv(u, buf, w_dw, b_dw, ln_g, ln_b):
    glu = u[..., :CONV_CH] * jax.nn.sigmoid(u[..., CONV_CH:])
    hp = jnp.concatenate([buf.astype(glu.dtype), glu], axis=1)
    y = lax.conv_general_dilated(hp, w_dw[:, None, :].astype(hp.dtype), (1,), 'VALID',
                                 dimension_numbers=('NWC', 'WIO', 'NWC'),
                                 feature_group_count=CONV_CH) + b_dw
    y = layernorm(y, ln_g, ln_b)
    return jax.nn.silu(y), hp[:, -(CONV_W - 1):]


def stick_breaking(q, k, v, q_pos, k_pos):
    z = jnp.einsum('bqhd,bkhd->bhqk', q, k).astype(jnp.float32) * (HEAD_DIM ** -0.5)
    mask = k_pos[None, :] < q_pos[:, None]
    log_stay = jnp.where(mask, jax.nn.log_sigmoid(-z), 0.0)
    log_after = lax.cumsum(log_stay, axis=3, reverse=True) - log_stay
    w = jnp.where(mask, jnp.exp(jax.nn.log_sigmoid(z) + log_after), 0.0)
    return jnp.einsum('bhqk,bkhd->bqhd', w.astype(v.dtype), v)


def even_mixer(a, w_in, w_out, conv_buf, conv_w, conv_b, ln_g, ln_b, past_k, past_v):
    B, T, _ = a.shape
    P = past_k.shape[1]
    u = a @ w_in
    conv_in = u[..., :2 * CONV_CH]
    q, k, v = jnp.split(u[..., 2 * CONV_CH:], 3, axis=-1)
    q = q.reshape(B, T, SB_HEADS, HEAD_DIM)
    k = k.reshape(B, T, SB_HEADS, HEAD_DIM)
    v = v.reshape(B, T, SB_HEADS, HEAD_DIM)
    y_conv, new_buf = conformer_conv(conv_in, conv_buf, conv_w, conv_b, ln_g, ln_b)
    k_all = jnp.concatenate([past_k.astype(k.dtype), k], axis=1)
    v_all = jnp.concatenate([past_v.astype(v.dtype), v], axis=1)
    k_pos = jnp.arange(P + T, dtype=jnp.int32)
    q_pos = P + jnp.arange(T, dtype=jnp.int32)
    o = sweep_queries(lambda qb, pb: stick_breaking(qb, k_all, v_all, pb, k_pos), q, q_pos)
    y = jnp.concatenate([y_conv, o.reshape(B, T, SB_DIM)], axis=-1) @ w_out
    return y, new_buf, k, v


def compress_blocks(kv, pe, w1, w2):
    B, Tk, G, d = kv.shape
    nb = -(-Tk // CMP_BLK)
    kv = jnp.pad(kv, ((0, 0), (0, nb * CMP_BLK - Tk), (0, 0), (0, 0)))
    blk = kv.reshape(B, nb, CMP_BLK, G, d) + pe[None, None, :, None, :]
    hid = jax.nn.gelu(jnp.einsum('bnlgd,ldh->bngh', blk, w1))
    return jnp.einsum('bngh,he->bnge', hid, w2)


def selected_branch(q, kb, vb, sel_idx, q_pos):
    B, T = q.shape[:2]
    G = kb.shape[1]
    n = sel_idx.shape[-1]
    qc = math.gcd(T, max(1, SEL_Q_ROWS // B))
    nc = T // qc
    bi = jnp.arange(B)[:, None, None, None]
    gi = jnp.arange(G)[None, :, None, None]
    offs = jnp.arange(SEL_BLK, dtype=jnp.int32)

    def one_chunk(args):
        qq, idx, qp = args
        kg = kb[bi, gi, idx].reshape(B, G, qc, n * SEL_BLK, HEAD_DIM)
        vg = vb[bi, gi, idx].reshape(B, G, qc, n * SEL_BLK, HEAD_DIM)
        kpos = (idx[..., None] * SEL_BLK + offs).reshape(B, G, qc, n * SEL_BLK)
        s = jnp.einsum('bqgrd,bgqkd->bgrqk', qq, kg).astype(jnp.float32) * (HEAD_DIM ** -0.5)
        mask = (kpos <= qp[None, None, :, None])[:, :, None]
        p = masked_softmax(s, mask)
        return jnp.einsum('bgrqk,bgqkd->bqgrd', p.astype(vg.dtype), vg)

    qs = jnp.moveaxis(q.reshape(B, nc, qc, *q.shape[2:]), 1, 0)
    ids = jnp.moveaxis(sel_idx.reshape(B, G, nc, qc, n), 2, 0)
    out = lax.map(one_chunk, (qs, ids, q_pos.reshape(nc, qc)))
    return jnp.moveaxis(out, 0, 1).reshape(q.shape)


def window_attn(q, k, v, q_pos, k_pos):
    s = jnp.einsum('bqgrd,bkgd->bgrqk', q, k).astype(jnp.float32) * (HEAD_DIM ** -0.5)
    diff = q_pos[:, None] - k_pos[None, :]
    mask = (diff >= 0) & (diff < WINDOW) & (k_pos[None, :] >= 0)
    p = masked_softmax(s, mask)
    return jnp.einsum('bgrqk,bkgd->bqgrd', p.astype(v.dtype), v)


def window_branch(q, k, v, q_pos, k_pos):
    B, T = q.shape[:2]
    Tk = k.shape[1]
    if T <= Q_BLK or T % Q_BLK:
        return window_attn(q, k, v, q_pos, k_pos)
    nb = T // Q_BLK
    pad = ((0, 0), (WINDOW, 0), (0, 0), (0, 0))
    kp = jnp.pad(k, pad)
    vp = jnp.pad(v, pad)
    kpos = jnp.concatenate([jnp.full((WINDOW,), -1, jnp.int32), k_pos])
    band = WINDOW + Q_BLK
    base = Tk - T

    def one_block(args):
        j, qq, qp = args
        start = j * Q_BLK + base
        kk = lax.dynamic_slice_in_dim(kp, start, band, axis=1)
        vv = lax.dynamic_slice_in_dim(vp, start, band, axis=1)
        pp = lax.dynamic_slice_in_dim(kpos, start, band, axis=0)
        return window_attn(qq, kk, vv, qp, pp)

    qs = jnp.moveaxis(q.reshape(B, nb, Q_BLK, *q.shape[2:]), 1, 0)
    out = lax.map(one_block, (jnp.arange(nb, dtype=jnp.int32), qs, q_pos.reshape(nb, Q_BLK)))
    return jnp.moveaxis(out, 0, 1).reshape(q.shape)


def odd_mixer(a, w_in, w_out, pe_k, w1_k, w2_k, pe_v, w1_v, w2_v,
              past_ck, past_cv, past_sk, past_sv, win_k, win_v, win_keep):
    B, T, _ = a.shape
    P = past_ck.shape[1]
    Wb = win_k.shape[1]
    u = a @ w_in
    d_q = NSA_HEADS * HEAD_DIM
    kvs = jnp.split(u[..., d_q:d_q + 6 * NSA_KV_DIM], 6, axis=-1)
    ck, cv, sk, sv, wk, wv = [t.reshape(B, T, NSA_KVH, HEAD_DIM) for t in kvs]
    gates = jax.nn.sigmoid(u[..., d_q + 6 * NSA_KV_DIM:].astype(jnp.float32))
    gates = gates.reshape(B, T, NSA_KVH, NSA_GQA, 3)
    q_pos = P + jnp.arange(T, dtype=jnp.int32)
    q = rope(u[..., :d_q].reshape(B, T, NSA_HEADS, HEAD_DIM), q_pos)
    q = q.reshape(B, T, NSA_KVH, NSA_GQA, HEAD_DIM)
    sk = rope(sk, q_pos)
    wk = rope(wk, q_pos)
    Tk = P + T
    ck_all = jnp.concatenate([past_ck.astype(ck.dtype), ck], axis=1)
    cv_all = jnp.concatenate([past_cv.astype(cv.dtype), cv], axis=1)
    kc = compress_blocks(ck_all, pe_k, w1_k, w2_k)
    vc = compress_blocks(cv_all, pe_v, w1_v, w2_v)
    nb = kc.shape[1]
    blk_end = jnp.arange(nb, dtype=jnp.int32) * CMP_BLK + (CMP_BLK - 1)
    kc = rope(kc, blk_end)
    s = jnp.einsum('btgrd,bngd->bgrtn', q, kc).astype(jnp.float32) * (HEAD_DIM ** -0.5)
    p_cmp = masked_softmax(s, blk_end[None, :] <= q_pos[:, None])
    o_cmp = jnp.einsum('bgrtn,bngd->btgrd', p_cmp.astype(vc.dtype), vc)
    blk = jnp.arange(nb, dtype=jnp.int32)[None, :]
    cur = (q_pos // SEL_BLK)[:, None]
    forced = (blk == 0) | (blk == cur) | (blk == cur - 1)
    score = jnp.where(blk > cur, -1.0, jnp.where(forced, FORCE_SCORE, jnp.sum(p_cmp, axis=2)))
    _, sel_idx = lax.top_k(score, min(SEL_TOPN, nb))
    pad = nb * SEL_BLK - Tk

    def to_blocks(t):
        t = jnp.pad(t, ((0, 0), (0, pad), (0, 0), (0, 0)))
        return jnp.transpose(t.reshape(B, nb, SEL_BLK, NSA_KVH, HEAD_DIM), (0, 3, 1, 2, 4))

    sk_all = jnp.concatenate([past_sk.astype(sk.dtype), sk], axis=1)
    sv_all = jnp.concatenate([past_sv.astype(sv.dtype), sv], axis=1)
    o_sel = selected_branch(q, to_blocks(sk_all), to_blocks(sv_all), sel_idx, q_pos)
    wk_all = jnp.concatenate([win_k.astype(wk.dtype), wk], axis=1)
    wv_all = jnp.concatenate([win_v.astype(wv.dtype), wv], axis=1)
    w_pos = (P - Wb) + jnp.arange(Wb + T, dtype=jnp.int32)
    o_win = window_branch(q, wk_all, wv_all, q_pos, w_pos)
    o = gates[..., 0:1] * o_cmp + gates[..., 1:2] * o_sel + gates[..., 2:3] * o_win
    y = o.astype(a.dtype).reshape(B, T, d_q) @ w_out
    return y, ck, cv, sk, sv, wk_all[:, -win_keep:], wv_all[:, -win_keep:]


def memory_attn(a, mk, mv, w_q, w_o):
    B, T, _ = a.shape
    q = (a @ w_q).reshape(B, T, MEM_HEADS, MEM_HD)
    s = jnp.einsum('bthd,bmhd->bhtm', q, mk.astype(q.dtype)).astype(jnp.float32) * (MEM_HD ** -0.5)
    p = jax.nn.softmax(s, axis=-1)
    o = jnp.einsum('bhtm,bmhd->bthd', p.astype(a.dtype), mv.astype(a.dtype)).reshape(B, T, D_MODEL)
    return o @ w_o


def swiglu(a, w_gate, w_up, w_down):
    return (jax.nn.silu(a @ w_gate) * (a @ w_up)) @ w_down


def setup_inputs(seed: int = 0) -> dict:
    key = jax.random.key(seed)
    ks = jax.random.split(key, 40)

    def nrm(i, shape, scale):
        return jax.random.normal(ks[i], shape, jnp.float32) * scale

    n_pages = PAST_LEN // PAGE_SIZE
    n_used = DEC_BATCH * n_pages
    n_pool = -(-5 * n_used // 4)
    win_buf = min(WINDOW, PAST_LEN)
    page_table = jax.random.permutation(ks[14], n_pool)[:n_used].reshape(DEC_BATCH, n_pages).astype(jnp.int32)
    paged_sb = (N_EVEN, n_pool, PAGE_SIZE, SB_HEADS, HEAD_DIM)
    paged_nsa = (N_ODD, n_pool, PAGE_SIZE, NSA_KVH, HEAD_DIM)
    win_shape = (N_ODD, DEC_BATCH, win_buf, NSA_KVH, HEAD_DIM)
    mem_shape = (DEPTH, DEC_BATCH, MEM_LEN, MEM_HEADS, MEM_HD)
    d_in = D_MODEL ** -0.5
    return {
        'x_prompt': nrm(0, (BATCH, SEQ, D_MODEL), 1.0),
        'x_sample': nrm(1, (DEC_BATCH, DEC_SEQ, D_MODEL), 1.0),
        'mem_prompt': nrm(2, (BATCH, MEM_LEN, D_MODEL), 1.0),
        'cache_sb_k': nrm(3, paged_sb, 1.0),
        'cache_sb_v': nrm(4, paged_sb, 1.0),
        'state_conv': nrm(5, (N_EVEN, DEC_BATCH, CONV_W - 1, CONV_CH), 0.5),
        'cache_nsa_cmp_k': nrm(6, paged_nsa, 1.0),
        'cache_nsa_cmp_v': nrm(7, paged_nsa, 1.0),
        'cache_nsa_sel_k': nrm(8, paged_nsa, 1.0),
        'cache_nsa_sel_v': nrm(9, paged_nsa, 1.0),
        'cache_nsa_win_k': nrm(10, win_shape, 1.0),
        'cache_nsa_win_v': nrm(11, win_shape, 1.0),
        'cache_mem_k': nrm(12, mem_shape, 1.0),
        'cache_mem_v': nrm(13, mem_shape, 1.0),
        'page_table': page_table,
        'norm_mix': 1.0 + nrm(15, (DEPTH, D_MODEL), 0.01),
        'norm_mem': 1.0 + nrm(16, (DEPTH, D_MODEL), 0.01),
        'norm_ffn': 1.0 + nrm(17, (DEPTH, D_MODEL), 0.01),
        'final_norm': 1.0 + nrm(18, (D_MODEL,), 0.01),
        'w_in_even': nrm(19, (N_EVEN, D_MODEL, IN_EVEN), d_in),
        'w_in_odd': nrm(20, (N_ODD, D_MODEL, IN_ODD), d_in),
        'w_mix_out': nrm(21, (DEPTH, D_MODEL, D_MODEL), d_in),
        'conv_w': nrm(22, (N_EVEN, CONV_W, CONV_CH), CONV_W ** -0.5),
        'conv_b': nrm(23, (N_EVEN, CONV_CH), 0.01),
        'conv_ln_g': 1.0 + nrm(24, (N_EVEN, CONV_CH), 0.01),
        'conv_ln_b': nrm(25, (N_EVEN, CONV_CH), 0.01),
        'cmp_pe_k': nrm(26, (N_ODD, CMP_BLK, HEAD_DIM), 0.1),
        'cmp_w1_k': nrm(27, (N_ODD, CMP_BLK, HEAD_DIM, CMP_HID), (CMP_BLK * HEAD_DIM) ** -0.5),
        'cmp_w2_k': nrm(28, (N_ODD, CMP_HID, HEAD_DIM), CMP_HID ** -0.5),
        'cmp_pe_v': nrm(29, (N_ODD, CMP_BLK, HEAD_DIM), 0.1),
        'cmp_w1_v': nrm(30, (N_ODD, CMP_BLK, HEAD_DIM, CMP_HID), (CMP_BLK * HEAD_DIM) ** -0.5),
        'cmp_w2_v': nrm(31, (N_ODD, CMP_HID, HEAD_DIM), CMP_HID ** -0.5),
        'w_mem_q': nrm(32, (DEPTH, D_MODEL, D_MODEL), d_in),
        'w_mem_k': nrm(33, (DEPTH, D_MODEL, D_MODEL), d_in),
        'w_mem_v': nrm(34, (DEPTH, D_MODEL, D_MODEL), d_in),
        'w_mem_o': nrm(35, (DEPTH, D_MODEL, D_MODEL), d_in),
        'w_ffn_gate': nrm(36, (DEPTH, D_MODEL, D_FF), d_in),
        'w_ffn_up': nrm(37, (DEPTH, D_MODEL, D_FF), d_in),
        'w_ffn_down': nrm(38, (DEPTH, D_FF, D_MODEL), D_FF ** -0.5),
    }


def reference(x_prompt, x_sample, mem_prompt, cache_sb_k, cache_sb_v, state_conv,
              cache_nsa_cmp_k, cache_nsa_cmp_v, cache_nsa_sel_k, cache_nsa_sel_v,
              cache_nsa_win_k, cache_nsa_win_v, cache_mem_k, cache_mem_v, page_table,
              norm_mix, norm_mem, norm_ffn, final_norm, w_in_even, w_in_odd, w_mix_out,
              conv_w, conv_b, conv_ln_g, conv_ln_b,
              cmp_pe_k, cmp_w1_k, cmp_w2_k, cmp_pe_v, cmp_w1_v, cmp_w2_v,
              w_mem_q, w_mem_k, w_mem_v, w_mem_o, w_ffn_gate, w_ffn_up, w_ffn_down):
    Bp, T = x_prompt.shape[:2]
    dt = x_prompt.dtype

    def trunk(h, conv_bufs, sb_past, nsa_past, win_past, mem_kv, win_keep):
        conv_new, sbk_new, sbv_new = [], [], []
        ck_new, cv_new, sk_new, sv_new, wk_new, wv_new = [], [], [], [], [], []
        for l in range(DEPTH):
            a = rmsnorm(h, norm_mix[l])
            if l % 2 == 0:
                e = l // 2
                y, buf, k, v = even_mixer(a, w_in_even[e], w_mix_out[l], conv_bufs[e],
                                          conv_w[e], conv_b[e], conv_ln_g[e], conv_ln_b[e],
                                          sb_past[e][0], sb_past[e][1])
                conv_new.append(buf)
                sbk_new.append(k)
                sbv_new.append(v)
            else:
                o = l // 2
                y, ck, cv, sk, sv, wk, wv = odd_mixer(
                    a, w_in_odd[o], w_mix_out[l],
                    cmp_pe_k[o], cmp_w1_k[o], cmp_w2_k[o], cmp_pe_v[o], cmp_w1_v[o], cmp_w2_v[o],
                    nsa_past[o][0], nsa_past[o][1], nsa_past[o][2], nsa_past[o][3],
                    win_past[o][0], win_past[o][1], win_keep)
                ck_new.append(ck)
                cv_new.append(cv)
                sk_new.append(sk)
                sv_new.append(sv)
                wk_new.append(wk)
                wv_new.append(wv)
            h = h + y
            h = h + memory_attn(rmsnorm(h, norm_mem[l]), mem_kv[l][0], mem_kv[l][1], w_mem_q[l], w_mem_o[l])
            h = h + swiglu(rmsnorm(h, norm_ffn[l]), w_ffn_gate[l], w_ffn_up[l], w_ffn_down[l])
        return (rmsnorm(h, final_norm), jnp.stack(sbk_new), jnp.stack(sbv_new), jnp.stack(conv_new),
                jnp.stack(ck_new), jnp.stack(cv_new), jnp.stack(sk_new), jnp.stack(sv_new),
                jnp.stack(wk_new), jnp.stack(wv_new))

    mem_k_list = [(mem_prompt @ w_mem_k[l]).reshape(Bp, MEM_LEN, MEM_HEADS, MEM_HD) for l in range(DEPTH)]
    mem_v_list = [(mem_prompt @ w_mem_v[l]).reshape(Bp, MEM_LEN, MEM_HEADS, MEM_HD) for l in range(DEPTH)]
    empty_sb = jnp.zeros((Bp, 0, SB_HEADS, HEAD_DIM), dt)
    empty_kv = jnp.zeros((Bp, 0, NSA_KVH, HEAD_DIM), dt)
    zero_buf = jnp.zeros((Bp, CONV_W - 1, CONV_CH), dt)
    (y_prompt, sb_k_p, sb_v_p, conv_p, cmp_k_p, cmp_v_p, sel_k_p, sel_v_p, win_k_p, win_v_p) = trunk(
        x_prompt,
        [zero_buf] * N_EVEN,
        [(empty_sb, empty_sb)] * N_EVEN,
        [(empty_kv, empty_kv, empty_kv, empty_kv)] * N_ODD,
        [(empty_kv, empty_kv)] * N_ODD,
        [(mem_k_list[l], mem_v_list[l]) for l in range(DEPTH)],
        min(WINDOW, T))

    (y_sample, sb_k_s, sb_v_s, conv_s, cmp_k_s, cmp_v_s, sel_k_s, sel_v_s, win_k_s, win_v_s) = trunk(
        x_sample,
        [state_conv[e] for e in range(N_EVEN)],
        [(gather_pages(cache_sb_k[e], page_table), gather_pages(cache_sb_v[e], page_table))
         for e in range(N_EVEN)],
        [(gather_pages(cache_nsa_cmp_k[o], page_table), gather_pages(cache_nsa_cmp_v[o], page_table),
          gather_pages(cache_nsa_sel_k[o], page_table), gather_pages(cache_nsa_sel_v[o], page_table))
         for o in range(N_ODD)],
        [(cache_nsa_win_k[o], cache_nsa_win_v[o]) for o in range(N_ODD)],
        [(cache_mem_k[l], cache_mem_v[l]) for l in range(DEPTH)],
        cache_nsa_win_k.shape[2])

    mem_k_p = jnp.stack(mem_k_list)
    mem_v_p = jnp.stack(mem_v_list)
    return (y_prompt, y_sample,
            sb_k_p, sb_v_p, conv_p, cmp_k_p, cmp_v_p, sel_k_p, sel_v_p, win_k_p, win_v_p, mem_k_p, mem_v_p,
            sb_k_s, sb_v_s, conv_s, cmp_k_s, cmp_v_s, sel_k_s, sel_v_s, win_k_s, win_v_s)
```

```python
import numpy as np
from contextlib import ExitStack
import concourse.bass as bass
import concourse.mybir as mybir
from concourse.bass_utils import run_bass_kernel_spmd

F32 = mybir.dt.float32
BF16 = mybir.dt.bfloat16
I32 = mybir.dt.int32
AF = mybir.ActivationFunctionType
ALU = mybir.AluOpType

NCORES = 8
T = 4096
D = 1024
GT = 512
NG = T // GT
DFF = 2816
NFC = DFF // 128
EPS = 1e-6

import os
STAGE = int(os.environ.get("KSTAGE", "5"))
KSUB = int(os.environ.get("KSUB", "99"))
KM = int(os.environ.get("KM", "99"))


class Buf:
    __slots__ = ("t", "name", "w", "r", "excl")

    def __init__(self, t, name, excl=False):
        self.t = t
        self.name = name
        self.w = None
        self.r = []
        self.excl = excl


class Op:
    __slots__ = ("eng", "fn", "deps", "is_dma", "signal", "sem", "semval", "waits", "out")

    def __init__(self, eng, fn, is_dma):
        self.eng = eng
        self.fn = fn
        self.deps = []
        self.is_dma = is_dma
        self.signal = False
        self.sem = None
        self.semval = 0
        self.waits = []
        self.out = False


ENGS = ("tensor", "vector", "scalar", "gpsimd", "sync")
SAME_ENGINE_SYNC = True
NDMA_SEMS = 48


class Sched:
    def __init__(self, nc, es):
        self.nc = nc
        self.es = es
        self.ops = []
        self.dma_slot_last = [None] * NDMA_SEMS
        self.dma_slot_cnt = [0] * NDMA_SEMS
        self.dma_next = 0
        self.out_ops = []
        self.last = {e: None for e in ENGS}
        self.dmas_since_barrier = []

    def _add(self, op, reads, writes):
        deps = []
        for b in reads:
            if b.w is not None:
                deps.append(b.w)
            if b.excl:
                deps.extend(o for o in b.r if o.eng != op.eng)
        for b in writes:
            if b.w is not None:
                deps.append(b.w)
            deps.extend(b.r)
        for b in reads:
            b.r.append(op)
        for b in writes:
            b.w = op
            b.r = []
        seen = set()
        for d in deps:
            if d is op or id(d) in seen:
                continue
            seen.add(id(d))
            op.deps.append(d)
        self.ops.append(op)
        self.last[op.eng] = op
        return op

    def op(self, eng, fn, reads, writes):
        return self._add(Op(eng, fn, False), reads, writes)

    def dma(self, eng, fn, reads, writes, out=False):
        op = Op(eng, fn, True)
        slot = self.dma_next
        self.dma_next = (self.dma_next + 1) % NDMA_SEMS
        prev = self.dma_slot_last[slot]
        self.dma_slot_cnt[slot] += 1
        op.sem = slot
        op.semval = 16 * self.dma_slot_cnt[slot]
        op.signal = True
        self.dma_slot_last[slot] = op
        self._add(op, reads, writes)
        if prev is not None and prev not in op.deps:
            op.deps.append(prev)
        if out:
            op.out = True
            self.out_ops.append(op)
        self.dmas_since_barrier.append(op)
        return op

    def barrier(self):
        lasts = [o for o in self.last.values() if o is not None]
        dm = list(self.dmas_since_barrier)
        self.dmas_since_barrier = []
        for e in ENGS:
            op = Op(e, None, False)
            op.deps = [d for d in lasts + dm]
            self.ops.append(op)
            self.last[e] = op

    def finish(self):
        nc = self.nc
        es = self.es
        esem = {e: es.enter_context(nc.semaphore("s_" + e)) for e in ENGS}
        dsem = [es.enter_context(nc.semaphore("d_%d" % i)) for i in range(NDMA_SEMS)]
        for op in self.ops:
            for d in op.deps:
                if not d.is_dma and d.fn is not None:
                    if d.eng != op.eng or (SAME_ENGINE_SYNC and d.eng != "tensor"):
                        d.signal = True
        cnt = {e: 0 for e in ENGS}
        for op in self.ops:
            if op.is_dma:
                op.sem = dsem[op.sem]
            elif op.signal:
                cnt[op.eng] += 1
                op.sem = esem[op.eng]
                op.semval = cnt[op.eng]
        seen = {e: {} for e in ENGS}
        for op in self.ops:
            sm = seen[op.eng]
            for d in op.deps:
                if d.fn is None:
                    continue
                if not d.is_dma and d.eng == op.eng and not (SAME_ENGINE_SYNC and d.eng != "tensor"):
                    continue
                if not d.signal:
                    continue
                k = id(d.sem)
                if sm.get(k, 0) >= d.semval:
                    continue
                sm[k] = d.semval
                op.waits.append((d.sem, d.semval))
        streams = {e: [o for o in self.ops if o.eng == e] for e in ENGS}
        fm = {}
        for o in self.out_ops:
            k = id(o.sem)
            if fm.get(k, (None, 0))[1] < o.semval:
                fm[k] = (o.sem, o.semval)
        final = list(fm.values())
        with nc.Block() as block:
            def run(e, name):
                for op in streams[name]:
                    for (s, v) in op.waits:
                        e.wait_ge(s, v)
                    if op.fn is None:
                        continue
                    ins = op.fn(e)
                    if op.signal:
                        ins.then_inc(op.sem, 16 if op.is_dma else 1)
                if name == "sync":
                    for (s, v) in final:
                        e.wait_ge(s, v)

            @block.tensor
            def _(e):
                run(e, "tensor")

            @block.vector
            def _(e):
                run(e, "vector")

            @block.scalar
            def _(e):
                run(e, "scalar")

            @block.gpsimd
            def _(e):
                run(e, "gpsimd")

            @block.sync
            def _(e):
                run(e, "sync")
        self.counts = {e: len(streams[e]) for e in ENGS}


class Prog:
    def __init__(self):
        self.nc = bass.Bass("TRN2", target_bir_lowering=False)
        self.din = {}
        self.dout = {}

    def inp(self, name, shape, dtype=F32):
        self.din[name] = self.nc.dram_tensor(name, list(shape), dtype, kind="ExternalInput").ap()
        return self.din[name]

    def outp(self, name, shape, dtype=F32):
        self.dout[name] = self.nc.dram_tensor(name, list(shape), dtype, kind="ExternalOutput").ap()
        return self.dout[name]


def build_program():
    P = Prog()
    nc = P.nc
    xp = P.inp("xp", [T, D])
    memp = P.inp("memp", [256, D])
    vecs = P.inp("vecs", [128, 80])
    cwT = P.inp("cwT", [512, 31])
    c_ident = P.inp("c_ident", [128, 128])
    c_ones = P.inp("c_ones", [128, 128])
    c_negtri = P.inp("c_negtri", [128, 128])
    c_negones = P.inp("c_negones", [128, 128])
    c_div512 = P.inp("c_div512", [128, 128])
    c_sbmask = P.inp("c_sbmask", [4, 128, 512])
    w_in_even = P.inp("w_in_even", [D, 2560])
    w_mix_out = P.inp("w_mix_out", [2, D, D])
    w_mem_q = P.inp("w_mem_q", [2, D, D])
    w_mem_k = P.inp("w_mem_k", [2, D, D])
    w_mem_v = P.inp("w_mem_v", [2, D, D])
    w_mem_o = P.inp("w_mem_o", [2, D, D])
    w_ffn_gate = P.inp("w_ffn_gate", [2, D, DFF])
    w_ffn_up = P.inp("w_ffn_up", [2, D, DFF])
    w_ffn_down = P.inp("w_ffn_down", [2, DFF, D])

    w_in_odd = P.inp("w_in_odd", [D, 2608])
    w1kv = P.inp("w1kv", [128, 64, 256])
    w2kv = P.inp("w2kv", [2, 256, 64])
    peT = P.inp("peT", [128, 64])
    ropec = P.inp("ropec", [T, 32])
    ropes = P.inp("ropes", [T, 32])
    ropeb = P.inp("ropeb", [64, 64])
    tabs = P.inp("tabs", [T, 3, 64])
    c_ex = P.inp("c_ex", [64, T])
    c_m3 = P.inp("c_m3", [128, 2, 128])
    gfin = P.inp("gfin", [1, D])
    cmpkp = P.outp("cmpkp", [T, 256])
    cmpvp = P.outp("cmpvp", [T, 256])
    selkp = P.outp("selkp", [T, 256])
    selvp = P.outp("selvp", [T, 256])
    winkp = P.outp("winkp", [512, 256])
    winvp = P.outp("winvp", [512, 256])
    xs = P.inp("xs", [128, D])
    ptab = P.inp("ptab", [1, 256], I32)
    iota = P.inp("iota", [128, 1])
    sbk_rows = P.inp("sbk_rows", [2560 * 128, 512])
    sbv_rows = P.inp("sbv_rows", [2560 * 128, 512])
    sconv = P.inp("sconv", [16, 30, 512])
    cmk = P.inp("cmk", [2, 16, 256, D])
    cmv = P.inp("cmv", [2, 16, 256, D])
    c_masknew = P.inp("c_masknew", [8, 64])
    cck_rows = P.inp("cck_rows", [2560 * 128, 256])
    ccv_rows = P.inp("ccv_rows", [2560 * 128, 256])
    csk_rows = P.inp("csk_rows", [2560 * 128, 256])
    csv_rows = P.inp("csv_rows", [2560 * 128, 256])
    cwk_d = P.inp("cwk", [16, 512, 256])
    cwv_d = P.inp("cwv", [16, 512, 256])
    ropecs = P.inp("ropecs", [128, 64])
    ropeb_s = P.inp("ropeb_s", [64, 64])
    tabs_s = P.inp("tabs_s", [8, 2 * 3 * 65])
    cmpks = P.outp("cmpks", [128, 256])
    cmpvs = P.outp("cmpvs", [128, 256])
    selks = P.outp("selks", [128, 256])
    selvs = P.outp("selvs", [128, 256])
    winks = P.outp("winks", [16, 512, 256])
    winvs = P.outp("winvs", [16, 512, 256])
    ys = P.outp("ys", [128, D])
    sbks = P.outp("sbks", [128, 512])
    sbvs = P.outp("sbvs", [128, 512])
    convs = P.outp("convs", [16, 30, 512])
    yp = P.outp("yp", [T, D])
    sbkp = P.outp("sbkp", [T, 512])
    sbvp = P.outp("sbvp", [T, 512])
    convp = P.outp("convp", [30, 512])
    memkp = P.outp("memkp", [2, 256, D])
    memvp = P.outp("memvp", [2, 256, D])

    with ExitStack() as es:
        S = Sched(nc, es)
        AR = es.enter_context(nc.sbuf_tensor("arena", [128, 52900], F32))
        PS = [Buf(es.enter_context(nc.psum_tensor("ps%d" % i, [128, 512], F32))[:, :], "ps%d" % i, True) for i in range(8)]
        cur = [0]

        def alloc32(name, n, parts=128):
            o = cur[0]
            cur[0] += n
            assert cur[0] <= 52900, (name, cur[0])
            return Buf(AR[0:parts, o:o + n], name)

        def alloc16(name, nel, parts=128):
            n = (nel + 1) // 2
            o = cur[0]
            cur[0] += n
            assert cur[0] <= 52900, (name, cur[0])
            return Buf(AR[0:parts, o:o + n].bitcast(BF16), name)

        def X(eng, meth, R, W, **kw):
            return S.op(eng, lambda e: getattr(e, meth)(**kw), R, W)

        def MM(out, lhsT, rhs, start, stop, R, W):
            return S.op("tensor", lambda e: e.matmul(out, lhsT=lhsT, rhs=rhs, start=start, stop=stop), R, W)

        def TR(out, in_, ident, R, W):
            return S.op("tensor", lambda e: e.transpose(out=out, in_=in_, identity=ident), R, W)

        def DMA(eng, out, in_, R, W, is_out=False):
            return S.dma(eng, lambda e: e.dma_start(out=out, in_=in_), R, W, out=is_out)

        ident_f = alloc32("ident_f", 128)
        negtri = alloc32("negtri", 128)
        negones = alloc32("negones", 128)
        div512 = alloc32("div512", 128)
        vec = alloc32("vec", 80)
        ident_b = alloc16("ident_b", 128)
        ones_b = alloc16("ones_b", 128)
        cw = alloc32("cw", 4 * 31)
        DMA("sync", ident_f.t, c_ident, [], [ident_f])
        DMA("sync", negtri.t, c_negtri, [], [negtri])
        DMA("sync", negones.t, c_negones, [], [negones])
        DMA("sync", div512.t, c_div512, [], [div512])
        DMA("sync", vec.t, vecs, [], [vec])
        DMA("gpsimd", ident_b.t, c_ident, [], [ident_b])
        DMA("gpsimd", ones_b.t, c_ones, [], [ones_b])
        DMA("sync", cw.t.rearrange("p (c j) -> p c j", c=4), cwT.rearrange("(c p) j -> p c j", p=128), [], [cw])
        cwv = cw.t.rearrange("p (c j) -> p c j", c=4)
        GCOL = {"mix0": 0, "mix1": 8, "mem0": 16, "mem1": 24, "ffn0": 32, "ffn1": 40, "fin": 48}

        memKT = alloc16("memKT", 8 * 256)
        memV = alloc16("memV", 2 * 1024)
        memKTv = memKT.t.rearrange("p (c m) -> p c m", c=8)
        memVv = memV.t.rearrange("p (h n) -> p h n", h=2)
        hb = alloc32("h", 4 * 1024)
        hv = hb.t.rearrange("p (t d) -> p t d", t=4)
        aT = alloc16("aT", 8 * 512)
        aTv = aT.t.rearrange("p (c t) -> p c t", c=8)
        atok = [alloc16("atok%d" % i, 1024) for i in range(2)]
        NWB = 3
        wb = [alloc16("wb%d" % i, 8 * 512) for i in range(NWB)]
        wbv = [w.t.rearrange("p (c n) -> p c n", c=8) for w in wb]
        wbi = [0]
        stg = [alloc32("stg%d" % i, 512) for i in range(3)]
        stgi = [0]
        small = alloc32("small", 64)
        smi = [0]

        def next_stg():
            s = stg[stgi[0] % 3]
            stgi[0] += 1
            return s

        def load_w(src, k0, nch, c0, ncols):
            i = wbi[0] % NWB
            wbi[0] += 1
            DMA("gpsimd", wbv[i][:, 0:nch, 0:ncols],
                src[k0:k0 + nch * 128, c0:c0 + ncols].rearrange("(c p) n -> p c n", p=128), [], [wb[i]])
            return wb[i], wbv[i]

        scr = alloc16("scrA", NFC * 512)
        scr_base = cur[0] - (NFC * 512) // 2

        def scr16(name, off_w, nel):
            return Buf(AR[:, scr_base + off_w: scr_base + off_w + nel // 2].bitcast(BF16), name)

        def scr32(name, off_w, n):
            return Buf(AR[:, scr_base + off_w: scr_base + off_w + n], name)

        QT = scr16("QT", 0, 4 * 512)
        oT = scr16("oT", 1024, 4 * 512)
        ycT = scr16("ycT", 2048, 4 * 512)
        acc = scr32("acc", 3072, 4 * 512)
        QTv = QT.t.rearrange("p (c t) -> p c t", c=4)
        oTv = oT.t.rearrange("p (c t) -> p c t", c=4)
        ycTv = ycT.t.rearrange("p (c t) -> p c t", c=4)
        accv = acc.t.rearrange("p (c t) -> p c t", c=4)
        actT = scr
        actTv = scr.t.rearrange("p (c t) -> p c t", c=NFC)
        qmT = scr16("qmT", 0, 8 * 512)
        omT = scr16("omT", 2048, 8 * 512)
        pmT = scr16("pmT", 4096, 2 * 512)
        qmTv = qmT.t.rearrange("p (c t) -> p c t", c=8)
        omTv = omT.t.rearrange("p (c t) -> p c t", c=8)
        pmTv = pmT.t.rearrange("p (c t) -> p c t", c=2)

        l0_base = cur[0]
        sbmask = alloc32("sbmask", 4 * 512)
        DMA("sync", sbmask.t.rearrange("p (j q) -> p j q", j=4), c_sbmask.rearrange("j p q -> p j q"), [], [sbmask])
        sbm = sbmask.t.rearrange("p (j q) -> p j q", j=4)
        KT = alloc16("KT", 4 * T)
        KTv = KT.t.rearrange("p (c t) -> p c t", c=4)
        Vr = alloc16("V", 32 * 512)
        Vv = Vr.t.rearrange("p (t n) -> p t n", t=32)
        hp = alloc32("hp", 4 * 542)
        hpv = hp.t.rearrange("p (c t) -> p c t", c=4)
        Eb = [alloc32("E%d" % i, 512) for i in range(2)]
        SPb = [alloc32("SP%d" % i, 512) for i in range(2)]
        TMb = [alloc32("TM%d" % i, 512) for i in range(2)]
        Wb = [alloc16("W%d" % i, 512) for i in range(2)]
        SPm = alloc32("SPm", 512)
        SPacc = alloc32("SPacc", 512)
        h1 = nc.dram_tensor("h1_scratch", [T, D], F32, kind="Internal").ap()
        print("arena words used", cur[0])

        def fence(olds, news):
            ops = []
            for o in olds:
                ops.extend(o.r)
                if o.w is not None:
                    ops.append(o.w)
            for n in news:
                n.r = list(n.r) + ops

        MIXB = None

        def rmsnorm_to_aT(ntiles, gcol):
            for ti in range(ntiles):
                at = atok[ti % 2]
                k = smi[0] % 8
                smi[0] += 1
                ss = small.t[:, k * 4:k * 4 + 1]
                sd = small.t[:, k * 4 + 1:k * 4 + 2]
                rs = small.t[:, k * 4 + 2:k * 4 + 3]
                S.op("scalar", lambda e, at=at, ti=ti, ss=ss: e.activation(out=at.t, in_=hv[:, ti, :], func=AF.Square, accum_out=ss), [hb], [at, small])
                X("vector", "tensor_scalar", [small], [small], out=sd, in0=ss, scalar1=1.0 / D, scalar2=EPS, op0=ALU.mult, op1=ALU.add)
                S.op("scalar", lambda e, sd=sd: e.activation(out=sd, in_=sd, func=AF.Sqrt), [small], [small])
                X("vector", "reciprocal", [small], [small], out=rs, in_=sd)
                X("vector", "tensor_scalar", [hb, small], [at], out=at.t, in0=hv[:, ti, :], scalar1=rs, scalar2=None, op0=ALU.mult)
                pst = PS[2]
                pstv = pst.t[:, 0:512].bitcast(BF16).rearrange("p (c t) -> p c t", c=8)
                for c in range(8):
                    TR(pstv[:, c, :], at.t[:, c * 128:(c + 1) * 128], ident_b.t, [at, ident_b], [pst])
                for c in range(8):
                    X("vector" if c % 2 == 0 else "gpsimd" if False else "vector", "tensor_scalar", [pst, vec], [aT],
                      out=aTv[:, c, ti * 128:(ti + 1) * 128], in0=pstv[:, c, :], scalar1=vec.t[:, gcol + c:gcol + c + 1], scalar2=None, op0=ALU.mult)

        psi = [0]

        def next_ps():
            p = PS[psi[0] % 2]
            psi[0] += 1
            return p

        def proj_fm(wbuf, wv, colchunk, ntok, ps, nch=8):
            for c in range(nch):
                MM(ps.t[:, 0:ntok], wv[:, c, colchunk * 128:(colchunk + 1) * 128], aTv[:, c, 0:ntok], c == 0, c == nch - 1, [wbuf, aT], [ps])

        def proj_tm(wbuf, wv, ti, ncols, ps, src=None, srcv=None, nch=8):
            sb = aT if src is None else src
            sv = aTv if srcv is None else srcv
            for c in range(nch):
                MM(ps.t[:, 0:ncols], sv[:, c, ti * 128:(ti + 1) * 128], wv[:, c, 0:ncols], c == 0, c == nch - 1, [wbuf, sb], [ps])

        def residual_proj(src, srcv, wsrc, nrowchunks, ntiles):
            for ch in range(2):
                pieces = []
                k0 = 0
                while k0 < nrowchunks:
                    n = min(8, nrowchunks - k0)
                    pieces.append((k0, n))
                    k0 += n
                if len(pieces) == 1:
                    wbuf, wv = load_w(wsrc, 0, nrowchunks, ch * 512, 512)
                    for ti in range(ntiles):
                        ps = next_ps()
                        for c in range(nrowchunks):
                            MM(ps.t[:, 0:512], srcv[:, c, ti * 128:(ti + 1) * 128], wv[:, c, 0:512], c == 0, c == nrowchunks - 1, [wbuf, src], [ps])
                        X("vector", "tensor_tensor", [hb, ps], [hb], out=hv[:, ti, ch * 512:(ch + 1) * 512], in0=hv[:, ti, ch * 512:(ch + 1) * 512], in1=ps.t[:, 0:512], op=ALU.add)
                else:
                    accs = [PS[4 + ti] for ti in range(ntiles)]
                    for pi, (k0, n) in enumerate(pieces):
                        wbuf, wv = load_w(wsrc, k0 * 128, n, ch * 512, 512)
                        for ti in range(ntiles):
                            for c in range(n):
                                MM(accs[ti].t[:, 0:512], srcv[:, k0 + c, ti * 128:(ti + 1) * 128], wv[:, c, 0:512],
                                   (pi == 0 and c == 0), (pi == len(pieces) - 1 and c == n - 1), [wbuf, src], [accs[ti]])
                    for ti in range(ntiles):
                        X("vector", "tensor_tensor", [hb, accs[ti]], [hb], out=hv[:, ti, ch * 512:(ch + 1) * 512], in0=hv[:, ti, ch * 512:(ch + 1) * 512], in1=accs[ti].t[:, 0:512], op=ALU.add)

        def mem_q_proj(l, ntiles, gname):
            ntok = ntiles * 128
            fence([QT, oT, ycT, acc, actT], [qmT, omT, pmT])
            rmsnorm_to_aT(ntiles, GCOL[gname])
            for ch in range(2):
                wbuf, wv = load_w(w_mem_q[l], 0, 8, ch * 512, 512)
                for cc in range(4):
                    ps = next_ps()
                    proj_fm(wbuf, wv, cc, ntok, ps)
                    S.op("scalar", lambda e, ps=ps, cc=cc, ch=ch: e.activation(out=qmTv[:, ch * 4 + cc, 0:ntok], in_=ps.t[:, 0:ntok], func=AF.Copy), [ps], [qmT])

        def mem_attn_core(c0, ntok, KTb, KTv_, Vb, Vv_):
            cs = slice(c0, c0 + ntok)
            for hd in range(4):
                den = PS[3]
                for mh in range(2):
                    ps = next_ps()
                    for dc in range(2):
                        MM(ps.t[:, 0:ntok], KTv_[:, hd * 2 + dc, mh * 128:(mh + 1) * 128], qmTv[:, hd * 2 + dc, cs], dc == 0, dc == 1, [KTb, qmT], [ps])
                    S.op("scalar", lambda e, ps=ps, mh=mh: e.activation(out=pmTv[:, mh, 0:ntok], in_=ps.t[:, 0:ntok], func=AF.Exp, scale=1.0 / 16.0), [ps], [pmT])
                for mh in range(2):
                    MM(den.t[:, 0:ntok], ones_b.t, pmTv[:, mh, 0:ntok], mh == 0, mh == 1, [ones_b, pmT], [den])
                rd = next_stg()
                X("vector", "reciprocal", [den], [rd], out=rd.t[:, 0:ntok], in_=den.t[:, 0:ntok])
                for dc in range(2):
                    po = PS[4 + dc]
                    for mh in range(2):
                        MM(po.t[:, 0:ntok], Vv_[:, mh, hd * 256 + dc * 128: hd * 256 + (dc + 1) * 128], pmTv[:, mh, 0:ntok], mh == 0, mh == 1, [Vb, pmT], [po])
                    X("vector", "tensor_tensor", [po, rd], [omT], out=omTv[:, hd * 2 + dc, cs], in0=po.t[:, 0:ntok], in1=rd.t[:, 0:ntok], op=ALU.mult)

        def mem_attn(l, ntiles, gname):
            mem_q_proj(l, ntiles, gname)
            mem_attn_core(0, ntiles * 128, memKT, memKTv, memV, memVv)
            residual_proj(omT, omTv, w_mem_o[l], 8, ntiles)

        def ffn(l, ntiles, gname):
            ntok = ntiles * 128
            fence([qmT, omT, pmT], [actT])
            rmsnorm_to_aT(ntiles, GCOL[gname])
            for pc in range(6):
                ncols = 512 if pc < 5 else 256
                wg, wgv = load_w(w_ffn_gate[l], 0, 8, pc * 512, ncols)
                wu, wuv = load_w(w_ffn_up[l], 0, 8, pc * 512, ncols)
                for cc in range(ncols // 128):
                    fc = pc * 4 + cc
                    pg = next_ps()
                    proj_fm(wg, wgv, cc, ntok, pg)
                    pu = PS[4 + fc % 2]
                    for c in range(8):
                        MM(pu.t[:, 0:ntok], wuv[:, c, cc * 128:(cc + 1) * 128], aTv[:, c, 0:ntok], c == 0, c == 7, [wu, aT], [pu])
                    sg = next_stg()
                    S.op("scalar", lambda e, pg=pg, sg=sg: e.activation(out=sg.t[:, 0:ntok], in_=pg.t[:, 0:ntok], func=AF.Silu), [pg], [sg])
                    X("vector", "tensor_tensor", [pu, sg], [actT], out=actTv[:, fc, 0:ntok], in0=pu.t[:, 0:ntok], in1=sg.t[:, 0:ntok], op=ALU.mult)
            residual_proj(actT, actTv, w_ffn_down[l], NFC, ntiles)

        def prompt_mem_kv(l):
            DMA("sync", hv[:, 0:2, :], memp.rearrange("(t p) d -> p t d", p=128), [], [hb])
            for mh in range(2):
                at = atok[mh % 2]
                S.op("scalar", lambda e, at=at, mh=mh: e.activation(out=at.t, in_=hv[:, mh, :], func=AF.Copy), [hb], [at])
                if KM < 2:
                    continue
                pst = PS[2]
                pstv = pst.t[:, 0:512].bitcast(BF16).rearrange("p (c t) -> p c t", c=8)
                for c in range(8):
                    TR(pstv[:, c, :], at.t[:, c * 128:(c + 1) * 128], ident_b.t, [at, ident_b], [pst])
                if KM < 3:
                    continue
                X("vector", "tensor_copy", [pst], [aT], out=aTv[:, :, mh * 128:(mh + 1) * 128], in_=pstv)
            if KM < 4:
                DMA("sync", memkp[0, 0:128, 0:512], atok[0].t.bitcast(F32), [atok[0], aT, PS[2]], [], is_out=True)
                return
            for ch in range(2):
                wbuf, wv = load_w(w_mem_k[l], 0, 8, ch * 512, 512)
                if KM < 5:
                    DMA("sync", memkp[0, 0:128, 0:512], atok[0].t.bitcast(F32), [atok[0], aT, PS[2], wbuf], [], is_out=True)
                    continue
                for mh in range(2):
                    ps = next_ps()
                    proj_tm(wbuf, wv, mh, 512, ps)
                    st = next_stg()
                    S.op("scalar", lambda e, ps=ps, st=st: e.activation(out=st.t, in_=ps.t, func=AF.Copy), [ps], [st])
                    DMA("sync", memkp[l, mh * 128:(mh + 1) * 128, ch * 512:(ch + 1) * 512], st.t, [st], [], is_out=True)
                if KM < 6:
                    continue
                for cc in range(4):
                    ps = next_ps()
                    proj_fm(wbuf, wv, cc, 256, ps)
                    X("vector", "tensor_copy", [ps], [memKT], out=memKTv[:, ch * 4 + cc, :], in_=ps.t[:, 0:256])
            if KM < 7:
                return
            for ch in range(2):
                wbuf, wv = load_w(w_mem_v[l], 0, 8, ch * 512, 512)
                for mh in range(2):
                    ps = next_ps()
                    proj_tm(wbuf, wv, mh, 512, ps)
                    st = next_stg()
                    S.op("scalar", lambda e, ps=ps, st=st: e.activation(out=st.t, in_=ps.t, func=AF.Copy), [ps], [st])
                    X("vector", "tensor_copy", [ps], [memV], out=memVv[:, mh, ch * 512:(ch + 1) * 512], in_=ps.t)
                    DMA("sync", memvp[l, mh * 128:(mh + 1) * 128, ch * 512:(ch + 1) * 512], st.t, [st], [], is_out=True)

        def sb_attention(gi):
            nkb = 4 * gi + 4
            for hd in range(8):
                pj, hh = hd // 2, hd % 2
                prt = slice(hh * 64, (hh + 1) * 64)
                po = PS[6 + hd % 2]
                first = True
                for it, kb in enumerate(reversed(range(nkb))):
                    j = kb - 4 * gi
                    diag = j >= 0
                    ps = next_ps()
                    MM(ps.t, KTv[prt, pj, kb * 128:(kb + 1) * 128], QTv[prt, pj, :], True, True, [KT, QT], [ps])
                    E = Eb[it % 2]
                    SP = SPb[it % 2]
                    TM = TMb[it % 2]
                    Wt = Wb[it % 2]
                    S.op("scalar", lambda e, ps=ps, E=E: e.activation(out=E.t, in_=ps.t, func=AF.Exp, scale=0.125), [ps], [E])
                    S.op("scalar", lambda e, E=E, SP=SP: e.activation(out=SP.t, in_=E.t, func=AF.Ln, bias=1.0), [E], [SP])
                    X("vector", "scalar_tensor_tensor", [ps, SP], [TM], out=TM.t, in0=ps.t, scalar=0.125, in1=SP.t, op0=ALU.mult, op1=ALU.subtract)
                    if diag:
                        X("gpsimd", "tensor_tensor", [SP, sbmask], [SPm], out=SPm.t, in0=SP.t, in1=sbm[:, j, :], op=ALU.mult)
                        spm = SPm
                    else:
                        spm = SP
                    pa = PS[4 + it % 2]
                    MM(pa.t, negtri.t, spm.t, True, first, [negtri, spm], [pa])
                    if not first:
                        MM(pa.t, negones.t, SPacc.t, False, True, [negones, SPacc], [pa])
                    X("vector", "tensor_tensor", [TM, pa], [TM], out=TM.t, in0=TM.t, in1=pa.t, op=ALU.add)
                    S.op("scalar", lambda e, TM=TM, Wt=Wt: e.activation(out=Wt.t, in_=TM.t, func=AF.Exp), [TM], [Wt])
                    if diag:
                        X("gpsimd", "tensor_tensor", [Wt, sbmask], [Wt], out=Wt.t, in0=Wt.t, in1=sbm[:, j, :], op=ALU.mult)
                    if first:
                        X("gpsimd", "tensor_copy", [spm], [SPacc], out=SPacc.t, in_=spm.t)
                    elif it < nkb - 1:
                        X("gpsimd", "tensor_tensor", [spm, SPacc], [SPacc], out=SPacc.t, in0=SPacc.t, in1=spm.t, op=ALU.add)
                    MM(po.t[prt, :], Vv[:, kb, hd * 64:(hd + 1) * 64], Wt.t, first, it == nkb - 1, [Vr, Wt], [po])
                    first = False
                S.op("scalar", lambda e, po=po, prt=prt, pj=pj: e.activation(out=oTv[prt, pj, :], in_=po.t[prt, :], func=AF.Copy), [po], [oT])

        def conv_module(gi):
            for cc in range(4):
                eng = "vector"
                X(eng, "tensor_scalar", [hp, cw, vec], [acc], out=accv[:, cc, :], in0=hpv[:, cc, 0:512], scalar1=cwv[:, cc, 0:1], scalar2=vec.t[:, 56 + cc:57 + cc], op0=ALU.mult, op1=ALU.add)
                for jj in range(1, 31):
                    X(eng, "scalar_tensor_tensor", [hp, cw, acc], [acc], out=accv[:, cc, :], in0=hpv[:, cc, jj:jj + 512], scalar=cwv[:, cc, jj:jj + 1], in1=accv[:, cc, :], op0=ALU.mult, op1=ALU.add)
            pm = PS[3]
            pq = PS[4]
            sq = Eb[0]
            for cc in range(4):
                MM(pm.t, div512.t, accv[:, cc, :], cc == 0, cc == 3, [div512, acc], [pm])
            for cc in range(4):
                S.op("scalar", lambda e, cc=cc: e.activation(out=sq.t, in_=accv[:, cc, :], func=AF.Square), [acc], [sq])
                MM(pq.t, div512.t, sq.t, cc == 0, cc == 3, [div512, sq], [pq])
            mean = SPb[0]
            var = SPb[1]
            rstd = TMb[0]
            S.op("scalar", lambda e: e.activation(out=mean.t, in_=pm.t, func=AF.Copy), [pm], [mean])
            X("gpsimd", "tensor_tensor", [mean], [var], out=var.t, in0=mean.t, in1=mean.t, op=ALU.mult)
            X("vector", "tensor_tensor", [pq, var], [var], out=var.t, in0=pq.t, in1=var.t, op=ALU.subtract)
            X("vector", "tensor_scalar", [var], [var], out=var.t, in0=var.t, scalar1=EPS, scalar2=None, op0=ALU.add)
            S.op("scalar", lambda e: e.activation(out=var.t, in_=var.t, func=AF.Sqrt), [var], [var])
            X("vector", "reciprocal", [var], [rstd], out=rstd.t, in_=var.t)
            for cc in range(4):
                tm = TMb[1]
                X("vector", "tensor_tensor", [acc, mean], [tm], out=tm.t, in0=accv[:, cc, :], in1=mean.t, op=ALU.subtract)
                X("vector", "tensor_tensor", [tm, rstd], [tm], out=tm.t, in0=tm.t, in1=rstd.t, op=ALU.mult)
                S.op("scalar", lambda e, cc=cc, tm=tm: e.activation(out=ycTv[:, cc, :], in_=tm.t, func=AF.Silu, scale=vec.t[:, 60 + cc:61 + cc], bias=vec.t[:, 64 + cc:65 + cc]), [tm, vec], [ycT])

        def prompt_l0_group(gi):
            r0 = gi * GT
            fence([actT, qmT, omT, pmT], [QT, oT, ycT, acc])
            DMA("sync", hv, xp[r0:r0 + GT, :].rearrange("(t p) d -> p t d", p=128), [], [hb])
            rmsnorm_to_aT(4, GCOL["mix0"])
            wval, wvalv = load_w(w_in_even, 0, 8, 0, 512)
            for cc in range(4):
                ps = next_ps()
                proj_fm(wval, wvalv, cc, 512, ps)
                S.op("scalar", lambda e, ps=ps, cc=cc: e.activation(out=accv[:, cc, :], in_=ps.t, func=AF.Copy), [ps], [acc])
            wgt, wgtv = load_w(w_in_even, 0, 8, 512, 512)
            for cc in range(4):
                ps = next_ps()
                proj_fm(wgt, wgtv, cc, 512, ps)
                sg = next_stg()
                S.op("scalar", lambda e, ps=ps, sg=sg: e.activation(out=sg.t, in_=ps.t, func=AF.Sigmoid), [ps], [sg])
                X("vector", "tensor_tensor", [acc, sg], [hp], out=hpv[:, cc, 30:542], in0=accv[:, cc, :], in1=sg.t, op=ALU.mult)
            if KSUB < 2:
                DMA("sync", yp[r0:r0 + GT, :].rearrange("(t p) d -> p t d", p=128), hv, [hb], [], is_out=True)
                return
            wq, wqv = load_w(w_in_even, 0, 8, 1024, 512)
            for pj in range(4):
                ps = next_ps()
                proj_fm(wq, wqv, pj, 512, ps)
                S.op("scalar", lambda e, ps=ps, pj=pj: e.activation(out=QTv[:, pj, :], in_=ps.t, func=AF.Copy), [ps], [QT])
            wk, wkv = load_w(w_in_even, 0, 8, 1536, 512)
            for pj in range(4):
                ps = next_ps()
                proj_fm(wk, wkv, pj, 512, ps)
                S.op("scalar", lambda e, ps=ps, pj=pj: e.activation(out=KTv[:, pj, r0:r0 + GT], in_=ps.t, func=AF.Copy), [ps], [KT])
            for ti in range(4):
                ps = next_ps()
                proj_tm(wk, wkv, ti, 512, ps)
                st = next_stg()
                S.op("scalar", lambda e, ps=ps, st=st: e.activation(out=st.t, in_=ps.t, func=AF.Copy), [ps], [st])
                DMA("sync", sbkp[r0 + ti * 128:r0 + (ti + 1) * 128, :], st.t, [st], [], is_out=True)
            wv_, wvv = load_w(w_in_even, 0, 8, 2048, 512)
            for ti in range(4):
                ps = next_ps()
                proj_tm(wv_, wvv, ti, 512, ps)
                st = next_stg()
                S.op("scalar", lambda e, ps=ps, st=st: e.activation(out=st.t, in_=ps.t, func=AF.Copy), [ps], [st])
                X("vector", "tensor_copy", [ps], [Vr], out=Vv[:, gi * 4 + ti, :], in_=ps.t)
                DMA("sync", sbvp[r0 + ti * 128:r0 + (ti + 1) * 128, :], st.t, [st], [], is_out=True)
            if KSUB < 3:
                DMA("sync", yp[r0:r0 + GT, :].rearrange("(t p) d -> p t d", p=128), hv, [hb], [], is_out=True)
                return
            conv_module(gi)
            if gi == NG - 1 and os.environ.get('KNOCP', '0') == '1':
                pass
            elif gi == NG - 1:
                pt_ = PS[3]
                for cc in range(4):
                    MM(pt_.t[0:30, cc * 128:(cc + 1) * 128], hpv[:, cc, 512:542], ident_f.t, cc == 0, cc == 3, [hp, ident_f], [pt_])
                st = next_stg()
                S.op("scalar", lambda e, st=st: e.activation(out=st.t[0:30, :], in_=pt_.t[0:30, :], func=AF.Copy), [pt_], [st])
                DMA("sync", convp, st.t[0:30, :], [st], [], is_out=True)
            else:
                X("gpsimd", "tensor_copy", [hp], [hp], out=hpv[:, :, 0:30], in_=hpv[:, :, 512:542])
            if KSUB < 4:
                DMA("sync", yp[r0:r0 + GT, :].rearrange("(t p) d -> p t d", p=128), hv, [hb], [], is_out=True)
                return
            sb_attention(gi)
            if KSUB < 5:
                DMA("sync", yp[r0:r0 + GT, :].rearrange("(t p) d -> p t d", p=128), hv, [hb], [], is_out=True)
                return
            for ch in range(2):
                wo, wov = load_w(w_mix_out[0], 0, 8, ch * 512, 512)
                for ti in range(4):
                    ps = next_ps()
                    for c in range(8):
                        if c < 4:
                            MM(ps.t, ycTv[:, c, ti * 128:(ti + 1) * 128], wov[:, c, :], c == 0, False, [wo, ycT], [ps])
                        else:
                            MM(ps.t, oTv[:, c - 4, ti * 128:(ti + 1) * 128], wov[:, c, :], False, c == 7, [wo, oT], [ps])
                    X("vector", "tensor_tensor", [hb, ps], [hb], out=hv[:, ti, ch * 512:(ch + 1) * 512], in0=hv[:, ti, ch * 512:(ch + 1) * 512], in1=ps.t, op=ALU.add)
            if KSUB < 6:
                DMA("sync", yp[r0:r0 + GT, :].rearrange("(t p) d -> p t d", p=128), hv, [hb], [], is_out=True)
                return
            mem_attn(0, 4, "mem0")
            if KSUB < 7:
                DMA("sync", yp[r0:r0 + GT, :].rearrange("(t p) d -> p t d", p=128), hv, [hb], [], is_out=True)
                return
            ffn(0, 4, "ffn0")
            if KSUB < 8:
                DMA("sync", yp[r0:r0 + GT, :].rearrange("(t p) d -> p t d", p=128), hv, [hb], [], is_out=True)
                return
            DMA("sync", h1[r0:r0 + GT, :].rearrange("(t p) d -> p t d", p=128), hv, [hb], [])
            if STAGE <= 2:
                DMA("sync", yp[r0:r0 + GT, :].rearrange("(t p) d -> p t d", p=128), hv, [hb], [], is_out=True)


        AX = mybir.AxisListType
        L = {}

        def bc(ap, axis, shape):
            return ap.unsqueeze(axis).to_broadcast(list(shape))

        def l1_alloc():
            cur[0] = l0_base
            L["SKT"] = alloc16("SKT", 4 * T, parts=64)
            L["SV"] = alloc16("SV", 32 * 260)
            L["CKVT"] = alloc16("CKVT", 4 * T)
            ckvt_base = cur[0] - 2 * T
            L["pe"] = alloc32("pe", 64)
            L["ropeC"] = alloc32("ropeC", 128)
            L["ropeS"] = alloc32("ropeS", 128)
            L["ropeb"] = alloc32("ropeb", 64, parts=64)
            L["m3"] = alloc32("m3", 256)
            L["selx"] = alloc16("selx", 128)
            L["KCT"] = alloc16("KCT", 256, parts=64)
            L["VC"] = alloc16("VC", 256, parts=64)
            L["w2"] = alloc16("w2", 2 * 2 * 64)
            L["G"] = alloc32("G", 4 * 48)
            L["tb"] = [alloc32("tb%d" % i, 192) for i in range(2)]
            L["Ec"] = alloc32("Ec", 256)
            L["Pn"] = alloc32("Pn", 256)
            L["Pb"] = alloc16("Pb", 256)
            L["PT"] = alloc16("PT", 512, parts=64)
            L["sc"] = alloc32("sc", 64)
            L["sc2"] = alloc32("sc2", 64)
            L["wk16"] = alloc32("wk16", 64)
            L["m8"] = alloc32("m8", 8)
            L["selb"] = alloc16("selb", 64)
            L["selT"] = alloc16("selT", 128, parts=64)
            L["sm"] = alloc32("sm", 32)
            L["Es"] = [alloc16("Es%d" % i, 512) for i in range(2)]
            L["Pms"] = [alloc16("Pms%d" % i, 512) for i in range(2)]
            L["Of"] = alloc32("Of", 1024)
            L["Ot"] = alloc32("Ot", 256)
            L["Otok"] = alloc16("Otok", 1024)
            L["skb"] = alloc16("skb", 256)
            L["hid"] = alloc16("hid", 512)
            L["kcb"] = alloc16("kcb", 256, parts=64)
            print("L1 arena words used", cur[0])
            end = cur[0]
            cur[0] = ckvt_base
            L["QT"] = [alloc16("QT1_%d" % i, 16 * 128, parts=64) for i in range(2)]
            L["qtok"] = alloc16("qtok", 4 * 1024)
            L["WKT"] = alloc16("WKT", 4 * 1024, parts=64)
            L["WV"] = alloc16("WV", 8 * 260)
            assert cur[0] <= ckvt_base + 2 * T, cur[0] - ckvt_base
            cur[0] = end

        def rope(xin, xout, H, cs, sn, R, W, np_=128):
            cb = bc(cs, 1, [np_, H, 32])
            sb_ = bc(sn, 1, [np_, H, 32])
            tA = L["Of"].t[0:np_, 0:H * 32].rearrange("p (h d) -> p h d", h=H)
            tB = L["Of"].t[0:np_, 512:512 + H * 32].rearrange("p (h d) -> p h d", h=H)
            x1 = xin[:, :, 0:32]
            x2 = xin[:, :, 32:64]
            rA = rB = L["Of"]
            X("vector", "tensor_tensor", R, [rA], out=tA, in0=x1, in1=cb, op=ALU.mult)
            X("vector", "tensor_tensor", R, [rB], out=tB, in0=x2, in1=sb_, op=ALU.mult)
            X("vector", "tensor_tensor", [rA], W, out=xout[:, :, 0:32], in0=tA, in1=tB, op=ALU.subtract)
            X("vector", "tensor_tensor", R, [rA], out=tA, in0=x2, in1=cb, op=ALU.mult)
            X("vector", "tensor_tensor", R, [rB], out=tB, in0=x1, in1=sb_, op=ALU.mult)
            X("vector", "tensor_tensor", [rA], W, out=xout[:, :, 32:64], in0=tA, in1=tB, op=ALU.add)

        def load_rope(gi):
            r0 = gi * GT
            DMA("sync", L["ropeC"].t.rearrange("p (t d) -> p t d", t=4), ropec[r0:r0 + GT, :].rearrange("(t p) d -> p t d", p=128), [], [L["ropeC"]])
            DMA("sync", L["ropeS"].t.rearrange("p (t d) -> p t d", t=4), ropes[r0:r0 + GT, :].rearrange("(t p) d -> p t d", p=128), [], [L["ropeS"]])

        def l1_consts():
            DMA("sync", L["pe"].t, peT, [], [L["pe"]])
            DMA("sync", L["ropeb"].t, ropeb, [], [L["ropeb"]])
            DMA("sync", L["m3"].t.rearrange("p (j q) -> p j q", j=2), c_m3, [], [L["m3"]])
            DMA("gpsimd", L["w2"].t.rearrange("p (k c e) -> p k c e", k=2, c=2), w2kv.rearrange("k (c p) e -> p k c e", p=128), [], [L["w2"]])
            SVv = L["SV"].t.rearrange("p (t g e) -> p t g e", t=32, g=4)
            X("vector", "memset", [], [L["SV"]], ap=SVv[:, :, :, 64:65], constant=1.0)

        def kv_tile_to_residents(ps, tile, ti, kT, kTslot, vR, vslot, outk, outv, orow):
            st = next_stg()
            cs = L["ropeC"].t[:, ti * 32:(ti + 1) * 32]
            sn = L["ropeS"].t[:, ti * 32:(ti + 1) * 32]
            rope(ps.t[:, 0:256].rearrange("p (h d) -> p h d", h=4), st.t[:, 0:256].rearrange("p (h d) -> p h d", h=4), 4, cs, sn,
                 [ps, L["ropeC"], L["ropeS"]], [st])
            X("vector", "tensor_copy", [ps], [st], out=st.t[:, 256:512], in_=ps.t[:, 256:512])
            if orow is not None:
                DMA("sync", outk[orow:orow + 128, :], st.t[:, 0:256], [st], [], is_out=True)
                DMA("sync", outv[orow:orow + 128, :], st.t[:, 256:512], [st], [], is_out=True)
            skb = L["skb"]
            X("gpsimd", "tensor_copy", [st], [skb], out=skb.t, in_=st.t[:, 0:256])
            vv = vR.t.rearrange("p (t g e) -> p t g e", g=4, e=65)
            X("gpsimd", "tensor_copy", [st], [vR], out=vv[:, vslot, :, 0:64], in_=st.t[:, 256:512].rearrange("p (g e) -> p g e", g=4))
            pst = PS[2]
            p64 = pst.t[0:64, 0:256].bitcast(BF16).rearrange("p (g t) -> p g t", g=4)
            for g in range(4):
                TR(p64[:, g, :], skb.t[:, g * 64:(g + 1) * 64], ident_b.t, [skb, ident_b], [pst])
            kv_ = kT.t.rearrange("p (g t) -> p g t", g=4)
            S.op("scalar", lambda e: e.activation(out=kv_[:, :, kTslot * 128:(kTslot + 1) * 128], in_=p64, func=AF.Copy), [pst], [kT])

        def l1_sweepA(gi):
            r0 = gi * GT
            fence([actT, qmT, omT, pmT], [QT, oT, ycT, acc])
            DMA("sync", hv, h1[r0:r0 + GT, :].rearrange("(t p) d -> p t d", p=128), [], [hb])
            rmsnorm_to_aT(4, GCOL["mix1"])
            load_rope(gi)
            CKVT = L["CKVT"]
            CKVTv = CKVT.t.rearrange("p (g t) -> p g t", g=4)
            wc, wcv = load_w(w_in_odd, 0, 8, 1024, 512)
            for ti in range(4):
                ps = next_ps()
                proj_tm(wc, wcv, ti, 512, ps)
                st = next_stg()
                S.op("scalar", lambda e, ps=ps, st=st: e.activation(out=st.t, in_=ps.t, func=AF.Copy), [ps], [st])
                DMA("sync", cmpkp[r0 + ti * 128:r0 + (ti + 1) * 128, :], st.t[:, 0:256], [st], [], is_out=True)
                DMA("sync", cmpvp[r0 + ti * 128:r0 + (ti + 1) * 128, :], st.t[:, 256:512], [st], [], is_out=True)
            for g in range(4):
                ps = next_ps()
                for c in range(8):
                    MM(ps.t[0:64, :], wcv[:, c, g * 64:(g + 1) * 64], aTv[:, c, :], c == 0, c == 7, [wc, aT], [ps])
                for c in range(8):
                    MM(ps.t[64:128, :], wcv[:, c, 256 + g * 64:256 + (g + 1) * 64], aTv[:, c, :], c == 0, c == 7, [wc, aT], [ps])
                X("vector", "tensor_tensor", [ps, L["pe"]], [CKVT], out=CKVTv[:, g, r0:r0 + GT].rearrange("p (n l) -> p n l", l=64),
                  in0=ps.t.rearrange("p (n l) -> p n l", l=64), in1=bc(L["pe"].t, 1, [128, 8, 64]), op=ALU.add)
            ws, wsv = load_w(w_in_odd, 0, 8, 1536, 512)
            for ti in range(4):
                tile = gi * 4 + ti
                ps = next_ps()
                proj_tm(ws, wsv, ti, 512, ps)
                kv_tile_to_residents(ps, tile, ti, L["SKT"], tile, L["SV"], tile, selkp, selvp, r0 + ti * 128)

        def gelu_to(ps, out_bf):
            a = L["Of"]
            at_ = a.t[:, 0:512]
            bt_ = a.t[:, 512:1024]
            S.op("scalar", lambda e: e.activation(out=at_, in_=ps.t, func=AF.Square), [ps], [a])
            X("vector", "tensor_scalar", [a], [a], out=at_, in0=at_, scalar1=0.044715, scalar2=1.0, op0=ALU.mult, op1=ALU.add)
            X("vector", "tensor_tensor", [a, ps], [a], out=at_, in0=at_, in1=ps.t, op=ALU.mult)
            S.op("scalar", lambda e: e.activation(out=bt_, in_=at_, func=AF.Sigmoid, scale=1.5957691216057308), [a], [a])
            X("vector", "tensor_tensor", [a, ps], [out_bf], out=out_bf.t, in0=bt_, in1=ps.t, op=ALU.mult)

        def l1_compress():
            CKVT = L["CKVT"]
            CKVTv = CKVT.t.rearrange("p (g n l) -> p g n l", g=4, l=64)
            hps = [PS[4], PS[5]]
            for pc in range(4):
                i = wbi[0] % NWB
                wbi[0] += 1
                wp = wb[i]
                wpv = wp.t.rearrange("p (l h) -> p l h", l=16)
                DMA("gpsimd", wpv, w1kv[:, pc * 16:(pc + 1) * 16, :], [], [wp])
                for kvi in range(2):
                    prt = slice(kvi * 64, (kvi + 1) * 64)
                    for g in range(4):
                        for hc in range(2):
                            col = (g * 2 + hc) * 64
                            for ll in range(16):
                                l = pc * 16 + ll
                                first = (pc == 0 and g == 0 and hc == 0 and ll == 0)
                                last = (pc == 3 and g == 3 and hc == 1 and ll == 15)
                                MM(hps[kvi].t[:, col:col + 64], wpv[prt, ll, hc * 128:(hc + 1) * 128], CKVTv[prt, g, :, l], first, last, [wp, CKVT], [hps[kvi]])
            w2v = L["w2"].t.rearrange("p (k c e) -> p k c e", k=2, c=2)
            for kvi in range(2):
                hid = L["hid"]
                gelu_to(hps[kvi], hid)
                pk = PS[3]
                n = 0
                for g in range(4):
                    for hc in range(2):
                        col = (g * 2 + hc) * 64
                        MM(pk.t[0:64, g * 64:(g + 1) * 64], hid.t[:, col:col + 64], w2v[:, kvi, hc, :], n == 0, n == 7, [hid, L["w2"]], [pk])
                        n += 1
                if kvi == 0:
                    kcb = L["kcb"]
                    rope(pk.t[0:64, 0:256].rearrange("p (h d) -> p h d", h=4), kcb.t.rearrange("p (h d) -> p h d", h=4), 4,
                         L["ropeb"].t[:, 0:32], L["ropeb"].t[:, 32:64], [pk, L["ropeb"]], [kcb], np_=64)
                    pst = PS[2]
                    p64 = pst.t[0:64, 0:128].bitcast(BF16).rearrange("p (g t) -> p g t", g=4)
                    for g in range(4):
                        TR(p64[:, g, :], kcb.t[:, g * 64:(g + 1) * 64], ident_b.t[0:64, 0:64], [kcb, ident_b], [pst])
                    X("vector", "tensor_copy", [pst], [L["KCT"]], out=L["KCT"].t.rearrange("p (g t) -> p g t", g=4), in_=p64)
                else:
                    X("vector", "tensor_copy", [pk], [L["VC"]], out=L["VC"].t, in_=pk.t[0:64, 0:256])

        def attn_branch(i, g, QTv, KTres, Vres, kblocks, masks, pso, sel_ex=None):
            KTv_ = KTres.t.rearrange("p (g t) -> p g t", g=4)
            Vv_ = Vres.t.rearrange("p (t g e) -> p t g e", g=4, e=65)
            nb = len(kblocks)
            for bi, (slot, mk) in enumerate(kblocks):
                pss = next_ps()
                MM(pss.t, KTv_[:, g, slot * 128:(slot + 1) * 128], QTv[:, 4 * g:4 * g + 4, :], True, True, [KTres, L["QTcur"]], [pss])
                Es = L["Es"][bi % 2]
                S.op("scalar", lambda e, pss=pss, Es=Es: e.activation(out=Es.t, in_=pss.t, func=AF.Exp, scale=0.125), [pss], [Es])
                src = Es
                if mk is not None:
                    Pm = L["Pms"][bi % 2]
                    if mk == "sel":
                        psm = PS[5]
                        kb = sel_ex[bi]
                        selx = L["selx"]
                        X("gpsimd", "tensor_copy", [L["selb"]], [selx], out=selx.t.rearrange("p (n l) -> p n l", n=2),
                          in_=L["selb"].t[:, 2 * kb:2 * kb + 2].unsqueeze(2).to_broadcast([128, 2, 64]))
                        pmb = psm.t[:, 0:64].bitcast(BF16)
                        TR(pmb, selx.t, ident_b.t, [selx, ident_b], [psm])
                        map_, mR = pmb, [psm]
                    else:
                        map_, mR = L["m3"].t[:, mk * 128:(mk + 1) * 128], [L["m3"]]
                    X("vector", "tensor_tensor", [Es] + mR, [Pm], out=Pm.t.rearrange("p (r t) -> p r t", r=4),
                      in0=Es.t.rearrange("p (r t) -> p r t", r=4), in1=bc(map_, 1, [128, 4, 128]), op=ALU.mult)
                    src = Pm
                for r in range(4):
                    MM(pso.t[:, r * 65:(r + 1) * 65], src.t[:, r * 128:(r + 1) * 128], Vv_[:, slot, g, :], (bi == 0 and r == 0), (bi == nb - 1 and r == 3), [src, Vres], [pso])

        def add_branch(pso, g, ti, br):
            Gv = L["G"].t[:, ti * 48 + g * 12: ti * 48 + (g + 1) * 12].rearrange("p (r b) -> p r b", b=3)
            pv = pso.t[:, 0:260].rearrange("p (r e) -> p r e", e=65)
            sm = L["sm"]
            cf = sm.t[:, 8:12]
            X("vector", "reciprocal", [pso], [sm], out=cf.unsqueeze(2), in_=pv[:, :, 64:65])
            X("vector", "tensor_tensor", [sm, L["G"]], [sm], out=cf.unsqueeze(2), in0=cf.unsqueeze(2), in1=Gv[:, :, br:br + 1], op=ALU.mult)
            Ot = L["Ot"]
            Otv = Ot.t.rearrange("p (r d) -> p r d", r=4)
            Ofv = L["Of"].t[:, g * 256:(g + 1) * 256].rearrange("p (r d) -> p r d", r=4)
            X("vector", "tensor_tensor", [pso, sm], [Ot], out=Otv, in0=pv[:, :, 0:64], in1=cf.unsqueeze(2).to_broadcast([128, 4, 64]), op=ALU.mult)
            X("gpsimd", "tensor_tensor", [Ot, L["Of"]], [L["Of"]], out=Ofv, in0=Ofv, in1=Otv, op=ALU.add)

        def l1_attn_tile(gi, ti):
            i = gi * 4 + ti
            QTb = L["QT"][i % 2]
            L["QTcur"] = QTb
            QTv = QTb.t.rearrange("p (h t) -> p h t", h=16)
            qtokv = L["qtok"].t.rearrange("p (t n) -> p t n", t=4)
            pst = PS[2]
            p64 = pst.t[0:64, 0:512].bitcast(BF16).rearrange("p (h t) -> p h t", h=8)
            for b8 in range(2):
                for hh in range(8):
                    hd = b8 * 8 + hh
                    TR(p64[:, hh, :], qtokv[:, ti, hd * 64:(hd + 1) * 64], ident_b.t, [L["qtok"], ident_b], [pst])
                S.op("scalar", lambda e, b8=b8: e.activation(out=QTv[:, b8 * 8:(b8 + 1) * 8, :], in_=p64, func=AF.Copy), [pst], [QTb])
            tb = L["tb"][i % 2]
            tbv = tb.t.rearrange("p (j n) -> p j n", j=3)
            DMA("sync", tbv, tabs[i * 128:(i + 1) * 128, :, :], [], [tb])
            KCTv = L["KCT"].t.rearrange("p (g n) -> p g n", g=4)
            VCv = L["VC"].t.rearrange("p (g e) -> p g e", g=4)
            Of = L["Of"]
            for g in range(4):
                psc = PS[3]
                for r in range(4):
                    MM(psc.t[:, r * 64:(r + 1) * 64], QTv[:, 4 * g + r, :], KCTv[:, g, :], True, True, [QTb, L["KCT"]], [psc])
                sm = L["sm"]
                pscv = psc.t[:, 0:256].rearrange("p (r n) -> p r n", r=4)
                X("vector", "tensor_reduce", [psc], [sm], out=sm.t[:, 0:4], in_=pscv, axis=AX.X, op=ALU.max)
                X("vector", "tensor_scalar", [sm], [sm], out=sm.t[:, 0:4], in0=sm.t[:, 0:4], scalar1=-0.125, scalar2=None, op0=ALU.mult)
                Ec, Pn, Pb = L["Ec"], L["Pn"], L["Pb"]
                for r in range(4):
                    S.op("scalar", lambda e, r=r: e.activation(out=Ec.t[:, r * 64:(r + 1) * 64], in_=psc.t[:, r * 64:(r + 1) * 64], func=AF.Exp, scale=0.125, bias=sm.t[:, r:r + 1]), [psc, sm], [Ec])
                Ecv = Ec.t.rearrange("p (r n) -> p r n", r=4)
                Pnv = Pn.t.rearrange("p (r n) -> p r n", r=4)
                X("vector", "tensor_tensor", [Ec, tb], [Ec], out=Ecv, in0=Ecv, in1=bc(tbv[:, 0, :], 1, [128, 4, 64]), op=ALU.mult)
                X("vector", "tensor_reduce", [Ec], [sm], out=sm.t[:, 4:8], in_=Ecv, axis=AX.X, op=ALU.add)
                X("vector", "tensor_scalar", [sm], [sm], out=sm.t[:, 4:8], in0=sm.t[:, 4:8], scalar1=1e-30, scalar2=None, op0=ALU.max)
                X("vector", "reciprocal", [sm], [sm], out=sm.t[:, 4:8], in_=sm.t[:, 4:8])
                X("vector", "tensor_tensor", [Ec, sm], [Pn], out=Pnv, in0=Ecv, in1=sm.t[:, 4:8].unsqueeze(2).to_broadcast([128, 4, 64]), op=ALU.mult)
                sc, sc2, wk16, m8, selb, selT = L["sc"], L["sc2"], L["wk16"], L["m8"], L["selb"], L["selT"]
                X("vector", "tensor_reduce", [Pn], [sc], out=sc.t, in_=Pn.t.rearrange("p (r n) -> p n r", r=4), axis=AX.X, op=ALU.add)
                X("gpsimd", "tensor_copy", [Pn], [Pb], out=Pb.t, in_=Pn.t)
                pst = PS[2]
                p64c = pst.t[0:64, 0:256].bitcast(BF16).rearrange("p (r t) -> p r t", r=4)
                for r in range(4):
                    TR(p64c[:, r, :], Pb.t[:, r * 64:(r + 1) * 64], ident_b.t, [Pb, ident_b], [pst])
                PT = L["PT"]
                PTv = PT.t.rearrange("p (r t) -> p r t", r=4)
                S.op("scalar", lambda e: e.activation(out=PTv, in_=p64c, func=AF.Copy), [pst], [PT])
                poc = PS[4]
                for r in range(4):
                    MM(poc.t[:, r * 64:(r + 1) * 64], PTv[:, r, :], VCv[:, g, :], True, True, [PT, L["VC"]], [poc])
                Gv = L["G"].t[:, ti * 48 + g * 12: ti * 48 + (g + 1) * 12].rearrange("p (r b) -> p r b", b=3)
                Ofv = Of.t[:, g * 256:(g + 1) * 256].rearrange("p (r d) -> p r d", r=4)
                X("vector", "tensor_tensor", [poc, L["G"]], [Of], out=Ofv, in0=poc.t[:, 0:256].rearrange("p (r d) -> p r d", r=4),
                  in1=Gv[:, :, 0:1].to_broadcast([128, 4, 64]), op=ALU.mult)
                X("vector", "tensor_tensor", [sc, tb], [sc2], out=sc2.t, in0=sc.t, in1=tbv[:, 1, :], op=ALU.mult)
                X("vector", "tensor_tensor", [sc2, tb], [sc2], out=sc2.t, in0=sc2.t, in1=tbv[:, 2, :], op=ALU.add)
                S.op("vector", lambda e: e.max(out=m8.t, in_=sc2.t), [sc2], [m8])
                S.op("vector", lambda e: e.match_replace(out=wk16.t, in_to_replace=m8.t, in_values=sc2.t, imm_value=-2.0), [sc2, m8], [wk16])
                S.op("vector", lambda e: e.max(out=m8.t, in_=wk16.t), [wk16], [m8])
                S.op("vector", lambda e: e.match_replace(out=wk16.t, in_to_replace=m8.t, in_values=wk16.t, imm_value=-2.0), [wk16, m8], [wk16])
                X("vector", "tensor_tensor", [sc2, wk16], [wk16], out=wk16.t, in0=sc2.t, in1=wk16.t, op=ALU.subtract)
                X("vector", "tensor_scalar", [wk16], [selb], out=selb.t, in0=wk16.t, scalar1=1.0, scalar2=None, op0=ALU.min)
                kbl = [(kb, "sel") for kb in range(i)] + [(i, 0)]
                pos = PS[6]
                attn_branch(i, g, QTv, L["SKT"], L["SV"], kbl, None, pos, sel_ex=list(range(i + 1)))
                add_branch(pos, g, ti, 1)
                wbl = []
                for kt in range(max(0, i - 4), i + 1):
                    dlt = i - kt
                    wbl.append((kt % 8, 0 if dlt == 0 else (1 if dlt == 4 else None)))
                pow_ = PS[7]
                attn_branch(i, g, QTv, L["WKT"], L["WV"], wbl, None, pow_)
                add_branch(pow_, g, ti, 2)
            Otok = L["Otok"]
            S.op("scalar", lambda e: e.activation(out=Otok.t, in_=Of.t, func=AF.Copy), [Of], [Otok])
            pstv = pst.t[:, 0:512].bitcast(BF16).rearrange("p (c t) -> p c t", c=8)
            for c in range(8):
                TR(pstv[:, c, :], Otok.t[:, c * 128:(c + 1) * 128], ident_b.t, [Otok, ident_b], [pst])
            X("vector", "tensor_copy", [pst], [qmT], out=qmTv[:, :, ti * 128:(ti + 1) * 128], in_=pstv)

        def l1_sweepB(gi):
            r0 = gi * GT
            fence([actT, omT, pmT, QT, oT, ycT, acc], [qmT])
            DMA("sync", hv, h1[r0:r0 + GT, :].rearrange("(t p) d -> p t d", p=128), [], [hb])
            rmsnorm_to_aT(4, GCOL["mix1"])
            load_rope(gi)
            qtokv = L["qtok"].t.rearrange("p (t n) -> p t n", t=4)
            for half in range(2):
                wq, wqv = load_w(w_in_odd, 0, 8, half * 512, 512)
                for ti in range(4):
                    ps = next_ps()
                    proj_tm(wq, wqv, ti, 512, ps)
                    rope(ps.t.rearrange("p (h d) -> p h d", h=8), qtokv[:, ti, half * 512:(half + 1) * 512].rearrange("p (h d) -> p h d", h=8), 8,
                         L["ropeC"].t[:, ti * 32:(ti + 1) * 32], L["ropeS"].t[:, ti * 32:(ti + 1) * 32], [ps, L["ropeC"], L["ropeS"]], [L["qtok"]])
            ww, wwv = load_w(w_in_odd, 0, 8, 2048, 512)
            for ti in range(4):
                tile = gi * 4 + ti
                ps = next_ps()
                proj_tm(ww, wwv, ti, 512, ps)
                kv_tile_to_residents(ps, tile, ti, L["WKT"], tile % 8, L["WV"], tile % 8, winkp, winvp, (tile - 28) * 128 if tile >= 28 else None)
            wg_, wgv_ = load_w(w_in_odd, 0, 8, 2560, 48)
            for ti in range(4):
                ps = next_ps()
                proj_tm(wg_, wgv_, ti, 48, ps)
                S.op("scalar", lambda e, ps=ps, ti=ti, Gt=L["G"].t: e.activation(out=Gt[:, ti * 48:(ti + 1) * 48], in_=ps.t[:, 0:48], func=AF.Sigmoid), [ps], [L["G"]])
            for ti in range(4):
                l1_attn_tile(gi, ti)
            residual_proj(qmT, qmTv, w_mix_out[1], 8, 4)
            mem_attn(1, 4, "mem1")
            ffn(1, 4, "ffn1")
            DMA("sync", L["Of"].t, gfin.to_broadcast([128, D]), [], [L["Of"]])
            for ti in range(4):
                k = smi[0] % 8
                smi[0] += 1
                ss = small.t[:, k * 4:k * 4 + 1]
                sd = small.t[:, k * 4 + 1:k * 4 + 2]
                rs = small.t[:, k * 4 + 2:k * 4 + 3]
                at = atok[ti % 2]
                S.op("scalar", lambda e, at=at, ti=ti, ss=ss: e.activation(out=at.t, in_=hv[:, ti, :], func=AF.Square, accum_out=ss), [hb], [at, small])
                X("vector", "tensor_scalar", [small], [small], out=sd, in0=ss, scalar1=1.0 / D, scalar2=EPS, op0=ALU.mult, op1=ALU.add)
                S.op("scalar", lambda e, sd=sd: e.activation(out=sd, in_=sd, func=AF.Sqrt), [small], [small])
                X("vector", "reciprocal", [small], [small], out=rs, in_=sd)
                X("vector", "scalar_tensor_tensor", [hb, small, L["Of"]], [hb], out=hv[:, ti, :], in0=hv[:, ti, :], scalar=rs, in1=L["Of"].t, op0=ALU.mult, op1=ALU.mult)
            DMA("sync", yp[r0:r0 + GT, :].rearrange("(t p) d -> p t d", p=128), hv, [hb], [], is_out=True)


        Z = {}

        def sample_alloc():
            cur[0] = l0_base
            Z["ptb"] = Buf(AR[:, cur[0]:cur[0] + 256].bitcast(I32), "ptb"); cur[0] += 256
            Z["idx"] = Buf(AR[:, cur[0]:cur[0] + 256].bitcast(I32), "idx"); cur[0] += 256
            Z["ptf"] = alloc32("ptf", 256)
            Z["iota"] = alloc32("iota", 1)
            Z["Kst"] = [alloc16("Kst%d" % i, 2048) for i in range(1)]
            Z["Vst"] = [alloc16("Vst%d" % i, 2048) for i in range(1)]
            Z["KTm"] = [alloc16("KTm%d" % i, 2048) for i in range(1)]
            Z["base"] = cur[0]
            Z["hps"] = alloc32("hps", 4 * 16 * 38)
            Z["glu"] = alloc32("glu", 512)
            Z["accs"] = alloc32("accs", 512)
            Z["lnA"] = alloc32("lnA", 128)
            Z["lnB"] = alloc32("lnB", 128)
            Z["lnC"] = alloc32("lnC", 128)
            Z["lnD"] = alloc32("lnD", 128)
            Z["mixT"] = alloc16("mixT", 8 * 128)
            Z["QTs"] = alloc16("QTs", 4 * 128)
            Z["KTs"] = alloc16("KTs", 4 * 128)
            Z["Qbd"] = alloc16("Qbd", 64)
            Z["vtok"] = alloc16("vtok", 512)
            Z["Vn"] = alloc16("Vn", 512)
            Z["Kp"] = [alloc16("Kp%d" % i, 512) for i in range(2)]
            Z["Vp"] = [alloc16("Vp%d" % i, 512) for i in range(2)]
            Z["KpT"] = [alloc16("KpT%d" % i, 512) for i in range(2)]
            Z["E"] = [alloc32("sE%d" % i, 64) for i in range(2)]
            Z["SP"] = [alloc32("sSP%d" % i, 64) for i in range(2)]
            Z["TM"] = [alloc32("sTM%d" % i, 64) for i in range(2)]
            Z["W"] = [alloc16("sW%d" % i, 64) for i in range(2)]
            Z["SPacc"] = alloc32("sSPacc", 64)
            Z["mnew"] = alloc32("mnew", 64)
            print("sample arena words used", cur[0])

        def sample_setup():
            DMA("sync", Z["ptb"].t, ptab.to_broadcast([128, 256]), [], [Z["ptb"]])
            DMA("sync", Z["iota"].t, iota, [], [Z["iota"]])
            DMA("sync", Z["mnew"].t[0:8, :], c_masknew, [], [Z["mnew"]])
            X("vector", "tensor_copy", [Z["ptb"]], [Z["ptf"]], out=Z["ptf"].t, in_=Z["ptb"].t)
            X("vector", "tensor_scalar", [Z["ptf"], Z["iota"]], [Z["idx"]], out=Z["idx"].t, in0=Z["ptf"].t, scalar1=128.0, scalar2=Z["iota"].t[:, 0:1], op0=ALU.mult, op1=ALU.add)

        def gather(dst, rows_ap, j):
            idxb = Z["idx"]
            return S.dma("gpsimd", lambda e: e.indirect_dma_start(out=dst.t, out_offset=None, in_=rows_ap,
                                                                   in_offset=bass.IndirectOffsetOnAxis(ap=idxb.t[:, j:j + 1], axis=0)), [idxb], [dst])

        def sample_l0():
            fence([actT, qmT, omT, pmT], [QT, oT, ycT, acc])
            DMA("sync", hv[:, 0, :], xs, [], [hb])
            rmsnorm_to_aT(1, GCOL["mix0"])
            hps, glu, accs = Z["hps"], Z["glu"], Z["accs"]
            hpsv = hps.t.rearrange("p (c s t) -> p c s t", c=4, s=16)
            gluv = glu.t.rearrange("p (c t) -> p c t", c=4)
            accsv = accs.t.rearrange("p (c t) -> p c t", c=4)
            sc2d = sconv.rearrange("s t c -> (s t) c")
            for rt in range(4):
                st = next_stg()
                DMA("sync", st.t[0:120, :], sc2d[rt * 120:(rt + 1) * 120, :], [], [st])
                for cc in range(4):
                    ps = next_ps()
                    MM(ps.t[:, 0:120], st.t[0:120, cc * 128:(cc + 1) * 128], ident_f.t[0:120, 0:120], True, True, [st, ident_f], [ps])
                    S.op("scalar", lambda e, ps=ps, cc=cc, rt=rt: e.activation(out=hpsv[:, cc, rt * 4:(rt + 1) * 4, 0:30], in_=ps.t[:, 0:120].rearrange("p (s t) -> p s t", s=4), func=AF.Copy), [ps], [hps])
            DMA("sync", convs[:, 0:22, :], sconv[:, 8:30, :], [], [], is_out=True)
            wval, wvalv = load_w(w_in_even, 0, 8, 0, 512)
            for cc in range(4):
                ps = next_ps()
                proj_fm(wval, wvalv, cc, 128, ps)
                S.op("scalar", lambda e, ps=ps, cc=cc: e.activation(out=accsv[:, cc, :], in_=ps.t[:, 0:128], func=AF.Copy), [ps], [accs])
            wgt, wgtv = load_w(w_in_even, 0, 8, 512, 512)
            for cc in range(4):
                ps = next_ps()
                proj_fm(wgt, wgtv, cc, 128, ps)
                sg = next_stg()
                S.op("scalar", lambda e, ps=ps, sg=sg: e.activation(out=sg.t[:, 0:128], in_=ps.t[:, 0:128], func=AF.Sigmoid), [ps], [sg])
                X("vector", "tensor_tensor", [accs, sg], [glu], out=gluv[:, cc, :], in0=accsv[:, cc, :], in1=sg.t[:, 0:128], op=ALU.mult)
                X("vector", "tensor_copy", [glu], [hps], out=hpsv[:, cc, :, 30:38], in_=gluv[:, cc, :].rearrange("p (s t) -> p s t", s=16))
            pt_ = PS[3]
            for cc in range(4):
                MM(pt_.t[:, cc * 128:(cc + 1) * 128], gluv[:, cc, :], ident_f.t, True, True, [glu, ident_f], [pt_])
            stc = next_stg()
            S.op("scalar", lambda e: e.activation(out=stc.t, in_=pt_.t, func=AF.Copy), [pt_], [stc])
            for sq in range(16):
                DMA("sync", convs[sq, 22:30, :], stc.t[sq * 8:(sq + 1) * 8, :], [stc], [], is_out=True)
            QTs, KTs = Z["QTs"], Z["KTs"]
            QTsv = QTs.t.rearrange("p (c t) -> p c t", c=4)
            KTsv = KTs.t.rearrange("p (c t) -> p c t", c=4)
            wq, wqv = load_w(w_in_even, 0, 8, 1024, 512)
            for pj in range(4):
                ps = next_ps()
                proj_fm(wq, wqv, pj, 128, ps)
                S.op("scalar", lambda e, ps=ps, pj=pj: e.activation(out=QTsv[:, pj, :], in_=ps.t[:, 0:128], func=AF.Copy), [ps], [QTs])
            wk, wkv = load_w(w_in_even, 0, 8, 1536, 512)
            for pj in range(4):
                ps = next_ps()
                proj_fm(wk, wkv, pj, 128, ps)
                S.op("scalar", lambda e, ps=ps, pj=pj: e.activation(out=KTsv[:, pj, :], in_=ps.t[:, 0:128], func=AF.Copy), [ps], [KTs])
            ps = next_ps()
            proj_tm(wk, wkv, 0, 512, ps)
            st = next_stg()
            S.op("scalar", lambda e, ps=ps, st=st: e.activation(out=st.t, in_=ps.t, func=AF.Copy), [ps], [st])
            DMA("sync", sbks, st.t, [st], [], is_out=True)
            wv_, wvv = load_w(w_in_even, 0, 8, 2048, 512)
            ps = next_ps()
            proj_tm(wv_, wvv, 0, 512, ps)
            st = next_stg()
            S.op("scalar", lambda e, ps=ps, st=st: e.activation(out=st.t, in_=ps.t, func=AF.Copy), [ps], [st])
            X("vector", "tensor_copy", [ps], [Z["vtok"]], out=Z["vtok"].t, in_=ps.t)
            DMA("sync", sbvs, st.t, [st], [], is_out=True)
            for cc in range(4):
                av = accsv[:, cc, :].rearrange("p (s t) -> p s t", s=16)
                X("vector", "tensor_scalar", [hps, cw, vec], [accs], out=av, in0=hpsv[:, cc, :, 0:8], scalar1=cwv[:, cc, 0:1], scalar2=vec.t[:, 56 + cc:57 + cc], op0=ALU.mult, op1=ALU.add)
                for jj in range(1, 31):
                    X("vector", "scalar_tensor_tensor", [hps, cw, accs], [accs], out=av, in0=hpsv[:, cc, :, jj:jj + 8], scalar=cwv[:, cc, jj:jj + 1], in1=av, op0=ALU.mult, op1=ALU.add)
            pm, pq = PS[3], PS[4]
            sq_ = Z["lnA"]
            for cc in range(4):
                MM(pm.t[:, 0:128], div512.t, accsv[:, cc, :], cc == 0, cc == 3, [div512, accs], [pm])
            for cc in range(4):
                S.op("scalar", lambda e, cc=cc: e.activation(out=sq_.t, in_=accsv[:, cc, :], func=AF.Square), [accs], [sq_])
                MM(pq.t[:, 0:128], div512.t, sq_.t, cc == 0, cc == 3, [div512, sq_], [pq])
            mean, var, rstd, tm = Z["lnB"], Z["lnC"], Z["lnD"], Z["lnA"]
            S.op("scalar", lambda e: e.activation(out=mean.t, in_=pm.t[:, 0:128], func=AF.Copy), [pm], [mean])
            X("gpsimd", "tensor_tensor", [mean], [var], out=var.t, in0=mean.t, in1=mean.t, op=ALU.mult)
            X("vector", "tensor_tensor", [pq, var], [var], out=var.t, in0=pq.t[:, 0:128], in1=var.t, op=ALU.subtract)
            X("vector", "tensor_scalar", [var], [var], out=var.t, in0=var.t, scalar1=EPS, scalar2=None, op0=ALU.add)
            S.op("scalar", lambda e: e.activation(out=var.t, in_=var.t, func=AF.Sqrt), [var], [var])
            X("vector", "reciprocal", [var], [rstd], out=rstd.t, in_=var.t)
            mixT = Z["mixT"]
            mixTv = mixT.t.rearrange("p (c t) -> p c t", c=8)
            for cc in range(4):
                X("vector", "tensor_tensor", [accs, mean], [tm], out=tm.t, in0=accsv[:, cc, :], in1=mean.t, op=ALU.subtract)
                X("vector", "tensor_tensor", [tm, rstd], [tm], out=tm.t, in0=tm.t, in1=rstd.t, op=ALU.mult)
                S.op("scalar", lambda e, cc=cc: e.activation(out=mixTv[:, cc, :], in_=tm.t, func=AF.Silu, scale=vec.t[:, 60 + cc:61 + cc], bias=vec.t[:, 64 + cc:65 + cc]), [tm, vec], [mixT])
            Qbd = Z["Qbd"]
            Qbdv = Qbd.t.rearrange("p (c q) -> p c q", c=4)
            X("vector", "memset", [], [Qbd], ap=Qbd.t, constant=0.0)
            Vn = Z["Vn"]
            SPacc = Z["SPacc"]
            mnew = Z["mnew"]
            for b in range(16):
                bs = slice(b * 8, (b + 1) * 8)
                X("vector", "tensor_copy", [QTs], [Qbd], out=Qbdv[0:64, :, 0:8], in_=QTsv[0:64, :, bs])
                X("vector", "tensor_copy", [QTs], [Qbd], out=Qbdv[64:128, :, 8:16], in_=QTsv[64:128, :, bs])
                DMA("sync", Vn.t[0:8, :], Z["vtok"].t[bs, :], [Z["vtok"]], [Vn])
                po = PS[6 + b % 2]
                X("vector", "memset", [], [SPacc], ap=SPacc.t, constant=0.0)
                for it, blk in enumerate([16] + list(reversed(range(16)))):
                    new = blk == 16
                    nk = 8 if new else 128
                    E, SP, TM, Wt = Z["E"][it % 2], Z["SP"][it % 2], Z["TM"][it % 2], Z["W"][it % 2]
                    ps = next_ps()
                    if new:
                        for pj in range(4):
                            MM(ps.t[0:8, pj * 16:(pj + 1) * 16], KTsv[:, pj, bs], Qbdv[:, pj, :], True, True, [KTs, Qbd], [ps])
                        Vsrc, Vb_ = Vn.t, Vn
                    else:
                        Kp, Vp, KpT = Z["Kp"][it % 2], Z["Vp"][it % 2], Z["KpT"][it % 2]
                        gather(Kp, sbk_rows, b * 16 + blk)
                        gather(Vp, sbv_rows, b * 16 + blk)
                        pst = PS[2]
                        pstv = pst.t[:, 0:256].bitcast(BF16).rearrange("p (c t) -> p c t", c=4)
                        for pj in range(4):
                            TR(pstv[:, pj, :], Kp.t[:, pj * 128:(pj + 1) * 128], ident_b.t, [Kp, ident_b], [pst])
                        KpTv = KpT.t.rearrange("p (c t) -> p c t", c=4)
                        S.op("scalar", lambda e, KpTv=KpTv, pstv=pstv: e.activation(out=KpTv, in_=pstv, func=AF.Copy), [pst], [KpT])
                        for pj in range(4):
                            MM(ps.t[:, pj * 16:(pj + 1) * 16], KpTv[:, pj, :], Qbdv[:, pj, :], True, True, [KpT, Qbd], [ps])
                        Vsrc, Vb_ = Vp.t, Vp
                    kp = slice(0, nk)
                    S.op("scalar", lambda e, ps=ps, E=E, kp=kp: e.activation(out=E.t[kp, :], in_=ps.t[kp, 0:64], func=AF.Exp, scale=0.125), [ps], [E])
                    S.op("scalar", lambda e, E=E, SP=SP, kp=kp: e.activation(out=SP.t[kp, :], in_=E.t[kp, :], func=AF.Ln, bias=1.0), [E], [SP])
                    X("vector", "scalar_tensor_tensor", [ps, SP], [TM], out=TM.t[kp, :], in0=ps.t[kp, 0:64], scalar=0.125, in1=SP.t[kp, :], op0=ALU.mult, op1=ALU.subtract)
                    pa = PS[4 + it % 2]
                    if new:
                        X("vector", "tensor_tensor", [SP, mnew], [SP], out=SP.t[kp, :], in0=SP.t[kp, :], in1=mnew.t[kp, :], op=ALU.mult)
                        MM(pa.t[kp, 0:64], negtri.t[0:8, 0:8], SP.t[kp, :], True, True, [negtri, SP], [pa])
                    else:
                        MM(pa.t[:, 0:64], negtri.t, SP.t, True, False, [negtri, SP], [pa])
                        MM(pa.t[:, 0:64], negones.t, SPacc.t, False, True, [negones, SPacc], [pa])
                    X("vector", "tensor_tensor", [TM, pa], [TM], out=TM.t[kp, :], in0=TM.t[kp, :], in1=pa.t[kp, 0:64], op=ALU.add)
                    S.op("scalar", lambda e, TM=TM, Wt=Wt, kp=kp: e.activation(out=Wt.t[kp, :], in_=TM.t[kp, :], func=AF.Exp), [TM], [Wt])
                    if new:
                        X("vector", "tensor_tensor", [Wt, mnew], [Wt], out=Wt.t[kp, :], in0=Wt.t[kp, :], in1=mnew.t[kp, :], op=ALU.mult)
                        X("gpsimd", "tensor_copy", [SP], [SPacc], out=SPacc.t[kp, :], in_=SP.t[kp, :])
                    elif blk > 0:
                        X("gpsimd", "tensor_tensor", [SP, SPacc], [SPacc], out=SPacc.t, in0=SPacc.t, in1=SP.t, op=ALU.add)
                    for hd in range(8):
                        pj, hh = hd // 2, hd % 2
                        MM(po.t[hh * 64:(hh + 1) * 64, pj * 8:(pj + 1) * 8], Vsrc[kp, hd * 64:(hd + 1) * 64], Wt.t[kp, hd * 8:(hd + 1) * 8],
                           (new and pj == 0), (blk == 0 and pj == 3), [Vb_, Wt], [po])
                S.op("scalar", lambda e, po=po, bs=bs: e.activation(out=mixTv[:, 4:8, bs], in_=po.t[:, 0:32].rearrange("p (c q) -> p c q", c=4), func=AF.Copy), [po], [mixT])
            residual_proj(mixT, mixTv, w_mix_out[0], 8, 1)

        def sample_mem_attn(l, gname):
            mem_q_proj(l, 1, gname)
            for b in range(16):
                Kst, Vst, KTm = Z["Kst"][0], Z["Vst"][0], Z["KTm"][0]
                Kstv = Kst.t.rearrange("p (h d) -> p h d", h=2)
                Vstv = Vst.t.rearrange("p (h d) -> p h d", h=2)
                KTmv = KTm.t.rearrange("p (c m) -> p c m", c=8)
                DMA("gpsimd", Kstv, cmk[l, b].rearrange("(h p) d -> p h d", p=128), [], [Kst])
                DMA("gpsimd", Vstv, cmv[l, b].rearrange("(h p) d -> p h d", p=128), [], [Vst])
                for mh in range(2):
                    pst = PS[2]
                    pstv = pst.t[:, 0:512].bitcast(BF16).rearrange("p (c t) -> p c t", c=8)
                    for c in range(8):
                        TR(pstv[:, c, :], Kstv[:, mh, c * 128:(c + 1) * 128], ident_b.t, [Kst, ident_b], [pst])
                    X("vector", "tensor_copy", [pst], [KTm], out=KTmv[:, :, mh * 128:(mh + 1) * 128], in_=pstv)
                mem_attn_core(b * 8, 8, KTm, KTmv, Vst, Vstv)
            residual_proj(omT, omTv, w_mem_o[l], 8, 1)


        def sample_alloc1():
            cur[0] = Z["base"]
            for k_, n_, p_ in [("pe", 64, 128), ("ropeb", 64, 64), ("Of", 1024, 128), ("sm", 32, 128), ("Ec", 256, 128), ("Pn", 256, 128),
                               ("sc", 65, 128), ("sc2", 65, 128), ("wk16", 65, 128), ("m8", 8, 128), ("Ot", 256, 128), ("G", 16 * 48, 128), ("m3", 256, 128)]:
                L[k_] = alloc32(k_ + "_s", n_, parts=p_)
            for k_, n_, p_ in [("CKVT", 4 * T, 128), ("KCT", 256, 64), ("VC", 256, 64), ("w2", 256, 128), ("hid", 512, 128), ("kcb", 256, 64),
                               ("Pb", 256, 128), ("PT", 64, 64), ("selb", 66, 128), ("selx", 128, 128), ("Otok", 1024, 128)]:
                L[k_] = alloc16(k_ + "_s", n_, parts=p_)
            Z["rope"] = alloc32("rope_s", 64)
            Z["tbs"] = alloc32("tbs", 2 * 3 * 65)
            Z["qtok"] = alloc16("qtok_s", 1024)
            Z["QT1"] = alloc16("QT1s", 16 * 128, parts=64)
            Z["SKTn"] = alloc16("SKTn", 4 * 128, parts=64)
            Z["WKTn"] = alloc16("WKTn", 4 * 128, parts=64)
            Z["svtok"] = alloc16("svtok", 260)
            Z["wvtok"] = alloc16("wvtok", 260)
            Z["SVn"] = alloc16("SVn", 260)
            Z["WVn"] = alloc16("WVn", 260)
            Z["CKp"] = [alloc16("CKp%d" % i, 512) for i in range(2)]
            Z["SKp"] = [alloc16("SKp%d" % i, 256) for i in range(2)]
            Z["SKTb"] = alloc16("SKTb", 4 * 2048, parts=64)
            Z["SVp"] = alloc16("SVp", 16 * 260)
            Z["WKst"] = alloc16("WKst", 4 * 256)
            Z["WKTb"] = alloc16("WKTb", 4 * 512, parts=64)
            Z["WVp"] = alloc16("WVp", 4 * 260)
            Z["graw"] = [alloc16("graw%d" % i, 256) for i in range(3)]
            Z["Es"] = [alloc16("sEs%d" % i, 32) for i in range(2)]
            Z["Pms"] = [alloc16("sPms%d" % i, 32) for i in range(2)]
            print("sample L1 arena words used", cur[0])

        def s_branch(g, bs, blocks, pso):
            QT1v = Z["QT1"].t.rearrange("p (h t) -> p h t", h=16)
            nb = len(blocks)
            for bi, (kt_ap, ktb, v_ap, vb, nk, mk, mkb, pre) in enumerate(blocks):
                kp = slice(0, nk)
                pss = next_ps()
                MM(pss.t[kp, 0:32], kt_ap, QT1v[:, 4 * g:4 * g + 4, bs], True, True, [ktb, Z["QT1"]], [pss])
                Es = Z["Es"][bi % 2]
                S.op("scalar", lambda e, pss=pss, Es=Es, kp=kp: e.activation(out=Es.t[kp, :], in_=pss.t[kp, 0:32], func=AF.Exp, scale=0.125), [pss], [Es])
                src = Es
                if mk is not None or pre is not None:
                    if pre is not None:
                        mk, mkb = pre()
                    Pm = Z["Pms"][bi % 2]
                    X("vector", "tensor_tensor", [Es] + mkb, [Pm], out=Pm.t[kp, :].rearrange("p (r t) -> p r t", r=4),
                      in0=Es.t[kp, :].rearrange("p (r t) -> p r t", r=4), in1=bc(mk, 1, [nk, 4, 8]), op=ALU.mult)
                    src = Pm
                for r in range(4):
                    MM(pso.t[0:8, r * 65:(r + 1) * 65], src.t[kp, r * 8:(r + 1) * 8], v_ap, (bi == 0 and r == 0), (bi == nb - 1 and r == 3), [src, vb], [pso])

        def s_add(pso, g, b, br):
            Gv = L["G"].t[0:8, b * 48 + g * 12: b * 48 + (g + 1) * 12].rearrange("p (r b) -> p r b", b=3)
            pv = pso.t[0:8, 0:260].rearrange("p (r e) -> p r e", e=65)
            sm = L["sm"]
            cf = sm.t[0:8, 8:12]
            X("vector", "reciprocal", [pso], [sm], out=cf.unsqueeze(2), in_=pv[:, :, 64:65])
            X("vector", "tensor_tensor", [sm, L["G"]], [sm], out=cf.unsqueeze(2), in0=cf.unsqueeze(2), in1=Gv[:, :, br:br + 1], op=ALU.mult)
            Ot = L["Ot"]
            Otv = Ot.t[0:8, :].rearrange("p (r d) -> p r d", r=4)
            Ofv = L["Of"].t[0:8, g * 256:(g + 1) * 256].rearrange("p (r d) -> p r d", r=4)
            X("vector", "tensor_tensor", [pso, sm], [Ot], out=Otv, in0=pv[:, :, 0:64], in1=cf.unsqueeze(2).to_broadcast([8, 4, 64]), op=ALU.mult)
            X("vector", "tensor_tensor", [Ot, L["Of"]], [L["Of"]], out=Ofv, in0=Ofv, in1=Otv, op=ALU.add)

        def sample_l1():
            fence([actT, omT, pmT, QT, oT, ycT, acc], [qmT])
            rmsnorm_to_aT(1, GCOL["mix1"])
            DMA("sync", L["pe"].t, peT, [], [L["pe"]])
            DMA("sync", L["ropeb"].t, ropeb_s, [], [L["ropeb"]])
            DMA("sync", L["m3"].t.rearrange("p (j q) -> p j q", j=2), c_m3, [], [L["m3"]])
            DMA("gpsimd", L["w2"].t.rearrange("p (k c e) -> p k c e", k=2, c=2), w2kv.rearrange("k (c p) e -> p k c e", p=128), [], [L["w2"]])
            DMA("sync", Z["rope"].t, ropecs, [], [Z["rope"]])
            DMA("sync", Z["tbs"].t[0:8, :], tabs_s, [], [Z["tbs"]])
            cs, sn = Z["rope"].t[:, 0:32], Z["rope"].t[:, 32:64]
            qtok = Z["qtok"]
            for half in range(2):
                wq, wqv = load_w(w_in_odd, 0, 8, half * 512, 512)
                ps = next_ps()
                proj_tm(wq, wqv, 0, 512, ps)
                rope(ps.t.rearrange("p (h d) -> p h d", h=8), qtok.t[:, half * 512:(half + 1) * 512].rearrange("p (h d) -> p h d", h=8), 8, cs, sn, [ps, Z["rope"]], [qtok])
            QT1 = Z["QT1"]
            QT1v = QT1.t.rearrange("p (h t) -> p h t", h=16)
            pst = PS[2]
            p64 = pst.t[0:64, 0:512].bitcast(BF16).rearrange("p (h t) -> p h t", h=8)
            for b8 in range(2):
                for hh in range(8):
                    hd = b8 * 8 + hh
                    TR(p64[:, hh, :], qtok.t[:, hd * 64:(hd + 1) * 64], ident_b.t, [qtok, ident_b], [pst])
                S.op("scalar", lambda e, b8=b8: e.activation(out=QT1v[:, b8 * 8:(b8 + 1) * 8, :], in_=p64, func=AF.Copy), [pst], [QT1])
            wc, wcv = load_w(w_in_odd, 0, 8, 1024, 512)
            ps = next_ps()
            proj_tm(wc, wcv, 0, 512, ps)
            st = next_stg()
            S.op("scalar", lambda e, ps=ps, st=st: e.activation(out=st.t, in_=ps.t, func=AF.Copy), [ps], [st])
            DMA("sync", cmpks, st.t[:, 0:256], [st], [], is_out=True)
            DMA("sync", cmpvs, st.t[:, 256:512], [st], [], is_out=True)

            def new_kv(c0, KTn, vtok, outk, outv, winout):
                wbuf, wv = load_w(w_in_odd, 0, 8, c0, 512)
                ps = next_ps()
                proj_tm(wbuf, wv, 0, 512, ps)
                st = next_stg()
                rope(ps.t[:, 0:256].rearrange("p (h d) -> p h d", h=4), st.t[:, 0:256].rearrange("p (h d) -> p h d", h=4), 4, cs, sn, [ps, Z["rope"]], [st])
                X("vector", "tensor_copy", [ps], [st], out=st.t[:, 256:512], in_=ps.t[:, 256:512])
                if not winout:
                    DMA("sync", outk, st.t[:, 0:256], [st], [], is_out=True)
                    DMA("sync", outv, st.t[:, 256:512], [st], [], is_out=True)
                else:
                    for sq in range(16):
                        DMA("sync", outk[sq, 504:512, :], st.t[sq * 8:(sq + 1) * 8, 0:256], [st], [], is_out=True)
                        DMA("sync", outv[sq, 504:512, :], st.t[sq * 8:(sq + 1) * 8, 256:512], [st], [], is_out=True)
                skb = L["Pb"]
                X("gpsimd", "tensor_copy", [st], [skb], out=skb.t, in_=st.t[:, 0:256])
                vv = vtok.t.rearrange("p (g e) -> p g e", g=4)
                X("vector", "memset", [], [vtok], ap=vv[:, :, 64:65], constant=1.0)
                X("gpsimd", "tensor_copy", [st], [vtok], out=vv[:, :, 0:64], in_=st.t[:, 256:512].rearrange("p (g e) -> p g e", g=4))
                pq = pst.t[0:64, 0:256].bitcast(BF16).rearrange("p (g t) -> p g t", g=4)
                for g in range(4):
                    TR(pq[:, g, :], skb.t[:, g * 64:(g + 1) * 64], ident_b.t, [skb, ident_b], [pst])
                S.op("scalar", lambda e: e.activation(out=KTn.t.rearrange("p (g t) -> p g t", g=4), in_=pq, func=AF.Copy), [pst], [KTn])

            new_kv(1536, Z["SKTn"], Z["svtok"], selks, selvs, False)
            new_kv(2048, Z["WKTn"], Z["wvtok"], winks, winvs, True)
            DMA("sync", winks[:, 0:504, :], cwk_d[:, 8:512, :], [], [], is_out=True)
            DMA("sync", winvs[:, 0:504, :], cwv_d[:, 8:512, :], [], [], is_out=True)
            wg_, wgv_ = load_w(w_in_odd, 0, 8, 2560, 48)
            for b in range(16):
                ps = next_ps()
                for c in range(8):
                    MM(ps.t[0:8, 0:48], aTv[:, c, b * 8:(b + 1) * 8], wgv_[:, c, 0:48], c == 0, c == 7, [wg_, aT], [ps])
                S.op("scalar", lambda e, ps=ps, b=b, Gt=L["G"].t: e.activation(out=Gt[0:8, b * 48:(b + 1) * 48], in_=ps.t[0:8, 0:48], func=AF.Sigmoid), [ps], [L["G"]])
            SVpv = Z["SVp"].t.rearrange("p (t g e) -> p t g e", t=16, g=4)
            WVpv = Z["WVp"].t.rearrange("p (t g e) -> p t g e", t=4, g=4)
            X("vector", "memset", [], [Z["SVp"]], ap=SVpv[:, :, :, 64:65], constant=1.0)
            X("vector", "memset", [], [Z["WVp"]], ap=WVpv[:, :, :, 64:65], constant=1.0)
            CKVT = L["CKVT"]
            CKVTv = CKVT.t.rearrange("p (g t) -> p g t", g=4)
            SKTb, WKTb = Z["SKTb"], Z["WKTb"]
            SKTbv = SKTb.t.rearrange("p (g t) -> p g t", g=4)
            WKTbv = WKTb.t.rearrange("p (g t) -> p g t", g=4)
            SKTnv = Z["SKTn"].t.rearrange("p (g t) -> p g t", g=4)
            WKTnv = Z["WKTn"].t.rearrange("p (g t) -> p g t", g=4)
            tbs = Z["tbs"]
            for pair in range(8):
                for s2 in range(2):
                    b = pair * 2 + s2
                    for pg in range(16):
                        CKp = Z["CKp"][pg % 2]
                        CKpv = CKp.t.rearrange("p (g e) -> p g e", g=4)
                        j = b * 16 + pg
                        idxb = Z["idx"]
                        rk, rv = Z["graw"][0], Z["graw"][1]
                        gather(rk, cck_rows, j)
                        gather(rv, ccv_rows, j)
                        X("gpsimd", "tensor_copy", [rk], [CKp], out=CKpv[:, :, 0:64], in_=rk.t.rearrange("p (g e) -> p g e", g=4))
                        X("vector", "tensor_copy", [rv], [CKp], out=CKpv[:, :, 64:128], in_=rv.t.rearrange("p (g e) -> p g e", g=4))
                        pstv = pst.t[:, 0:256].bitcast(BF16).rearrange("p (g t) -> p g t", g=4)
                        for g in range(4):
                            TR(pstv[:, g, :], CKpv[:, g, :], ident_b.t, [CKp, ident_b], [pst])
                        t0 = s2 * 2048 + pg * 128
                        for g in range(4):
                            X("vector", "tensor_tensor", [pst, L["pe"]], [CKVT], out=CKVTv[:, g, t0:t0 + 128].rearrange("p (n l) -> p n l", l=64),
                              in0=pstv[:, g, :].rearrange("p (n l) -> p n l", l=64), in1=bc(L["pe"].t, 1, [128, 2, 64]), op=ALU.add)
                l1_compress()
                KCTv = L["KCT"].t.rearrange("p (g n) -> p g n", g=4)
                VCv = L["VC"].t.rearrange("p (g e) -> p g e", g=4)
                for s2 in range(2):
                    b = pair * 2 + s2
                    bs = slice(b * 8, (b + 1) * 8)
                    tbv = tbs.t[0:8, s2 * 195:(s2 + 1) * 195].rearrange("p (j n) -> p j n", j=3)
                    for pg in range(16):
                        SKp = Z["SKp"][pg % 2]
                        gather(SKp, csk_rows, b * 16 + pg)
                        idxb = Z["idx"]
                        j = b * 16 + pg
                        rs_ = Z["graw"][2]
                        gather(rs_, csv_rows, j)
                        X("gpsimd", "tensor_copy", [rs_], [Z["SVp"]], out=SVpv[:, pg, :, 0:64], in_=rs_.t.rearrange("p (g e) -> p g e", g=4))
                        pq = pst.t[0:64, 0:256].bitcast(BF16).rearrange("p (g t) -> p g t", g=4)
                        for g in range(4):
                            TR(pq[:, g, :], SKp.t[:, g * 64:(g + 1) * 64], ident_b.t, [SKp, ident_b], [pst])
                        S.op("scalar", lambda e, pg=pg, pq=pq: e.activation(out=SKTbv[:, :, pg * 128:(pg + 1) * 128], in_=pq, func=AF.Copy), [pst], [SKTb])
                    WKst = Z["WKst"]
                    WKstv = WKst.t.rearrange("p (t n) -> p t n", t=4)
                    DMA("gpsimd", WKstv, cwk_d[b].rearrange("(t p) n -> p t n", p=128), [], [WKst])
                    for wt in range(4):
                        DMA("gpsimd", WVpv[:, wt, :, 0:64], cwv_d[b, wt * 128:(wt + 1) * 128, :].rearrange("p (g e) -> p g e", g=4), [], [Z["WVp"]])
                    for wt in range(4):
                        pq = pst.t[0:64, 0:256].bitcast(BF16).rearrange("p (g t) -> p g t", g=4)
                        for g in range(4):
                            TR(pq[:, g, :], WKstv[:, wt, g * 64:(g + 1) * 64], ident_b.t, [WKst, ident_b], [pst])
                        S.op("scalar", lambda e, wt=wt, pq=pq: e.activation(out=WKTbv[:, :, wt * 128:(wt + 1) * 128], in_=pq, func=AF.Copy), [pst], [WKTb])
                    DMA("sync", Z["SVn"].t[0:8, :], Z["svtok"].t[bs, :], [Z["svtok"]], [Z["SVn"]])
                    DMA("sync", Z["WVn"].t[0:8, :], Z["wvtok"].t[bs, :], [Z["wvtok"]], [Z["WVn"]])
                    SVnv = Z["SVn"].t.rearrange("p (g e) -> p g e", g=4)
                    WVnv = Z["WVn"].t.rearrange("p (g e) -> p g e", g=4)
                    Of = L["Of"]
                    for g in range(4):
                        psc = PS[3]
                        for r in range(4):
                            MM(psc.t[0:8, r * 64:(r + 1) * 64], QT1v[:, 4 * g + r, bs], KCTv[:, g, :], True, True, [QT1, L["KCT"]], [psc])
                        sm = L["sm"]
                        pscv = psc.t[0:8, 0:256].rearrange("p (r n) -> p r n", r=4)
                        X("vector", "tensor_reduce", [psc], [sm], out=sm.t[0:8, 0:4], in_=pscv, axis=AX.X, op=ALU.max)
                        X("vector", "tensor_scalar", [sm], [sm], out=sm.t[0:8, 0:4], in0=sm.t[0:8, 0:4], scalar1=-0.125, scalar2=None, op0=ALU.mult)
                        Ec, Pn, Pb = L["Ec"], L["Pn"], L["Pb"]
                        for r in range(4):
                            S.op("scalar", lambda e, r=r, psc=psc, Ec=Ec, sm=sm: e.activation(out=Ec.t[0:8, r * 64:(r + 1) * 64], in_=psc.t[0:8, r * 64:(r + 1) * 64], func=AF.Exp, scale=0.125, bias=sm.t[0:8, r:r + 1]), [psc, sm], [Ec])
                        Ecv = Ec.t[0:8, :].rearrange("p (r n) -> p r n", r=4)
                        Pnv = Pn.t[0:8, :].rearrange("p (r n) -> p r n", r=4)
                        X("vector", "tensor_tensor", [Ec, tbs], [Ec], out=Ecv, in0=Ecv, in1=bc(tbv[:, 0, 0:64], 1, [8, 4, 64]), op=ALU.mult)
                        X("vector", "tensor_reduce", [Ec], [sm], out=sm.t[0:8, 4:8], in_=Ecv, axis=AX.X, op=ALU.add)
                        X("vector", "tensor_scalar", [sm], [sm], out=sm.t[0:8, 4:8], in0=sm.t[0:8, 4:8], scalar1=1e-30, scalar2=None, op0=ALU.max)
                        X("vector", "reciprocal", [sm], [sm], out=sm.t[0:8, 4:8], in_=sm.t[0:8, 4:8])
                        X("vector", "tensor_tensor", [Ec, sm], [Pn], out=Pnv, in0=Ecv, in1=sm.t[0:8, 4:8].unsqueeze(2).to_broadcast([8, 4, 64]), op=ALU.mult)
                        sc, sc2, wk16, m8, selb = L["sc"], L["sc2"], L["wk16"], L["m8"], L["selb"]
                        X("vector", "memset", [], [sc], ap=sc.t[0:8, 64:65], constant=0.0)
                        X("vector", "tensor_reduce", [Pn], [sc], out=sc.t[0:8, 0:64], in_=Pn.t[0:8, :].rearrange("p (r n) -> p n r", r=4), axis=AX.X, op=ALU.add)
                        X("gpsimd", "tensor_copy", [Pn], [Pb], out=Pb.t[0:8, :], in_=Pn.t[0:8, :])
                        p64c = pst.t[0:64, 0:16].bitcast(BF16).rearrange("p (r t) -> p r t", r=4)
                        for r in range(4):
                            TR(p64c[:, r, :], Pb.t[0:8, r * 64:(r + 1) * 64], ident_b.t[0:8, 0:8], [Pb, ident_b], [pst])
                        PT = L["PT"]
                        PTv = PT.t[:, 0:32].rearrange("p (r t) -> p r t", r=4)
                        S.op("scalar", lambda e, PTv=PTv, p64c=p64c: e.activation(out=PTv, in_=p64c, func=AF.Copy), [pst], [PT])
                        poc = PS[4]
                        for r in range(4):
                            MM(poc.t[0:8, r * 64:(r + 1) * 64], PTv[:, r, :], VCv[:, g, :], True, True, [PT, L["VC"]], [poc])
                        Gv = L["G"].t[0:8, b * 48 + g * 12: b * 48 + (g + 1) * 12].rearrange("p (r b) -> p r b", b=3)
                        Ofv = Of.t[0:8, g * 256:(g + 1) * 256].rearrange("p (r d) -> p r d", r=4)
                        X("vector", "tensor_tensor", [poc, L["G"]], [Of], out=Ofv, in0=poc.t[0:8, 0:256].rearrange("p (r d) -> p r d", r=4),
                          in1=Gv[:, :, 0:1].to_broadcast([8, 4, 64]), op=ALU.mult)
                        X("vector", "tensor_tensor", [sc, tbs], [sc2], out=sc2.t[0:8, :], in0=sc.t[0:8, :], in1=tbv[:, 1, :], op=ALU.mult)
                        X("vector", "tensor_tensor", [sc2, tbs], [sc2], out=sc2.t[0:8, :], in0=sc2.t[0:8, :], in1=tbv[:, 2, :], op=ALU.add)
                        S.op("vector", lambda e, m8=m8, sc2=sc2: e.max(out=m8.t[0:8, :], in_=sc2.t[0:8, :]), [sc2], [m8])
                        S.op("vector", lambda e, m8=m8, sc2=sc2, wk16=wk16: e.match_replace(out=wk16.t[0:8, :], in_to_replace=m8.t[0:8, :], in_values=sc2.t[0:8, :], imm_value=-2.0), [sc2, m8], [wk16])
                        S.op("vector", lambda e, m8=m8, wk16=wk16: e.max(out=m8.t[0:8, :], in_=wk16.t[0:8, :]), [wk16], [m8])
                        S.op("vector", lambda e, m8=m8, wk16=wk16: e.match_replace(out=wk16.t[0:8, :], in_to_replace=m8.t[0:8, :], in_values=wk16.t[0:8, :], imm_value=-2.0), [wk16, m8], [wk16])
                        X("vector", "tensor_tensor", [sc2, wk16], [wk16], out=wk16.t[0:8, :], in0=sc2.t[0:8, :], in1=wk16.t[0:8, :], op=ALU.subtract)
                        X("vector", "tensor_scalar", [wk16], [selb], out=selb.t[0:8, 0:65], in0=wk16.t[0:8, :], scalar1=1.0, scalar2=None, op0=ALU.min)
                        blocks = [(SKTnv[:, g, bs], Z["SKTn"], SVnv[0:8, g, :], Z["SVn"], 8, L["m3"].t[0:8, 0:8], [L["m3"]], None)]
                        for pg in range(16):
                            def pre(pg=pg, s2=s2):
                                selx = L["selx"]
                                c0 = s2 * 32 + 2 * pg
                                X("gpsimd", "tensor_copy", [L["selb"]], [selx], out=selx.t[0:8, :].rearrange("p (n l) -> p n l", n=2),
                                  in_=L["selb"].t[0:8, c0:c0 + 2].unsqueeze(2).to_broadcast([8, 2, 64]))
                                psm = PS[5]
                                pmb = psm.t[:, 0:4].bitcast(BF16)
                                TR(pmb, selx.t[0:8, :], ident_b.t[0:8, 0:8], [selx, ident_b], [psm])
                                return pmb, [psm]
                            blocks.append((SKTbv[:, g, pg * 128:(pg + 1) * 128], SKTb, SVpv[:, pg, g, :], Z["SVp"], 128, None, None, pre))
                        pos = PS[6]
                        s_branch(g, bs, blocks, pos)
                        s_add(pos, g, b, 1)
                        blocks = [(WKTnv[:, g, bs], Z["WKTn"], WVnv[0:8, g, :], Z["WVn"], 8, L["m3"].t[0:8, 0:8], [L["m3"]], None)]
                        for wt in range(4):
                            mk = L["m3"].t[:, 128:136] if wt == 0 else None
                            blocks.append((WKTbv[:, g, wt * 128:(wt + 1) * 128], WKTb, WVpv[:, wt, g, :], Z["WVp"], 128, mk, [L["m3"]] if wt == 0 else None, None))
                        pow_ = PS[7]
                        s_branch(g, bs, blocks, pow_)
                        s_add(pow_, g, b, 2)
                    Otok = L["Otok"]
                    S.op("scalar", lambda e, Otok=Otok, Of=Of: e.activation(out=Otok.t[0:8, :], in_=Of.t[0:8, :], func=AF.Copy), [Of], [Otok])
                    pstv8 = pst.t[:, 0:32].bitcast(BF16).rearrange("p (c t) -> p c t", c=8)
                    for c in range(8):
                        TR(pstv8[:, c, :], Otok.t[0:8, c * 128:(c + 1) * 128], ident_b.t[0:8, 0:8], [Otok, ident_b], [pst])
                    X("vector", "tensor_copy", [pst], [qmT], out=qmTv[:, :, bs], in_=pstv8)
            residual_proj(qmT, qmTv, w_mix_out[1], 8, 1)

        def sample_final():
            DMA("sync", L["Of"].t, gfin.to_broadcast([128, D]), [], [L["Of"]])
            k = smi[0] % 8
            smi[0] += 1
            ss = small.t[:, k * 4:k * 4 + 1]
            sd = small.t[:, k * 4 + 1:k * 4 + 2]
            rs = small.t[:, k * 4 + 2:k * 4 + 3]
            at = atok[0]
            S.op("scalar", lambda e: e.activation(out=at.t, in_=hv[:, 0, :], func=AF.Square, accum_out=ss), [hb], [at, small])
            X("vector", "tensor_scalar", [small], [small], out=sd, in0=ss, scalar1=1.0 / D, scalar2=EPS, op0=ALU.mult, op1=ALU.add)
            S.op("scalar", lambda e: e.activation(out=sd, in_=sd, func=AF.Sqrt), [small], [small])
            X("vector", "reciprocal", [small], [small], out=rs, in_=sd)
            X("vector", "scalar_tensor_tensor", [hb, small, L["Of"]], [hb], out=hv[:, 0, :], in0=hv[:, 0, :], scalar=rs, in1=L["Of"].t, op0=ALU.mult, op1=ALU.mult)
            DMA("sync", ys, hv[:, 0, :], [hb], [], is_out=True)

        if KSUB >= -1:
            X("vector", "memset", [], [hp], ap=hp.t, constant=0.0)
        if KSUB >= 0:
            prompt_mem_kv(0)
        else:
            DMA("sync", memkp[0, 0:128, 0:128], ident_f.t, [ident_f], [], is_out=True)
        ngroups = NG if STAGE >= 2 else 1
        if os.environ.get("KONLY", "") == "sample":
            ngroups = 0
        ngroups = int(os.environ.get('KNG', ngroups))
        if KSUB < 1:
            ngroups = 0
        for gi in range(ngroups):
            prompt_l0_group(gi)
        if STAGE >= 3 and os.environ.get("KONLY", "") != "sample":
            S.barrier()
            l1_alloc()
            l1_consts()
            prompt_mem_kv(1)
            for gi in range(ngroups):
                l1_sweepA(gi)
            l1_compress()
            S.barrier()
            WVv = L["WV"].t.rearrange("p (t g e) -> p t g e", t=8, g=4)
            X("vector", "memset", [], [L["WV"]], ap=WVv[:, :, :, 64:65], constant=1.0)
            nb_ = int(os.environ.get("KNB", ngroups))
            for gi in range(nb_):
                l1_sweepB(gi)
        if STAGE >= 4:
            S.barrier()
            sample_alloc()
            sample_setup()
            sample_l0()
            sample_mem_attn(0, "mem0")
            ffn(0, 1, "ffn0")
            if STAGE == 4:
                DMA("sync", ys, hv[:, 0, :], [hb], [], is_out=True)
            if STAGE >= 5:
                S.barrier()
                sample_alloc1()
                sample_l1()
                sample_mem_attn(1, "mem1")
                ffn(1, 1, "ffn1")
                sample_final()
        S.finish()
        print("op counts", S.counts)
    return P


_PROG = None
OUT_IDX = [0, 2, 3, 4, 5, 6, 7, 8, 9, 10, 11, 12, 1, 13, 14, 15, 16, 17, 18, 19, 20, 21]


def _tabs_s():
    out = np.zeros((8, 2, 3, 65), np.float32)
    for s2 in range(2):
        own = np.zeros(65, bool)
        own[s2 * 32:(s2 + 1) * 32] = True
        forced = np.zeros(65, bool)
        forced[[s2 * 32, s2 * 32 + 31, 64]] = True
        out[:, s2, 0, :] = own
        out[:, s2, 1, :] = own & ~forced
        out[:, s2, 2, :] = np.where(forced, 1.0e4, np.where(own, 0.0, -1.0))
    return np.ascontiguousarray(out.reshape(8, 390))


def _consts():
    k = np.arange(128)[:, None]
    q = np.arange(512)[None, :]
    sbmask = np.stack([(j * 128 + k < q).astype(np.float32) for j in range(4)])
    kk = np.arange(128)
    negtri = -(kk[:, None] > kk[None, :]).astype(np.float32)
    t = np.arange(T)
    inv = (10000.0 ** (-np.arange(32, dtype=np.float32) / 32)).astype(np.float32)
    ang = t[:, None].astype(np.float32) * inv[None, :]
    bpos = (np.arange(64) * 64 + 63).astype(np.float32)
    angb = bpos[:, None] * inv[None, :]
    n = np.arange(64)[None, :]
    curb = (t // 64)[:, None]
    valid = (n * 64 + 63 <= t[:, None])
    forced = (n == 0) | (n == curb) | (n == curb - 1)
    invalid = n > curb
    tabs = np.stack([valid.astype(np.float32), (~(forced | invalid)).astype(np.float32),
                     np.where(invalid, -1.0, np.where(forced, 1.0e4, 0.0)).astype(np.float32)], axis=1)
    kk2 = np.arange(128)
    m3 = np.stack([(kk2[:, None] <= kk2[None, :]).astype(np.float32), (kk2[:, None] > kk2[None, :]).astype(np.float32)], axis=1)
    ex = (np.arange(T)[None, :] // 64 == np.arange(64)[:, None]).astype(np.float32)
    return {
        "ropec": np.cos(ang).astype(np.float32), "ropes": np.sin(ang).astype(np.float32),
        "ropeb": np.concatenate([np.cos(angb), np.sin(angb)], axis=1).astype(np.float32),
        "tabs": np.ascontiguousarray(tabs), "c_m3": np.ascontiguousarray(m3), "c_ex": ex,
        "iota": np.arange(128, dtype=np.float32).reshape(128, 1),
        "c_masknew": np.tile((np.arange(8)[:, None] < np.arange(8)[None, :]).astype(np.float32), (1, 8)),
        "ropecs": np.concatenate([np.cos((2048 + (np.arange(128) % 8))[:, None].astype(np.float32) * inv[None, :]),
                                  np.sin((2048 + (np.arange(128) % 8))[:, None].astype(np.float32) * inv[None, :])], axis=1).astype(np.float32),
        "ropeb_s": np.concatenate([np.cos(angb[np.arange(64) % 32]), np.sin(angb[np.arange(64) % 32])], axis=1).astype(np.float32),
        "tabs_s": _tabs_s(),
        "c_ident": np.eye(128, dtype=np.float32),
        "c_ones": np.ones((128, 128), np.float32),
        "c_negtri": negtri,
        "c_negones": -np.ones((128, 128), np.float32),
        "c_div512": np.full((128, 128), 1.0 / 512, np.float32),
        "c_sbmask": sbmask,
    }


def kernel(**inp):
    global _PROG
    if _PROG is None:
        _PROG = build_program()
    P = _PROG
    f = lambda k: np.ascontiguousarray(np.asarray(inp[k], dtype=np.float32))
    cst = _consts()

    def col(v):
        v = np.asarray(v, np.float32)
        return np.ascontiguousarray(v.reshape(-1, 128).T)

    vecs = np.zeros((128, 80), np.float32)
    vecs[:, 0:8] = col(inp["norm_mix"][0])
    vecs[:, 8:16] = col(inp["norm_mix"][1])
    vecs[:, 16:24] = col(inp["norm_mem"][0])
    vecs[:, 24:32] = col(inp["norm_mem"][1])
    vecs[:, 32:40] = col(inp["norm_ffn"][0])
    vecs[:, 40:48] = col(inp["norm_ffn"][1])
    vecs[:, 48:56] = col(inp["final_norm"])
    vecs[:, 56:60] = col(inp["conv_b"][0])
    vecs[:, 60:64] = col(inp["conv_ln_g"][0])
    vecs[:, 64:68] = col(inp["conv_ln_b"][0])
    shared = {
        "vecs": vecs,
        "cwT": np.ascontiguousarray(np.asarray(inp["conv_w"][0], np.float32).T),
        "w_in_even": f("w_in_even")[0],
        "w_in_odd": f("w_in_odd")[0],
        "w1kv": np.ascontiguousarray(np.concatenate([f("cmp_w1_k")[0].transpose(1, 0, 2), f("cmp_w1_v")[0].transpose(1, 0, 2)], axis=0)),
        "w2kv": np.ascontiguousarray(np.stack([f("cmp_w2_k")[0], f("cmp_w2_v")[0]])),
        "peT": np.ascontiguousarray(np.concatenate([f("cmp_pe_k")[0].T, f("cmp_pe_v")[0].T], axis=0)),
        "gfin": f("final_norm").reshape(1, D),
        "w_mix_out": f("w_mix_out"),
        "w_mem_q": f("w_mem_q"), "w_mem_k": f("w_mem_k"), "w_mem_v": f("w_mem_v"), "w_mem_o": f("w_mem_o"),
        "w_ffn_gate": f("w_ffn_gate"), "w_ffn_up": f("w_ffn_up"), "w_ffn_down": f("w_ffn_down"),
    }
    shared.update(cst)
    xpr = f("x_prompt")
    mpr = f("mem_prompt")
    shared["sbk_rows"] = f("cache_sb_k").reshape(2560 * 128, 512)
    shared["sbv_rows"] = f("cache_sb_v").reshape(2560 * 128, 512)
    for nm, key in [("cck_rows", "cache_nsa_cmp_k"), ("ccv_rows", "cache_nsa_cmp_v"), ("csk_rows", "cache_nsa_sel_k"), ("csv_rows", "cache_nsa_sel_v")]:
        shared[nm] = f(key).reshape(2560 * 128, 256)
    cwkr = f("cache_nsa_win_k")[0].reshape(NCORES, 16, 512, 256)
    cwvr = f("cache_nsa_win_v")[0].reshape(NCORES, 16, 512, 256)
    xsr = f("x_sample").reshape(NCORES, 128, D)
    ptr = np.ascontiguousarray(np.asarray(inp["page_table"], np.int32).reshape(NCORES, 1, 256))
    scv = f("state_conv")[0].reshape(NCORES, 16, 30, 512)
    cmkr = f("cache_mem_k").reshape(2, NCORES, 16, 256, D)
    cmvr = f("cache_mem_v").reshape(2, NCORES, 16, 256, D)
    in_maps = []
    for c in range(NCORES):
        b = c // 2
        m = dict(shared)
        m["xp"] = xpr[b]
        m["memp"] = mpr[b]
        m["xs"] = xsr[c]
        m["ptab"] = ptr[c]
        m["sconv"] = scv[c]
        m["cmk"] = np.ascontiguousarray(cmkr[:, c])
        m["cmv"] = np.ascontiguousarray(cmvr[:, c])
        m["cwk"] = cwkr[c]
        m["cwv"] = cwvr[c]
        in_maps.append({k: m[k] for k in P.din})
    res = run_bass_kernel_spmd(P.nc, in_maps, core_ids=list(range(NCORES)))
    R = res.results
    ev = [R[2 * b] for b in range(4)]
    y_prompt = np.stack([r["yp"] for r in ev])
    sb_k_p = np.stack([r["sbkp"] for r in ev]).reshape(1, 4, T, 8, 64)
    sb_v_p = np.stack([r["sbvp"] for r in ev]).reshape(1, 4, T, 8, 64)
    conv_p = np.stack([r["convp"] for r in ev]).reshape(1, 4, 30, 512)
    mem_k_p = np.stack([r["memkp"] for r in ev], axis=1).reshape(2, 4, 256, 4, 256)
    mem_v_p = np.stack([r["memvp"] for r in ev], axis=1).reshape(2, 4, 256, 4, 256)
    st4 = lambda k, n: np.stack([r[k] for r in ev]).reshape(1, 4, n, 4, 64)
    outs = (y_prompt, sb_k_p, sb_v_p, conv_p, st4("cmpkp", T), st4("cmpvp", T), st4("selkp", T), st4("selvp", T),
            st4("winkp", 512), st4("winvp", 512), mem_k_p, mem_v_p,
            np.concatenate([r["ys"] for r in R]).reshape(128, 8, D),
            np.concatenate([r["sbks"] for r in R]).reshape(1, 128, 8, 8, 64),
            np.concatenate([r["sbvs"] for r in R]).reshape(1, 128, 8, 8, 64),
            np.concatenate([r["convs"] for r in R]).reshape(1, 128, 30, 512),
            np.concatenate([r["cmpks"] for r in R]).reshape(1, 128, 8, 4, 64),
            np.concatenate([r["cmpvs"] for r in R]).reshape(1, 128, 8, 4, 64),
            np.concatenate([r["selks"] for r in R]).reshape(1, 128, 8, 4, 64),
            np.concatenate([r["selvs"] for r in R]).reshape(1, 128, 8, 4, 64),
            np.concatenate([r["winks"] for r in R]).reshape(1, 128, 512, 4, 64),
            np.concatenate([r["winvs"] for r in R]).reshape(1, 128, 512, 4, 64))
    full = [None] * 22
    for o, i in zip(outs, OUT_IDX):
        full[i] = np.ascontiguousarray(o, dtype=np.float32)
    return tuple(full)
```

```python
import numpy as np
from contextlib import ExitStack
import concourse.bass as bass
import concourse.mybir as mybir
from concourse.bass_utils import run_bass_kernel_spmd

F32 = mybir.dt.float32
BF16 = mybir.dt.bfloat16
I32 = mybir.dt.int32
AF = mybir.ActivationFunctionType
ALU = mybir.AluOpType

NCORES = 8
T = 4096
D = 1024
GT = 512
NG = T // GT
DFF = 2816
NFC = DFF // 128
EPS = 1e-6

import os
STAGE = int(os.environ.get("KSTAGE", "5"))
KSUB = int(os.environ.get("KSUB", "99"))
KM = int(os.environ.get("KM", "99"))


class Buf:
    __slots__ = ("t", "name", "w", "r", "excl")

    def __init__(self, t, name, excl=False):
        self.t = t
        self.name = name
        self.w = None
        self.r = []
        self.excl = excl


class Op:
    __slots__ = ("eng", "fn", "deps", "is_dma", "signal", "sem", "semval", "waits", "out")

    def __init__(self, eng, fn, is_dma):
        self.eng = eng
        self.fn = fn
        self.deps = []
        self.is_dma = is_dma
        self.signal = False
        self.sem = None
        self.semval = 0
        self.waits = []
        self.out = False


ENGS = ("tensor", "vector", "scalar", "gpsimd", "sync")
SAME_ENGINE_SYNC = True
NDMA_SEMS = 48


class Sched:
    def __init__(self, nc, es):
        self.nc = nc
        self.es = es
        self.ops = []
        self.dma_slot_last = [None] * NDMA_SEMS
        self.dma_slot_cnt = [0] * NDMA_SEMS
        self.dma_next = 0
        self.out_ops = []
        self.last = {e: None for e in ENGS}
        self.dmas_since_barrier = []

    def _add(self, op, reads, writes):
        deps = []
        for b in reads:
            if b.w is not None:
                deps.append(b.w)
            if b.excl:
                deps.extend(o for o in b.r if o.eng != op.eng)
        for b in writes:
            if b.w is not None:
                deps.append(b.w)
            deps.extend(b.r)
        for b in reads:
            b.r.append(op)
        for b in writes:
            b.w = op
            b.r = []
        seen = set()
        for d in deps:
            if d is op or id(d) in seen:
                continue
            seen.add(id(d))
            op.deps.append(d)
        self.ops.append(op)
        self.last[op.eng] = op
        return op

    def op(self, eng, fn, reads, writes):
        return self._add(Op(eng, fn, False), reads, writes)

    def dma(self, eng, fn, reads, writes, out=False):
        op = Op(eng, fn, True)
        slot = self.dma_next
        self.dma_next = (self.dma_next + 1) % NDMA_SEMS
        prev = self.dma_slot_last[slot]
        self.dma_slot_cnt[slot] += 1
        op.sem = slot
        op.semval = 16 * self.dma_slot_cnt[slot]
        op.signal = True
        self.dma_slot_last[slot] = op
        self._add(op, reads, writes)
        if prev is not None and prev not in op.deps:
            op.deps.append(prev)
        if out:
            op.out = True
            self.out_ops.append(op)
        self.dmas_since_barrier.append(op)
        return op

    def barrier(self):
        lasts = [o for o in self.last.values() if o is not None]
        dm = list(self.dmas_since_barrier)
        self.dmas_since_barrier = []
        for e in ENGS:
            op = Op(e, None, False)
            op.deps = [d for d in lasts + dm]
            self.ops.append(op)
            self.last[e] = op

    def finish(self):
        nc = self.nc
        es = self.es
        esem = {e: es.enter_context(nc.semaphore("s_" + e)) for e in ENGS}
        dsem = [es.enter_context(nc.semaphore("d_%d" % i)) for i in range(NDMA_SEMS)]
        for op in self.ops:
            for d in op.deps:
                if not d.is_dma and d.fn is not None:
                    if d.eng != op.eng or (SAME_ENGINE_SYNC and d.eng != "tensor"):
                        d.signal = True
        cnt = {e: 0 for e in ENGS}
        for op in self.ops:
            if op.is_dma:
                op.sem = dsem[op.sem]
            elif op.signal:
                cnt[op.eng] += 1
                op.sem = esem[op.eng]
                op.semval = cnt[op.eng]
        seen = {e: {} for e in ENGS}
        for op in self.ops:
            sm = seen[op.eng]
            for d in op.deps:
                if d.fn is None:
                    continue
                if not d.is_dma and d.eng == op.eng and not (SAME_ENGINE_SYNC and d.eng != "tensor"):
                    continue
                if not d.signal:
                    continue
                k = id(d.sem)
                if sm.get(k, 0) >= d.semval:
                    continue
                sm[k] = d.semval
                op.waits.append((d.sem, d.semval))
        streams = {e: [o for o in self.ops if o.eng == e] for e in ENGS}
        fm = {}
        for o in self.out_ops:
            k = id(o.sem)
            if fm.get(k, (None, 0))[1] < o.semval:
                fm[k] = (o.sem, o.semval)
        final = list(fm.values())
        with nc.Block() as block:
            def run(e, name):
                for op in streams[name]:
                    for (s, v) in op.waits:
                        e.wait_ge(s, v)
                    if op.fn is None:
                        continue
                    ins = op.fn(e)
                    if op.signal:
                        ins.then_inc(op.sem, 16 if op.is_dma else 1)
                if name == "sync":
                    for (s, v) in final:
                        e.wait_ge(s, v)

            @block.tensor
            def _(e):
                run(e, "tensor")

            @block.vector
            def _(e):
                run(e, "vector")

            @block.scalar
            def _(e):
                run(e, "scalar")

            @block.gpsimd
            def _(e):
                run(e, "gpsimd")

            @block.sync
            def _(e):
                run(e, "sync")
        self.counts = {e: len(streams[e]) for e in ENGS}


class Prog:
    def __init__(self):
        self.nc = bass.Bass("TRN2", target_bir_lowering=False)
        self.din = {}
        self.dout = {}

    def inp(self, name, shape, dtype=F32):
        self.din[name] = self.nc.dram_tensor(name, list(shape), dtype, kind="ExternalInput").ap()
        return self.din[name]

    def outp(self, name, shape, dtype=F32):
        self.dout[name] = self.nc.dram_tensor(name, list(shape), dtype, kind="ExternalOutput").ap()
        return self.dout[name]


def build_program():
    P = Prog()
    nc = P.nc
    xp = P.inp("xp", [T, D])
    memp = P.inp("memp", [256, D])
    vecs = P.inp("vecs", [128, 80])
    cwT = P.inp("cwT", [512, 31])
    c_ident = P.inp("c_ident", [128, 128])
    c_ones = P.inp("c_ones", [128, 128])
    c_negtri = P.inp("c_negtri", [128, 128])
    c_negones = P.inp("c_negones", [128, 128])
    c_div512 = P.inp("c_div512", [128, 128])
    c_sbmask = P.inp("c_sbmask", [4, 128, 512])
    w_in_even = P.inp("w_in_even", [D, 2560])
    w_mix_out = P.inp("w_mix_out", [2, D, D])
    w_mem_q = P.inp("w_mem_q", [2, D, D])
    w_mem_k = P.inp("w_mem_k", [2, D, D])
    w_mem_v = P.inp("w_mem_v", [2, D, D])
    w_mem_o = P.inp("w_mem_o", [2, D, D])
    w_ffn_gate = P.inp("w_ffn_gate", [2, D, DFF])
    w_ffn_up = P.inp("w_ffn_up", [2, D, DFF])
    w_ffn_down = P.inp("w_ffn_down", [2, DFF, D])

    w_in_odd = P.inp("w_in_odd", [D, 2608])
    w1kv = P.inp("w1kv", [128, 64, 256])
    w2kv = P.inp("w2kv", [2, 256, 64])
    peT = P.inp("peT", [128, 64])
    ropec = P.inp("ropec", [T, 32])
    ropes = P.inp("ropes", [T, 32])
    ropeb = P.inp("ropeb", [64, 64])
    tabs = P.inp("tabs", [T, 3, 64])
    c_ex = P.inp("c_ex", [64, T])
    c_m3 = P.inp("c_m3", [128, 2, 128])
    gfin = P.inp("gfin", [1, D])
    cmpkp = P.outp("cmpkp", [T, 256])
    cmpvp = P.outp("cmpvp", [T, 256])
    selkp = P.outp("selkp", [T, 256])
    selvp = P.outp("selvp", [T, 256])
    winkp = P.outp("winkp", [512, 256])
    winvp = P.outp("winvp", [512, 256])
    xs = P.inp("xs", [128, D])
    ptab = P.inp("ptab", [1, 256], I32)
    iota = P.inp("iota", [128, 1])
    sbk_rows = P.inp("sbk_rows", [2560 * 128, 512])
    sbv_rows = P.inp("sbv_rows", [2560 * 128, 512])
    sconv = P.inp("sconv", [16, 30, 512])
    cmk = P.inp("cmk", [2, 16, 256, D])
    cmv = P.inp("cmv", [2, 16, 256, D])
    c_masknew = P.inp("c_masknew", [8, 64])
    cck_rows = P.inp("cck_rows", [2560 * 128, 256])
    ccv_rows = P.inp("ccv_rows", [2560 * 128, 256])
    csk_rows = P.inp("csk_rows", [2560 * 128, 256])
    csv_rows = P.inp("csv_rows", [2560 * 128, 256])
    cwk_d = P.inp("cwk", [16, 512, 256])
    cwv_d = P.inp("cwv", [16, 512, 256])
    ropecs = P.inp("ropecs", [128, 64])
    ropeb_s = P.inp("ropeb_s", [64, 64])
    tabs_s = P.inp("tabs_s", [8, 2 * 3 * 65])
    cmpks = P.outp("cmpks", [128, 256])
    cmpvs = P.outp("cmpvs", [128, 256])
    selks = P.outp("selks", [128, 256])
    selvs = P.outp("selvs", [128, 256])
    winks = P.outp("winks", [16, 512, 256])
    winvs = P.outp("winvs", [16, 512, 256])
    ys = P.outp("ys", [128, D])
    sbks = P.outp("sbks", [128, 512])
    sbvs = P.outp("sbvs", [128, 512])
    convs = P.outp("convs", [16, 30, 512])
    yp = P.outp("yp", [T, D])
    sbkp = P.outp("sbkp", [T, 512])
    sbvp = P.outp("sbvp", [T, 512])
    convp = P.outp("convp", [30, 512])
    memkp = P.outp("memkp", [2, 256, D])
    memvp = P.outp("memvp", [2, 256, D])

    with ExitStack() as es:
        S = Sched(nc, es)
        AR = es.enter_context(nc.sbuf_tensor("arena", [128, 52900], F32))
        PS = [Buf(es.enter_context(nc.psum_tensor("ps%d" % i, [128, 512], F32))[:, :], "ps%d" % i, True) for i in range(8)]
        cur = [0]

        def alloc32(name, n, parts=128):
            o = cur[0]
            cur[0] += n
            assert cur[0] <= 52900, (name, cur[0])
            return Buf(AR[0:parts, o:o + n], name)

        def alloc16(name, nel, parts=128):
            n = (nel + 1) // 2
            o = cur[0]
            cur[0] += n
            assert cur[0] <= 52900, (name, cur[0])
            return Buf(AR[0:parts, o:o + n].bitcast(BF16), name)

        def X(eng, meth, R, W, **kw):
            return S.op(eng, lambda e: getattr(e, meth)(**kw), R, W)

        def MM(out, lhsT, rhs, start, stop, R, W):
            return S.op("tensor", lambda e: e.matmul(out, lhsT=lhsT, rhs=rhs, start=start, stop=stop), R, W)

        def TR(out, in_, ident, R, W):
            return S.op("tensor", lambda e: e.transpose(out=out, in_=in_, identity=ident), R, W)

        def DMA(eng, out, in_, R, W, is_out=False):
            return S.dma(eng, lambda e: e.dma_start(out=out, in_=in_), R, W, out=is_out)

        ident_f = alloc32("ident_f", 128)
        negtri = alloc32("negtri", 128)
        negones = alloc32("negones", 128)
        div512 = alloc32("div512", 128)
        vec = alloc32("vec", 80)
        ident_b = alloc16("ident_b", 128)
        ones_b = alloc16("ones_b", 128)
        cw = alloc32("cw", 4 * 31)
        DMA("sync", ident_f.t, c_ident, [], [ident_f])
        DMA("sync", negtri.t, c_negtri, [], [negtri])
        DMA("sync", negones.t, c_negones, [], [negones])
        DMA("sync", div512.t, c_div512, [], [div512])
        DMA("sync", vec.t, vecs, [], [vec])
        DMA("gpsimd", ident_b.t, c_ident, [], [ident_b])
        DMA("gpsimd", ones_b.t, c_ones, [], [ones_b])
        DMA("sync", cw.t.rearrange("p (c j) -> p c j", c=4), cwT.rearrange("(c p) j -> p c j", p=128), [], [cw])
        cwv = cw.t.rearrange("p (c j) -> p c j", c=4)
        GCOL = {"mix0": 0, "mix1": 8, "mem0": 16, "mem1": 24, "ffn0": 32, "ffn1": 40, "fin": 48}

        memKT = alloc16("memKT", 8 * 256)
        memV = alloc16("memV", 2 * 1024)
        memKTv = memKT.t.rearrange("p (c m) -> p c m", c=8)
        memVv = memV.t.rearrange("p (h n) -> p h n", h=2)
        hb = alloc32("h", 4 * 1024)
        hv = hb.t.rearrange("p (t d) -> p t d", t=4)
        aT = alloc16("aT", 8 * 512)
        aTv = aT.t.rearrange("p (c t) -> p c t", c=8)
        atok = [alloc16("atok%d" % i, 1024) for i in range(2)]
        NWB = 3
        wb = [alloc16("wb%d" % i, 8 * 512) for i in range(NWB)]
        wbv = [w.t.rearrange("p (c n) -> p c n", c=8) for w in wb]
        wbi = [0]
        stg = [alloc32("stg%d" % i, 512) for i in range(3)]
        stgi = [0]
        small = alloc32("small", 64)
        smi = [0]

        def next_stg():
            s = stg[stgi[0] % 3]
            stgi[0] += 1
            return s

        wcache = {}

        def load_w(src, k0, nch, c0, ncols):
            i = wbi[0] % NWB
            wbi[0] += 1
            blk = src[k0:k0 + nch * 128, c0:c0 + ncols]
            key = repr(blk)
            dst = wbv[i][:, 0:nch, 0:ncols]
            if key not in wcache:
                DMA("gpsimd", dst, blk.rearrange("(c p) n -> p c n", p=128), [], [wb[i]])
                sc_t = nc.dram_tensor("wsc%d" % len(wcache), [128, nch, ncols], BF16, kind="Internal").ap()
                vb = Buf(None, "wsc")
                DMA("sync", sc_t, dst, [wb[i]], [vb])
                wcache[key] = (sc_t, vb)
            else:
                sc_t, vb = wcache[key]
                DMA("sync", dst, sc_t, [vb], [wb[i]])
            return wb[i], wbv[i]

        scr = alloc16("scrA", NFC * 512)
        scr_base = cur[0] - (NFC * 512) // 2

        def scr16(name, off_w, nel):
            return Buf(AR[:, scr_base + off_w: scr_base + off_w + nel // 2].bitcast(BF16), name)

        def scr32(name, off_w, n):
            return Buf(AR[:, scr_base + off_w: scr_base + off_w + n], name)

        QT = scr16("QT", 0, 4 * 512)
        oT = scr16("oT", 1024, 4 * 512)
        ycT = scr16("ycT", 2048, 4 * 512)
        acc = scr32("acc", 3072, 4 * 512)
        QTv = QT.t.rearrange("p (c t) -> p c t", c=4)
        oTv = oT.t.rearrange("p (c t) -> p c t", c=4)
        ycTv = ycT.t.rearrange("p (c t) -> p c t", c=4)
        accv = acc.t.rearrange("p (c t) -> p c t", c=4)
        actT = scr
        actTv = scr.t.rearrange("p (c t) -> p c t", c=NFC)
        qmT = scr16("qmT", 0, 8 * 512)
        omT = scr16("omT", 2048, 8 * 512)
        pmT = scr16("pmT", 4096, 2 * 512)
        qmTv = qmT.t.rearrange("p (c t) -> p c t", c=8)
        omTv = omT.t.rearrange("p (c t) -> p c t", c=8)
        pmTv = pmT.t.rearrange("p (c t) -> p c t", c=2)

        l0_base = cur[0]
        sbmask = alloc32("sbmask", 4 * 512)
        DMA("sync", sbmask.t.rearrange("p (j q) -> p j q", j=4), c_sbmask.rearrange("j p q -> p j q"), [], [sbmask])
        sbm = sbmask.t.rearrange("p (j q) -> p j q", j=4)
        KT = alloc16("KT", 4 * T)
        KTv = KT.t.rearrange("p (c t) -> p c t", c=4)
        Vr = alloc16("V", 32 * 512)
        Vv = Vr.t.rearrange("p (t n) -> p t n", t=32)
        hp = alloc32("hp", 4 * 542)
        hpv = hp.t.rearrange("p (c t) -> p c t", c=4)
        Eb = [alloc32("E%d" % i, 512) for i in range(2)]
        SPb = [alloc32("SP%d" % i, 512) for i in range(2)]
        TMb = [alloc32("TM%d" % i, 512) for i in range(2)]
        Wb = [alloc16("W%d" % i, 512) for i in range(2)]
        SPm = alloc32("SPm", 512)
        SPacc = alloc32("SPacc", 512)
        h1 = nc.dram_tensor("h1_scratch", [T, D], F32, kind="Internal").ap()
        print("arena words used", cur[0])

        def fence(olds, news):
            ops = []
            for o in olds:
                ops.extend(o.r)
                if o.w is not None:
                    ops.append(o.w)
            for n in news:
                n.r = list(n.r) + ops

        MIXB = None

        def rmsnorm_to_aT(ntiles, gcol):
            for ti in range(ntiles):
                at = atok[ti % 2]
                k = smi[0] % 8
                smi[0] += 1
                ss = small.t[:, k * 4:k * 4 + 1]
                sd = small.t[:, k * 4 + 1:k * 4 + 2]
                rs = small.t[:, k * 4 + 2:k * 4 + 3]
                S.op("scalar", lambda e, at=at, ti=ti, ss=ss: e.activation(out=at.t, in_=hv[:, ti, :], func=AF.Square, accum_out=ss), [hb], [at, small])
                X("vector", "tensor_scalar", [small], [small], out=sd, in0=ss, scalar1=1.0 / D, scalar2=EPS, op0=ALU.mult, op1=ALU.add)
                S.op("scalar", lambda e, sd=sd: e.activation(out=sd, in_=sd, func=AF.Sqrt), [small], [small])
                X("vector", "reciprocal", [small], [small], out=rs, in_=sd)
                X("vector", "tensor_scalar", [hb, small], [at], out=at.t, in0=hv[:, ti, :], scalar1=rs, scalar2=None, op0=ALU.mult)
                pst = PS[2]
                pstv = pst.t[:, 0:512].bitcast(BF16).rearrange("p (c t) -> p c t", c=8)
                for c in range(8):
                    TR(pstv[:, c, :], at.t[:, c * 128:(c + 1) * 128], ident_b.t, [at, ident_b], [pst])
                for c in range(8):
                    X("vector" if c % 2 == 0 else "gpsimd" if False else "vector", "tensor_scalar", [pst, vec], [aT],
                      out=aTv[:, c, ti * 128:(ti + 1) * 128], in0=pstv[:, c, :], scalar1=vec.t[:, gcol + c:gcol + c + 1], scalar2=None, op0=ALU.mult)

        psi = [0]

        def next_ps():
            p = PS[psi[0] % 2]
            psi[0] += 1
            return p

        def proj_fm(wbuf, wv, colchunk, ntok, ps, nch=8):
            for c in range(nch):
                MM(ps.t[:, 0:ntok], wv[:, c, colchunk * 128:(colchunk + 1) * 128], aTv[:, c, 0:ntok], c == 0, c == nch - 1, [wbuf, aT], [ps])

        def proj_tm(wbuf, wv, ti, ncols, ps, src=None, srcv=None, nch=8):
            sb = aT if src is None else src
            sv = aTv if srcv is None else srcv
            for c in range(nch):
                MM(ps.t[:, 0:ncols], sv[:, c, ti * 128:(ti + 1) * 128], wv[:, c, 0:ncols], c == 0, c == nch - 1, [wbuf, sb], [ps])

        def residual_proj(src, srcv, wsrc, nrowchunks, ntiles):
            for ch in range(2):
                pieces = []
                k0 = 0
                while k0 < nrowchunks:
                    n = min(8, nrowchunks - k0)
                    pieces.append((k0, n))
                    k0 += n
                if len(pieces) == 1:
                    wbuf, wv = load_w(wsrc, 0, nrowchunks, ch * 512, 512)
                    for ti in range(ntiles):
                        ps = next_ps()
                        for c in range(nrowchunks):
                            MM(ps.t[:, 0:512], srcv[:, c, ti * 128:(ti + 1) * 128], wv[:, c, 0:512], c == 0, c == nrowchunks - 1, [wbuf, src], [ps])
                        X("vector", "tensor_tensor", [hb, ps], [hb], out=hv[:, ti, ch * 512:(ch + 1) * 512], in0=hv[:, ti, ch * 512:(ch + 1) * 512], in1=ps.t[:, 0:512], op=ALU.add)
                else:
                    accs = [PS[4 + ti] for ti in range(ntiles)]
                    for pi, (k0, n) in enumerate(pieces):
                        wbuf, wv = load_w(wsrc, k0 * 128, n, ch * 512, 512)
                        for ti in range(ntiles):
                            for c in range(n):
                                MM(accs[ti].t[:, 0:512], srcv[:, k0 + c, ti * 128:(ti + 1) * 128], wv[:, c, 0:512],
                                   (pi == 0 and c == 0), (pi == len(pieces) - 1 and c == n - 1), [wbuf, src], [accs[ti]])
                    for ti in range(ntiles):
                        X("vector", "tensor_tensor", [hb, accs[ti]], [hb], out=hv[:, ti, ch * 512:(ch + 1) * 512], in0=hv[:, ti, ch * 512:(ch + 1) * 512], in1=accs[ti].t[:, 0:512], op=ALU.add)

        def mem_q_proj(l, ntiles, gname):
            ntok = ntiles * 128
            fence([QT, oT, ycT, acc, actT], [qmT, omT, pmT])
            rmsnorm_to_aT(ntiles, GCOL[gname])
            for ch in range(2):
                wbuf, wv = load_w(w_mem_q[l], 0, 8, ch * 512, 512)
                for cc in range(4):
                    ps = next_ps()
                    proj_fm(wbuf, wv, cc, ntok, ps)
                    S.op("scalar", lambda e, ps=ps, cc=cc, ch=ch: e.activation(out=qmTv[:, ch * 4 + cc, 0:ntok], in_=ps.t[:, 0:ntok], func=AF.Copy), [ps], [qmT])

        def mem_attn_core(c0, ntok, KTb, KTv_, Vb, Vv_):
            cs = slice(c0, c0 + ntok)
            for hd in range(4):
                den = PS[3]
                for mh in range(2):
                    ps = next_ps()
                    for dc in range(2):
                        MM(ps.t[:, 0:ntok], KTv_[:, hd * 2 + dc, mh * 128:(mh + 1) * 128], qmTv[:, hd * 2 + dc, cs], dc == 0, dc == 1, [KTb, qmT], [ps])
                    S.op("scalar", lambda e, ps=ps, mh=mh: e.activation(out=pmTv[:, mh, 0:ntok], in_=ps.t[:, 0:ntok], func=AF.Exp, scale=1.0 / 16.0), [ps], [pmT])
                for mh in range(2):
                    MM(den.t[:, 0:ntok], ones_b.t, pmTv[:, mh, 0:ntok], mh == 0, mh == 1, [ones_b, pmT], [den])
                rd = next_stg()
                X("vector", "reciprocal", [den], [rd], out=rd.t[:, 0:ntok], in_=den.t[:, 0:ntok])
                for dc in range(2):
                    po = PS[4 + dc]
                    for mh in range(2):
                        MM(po.t[:, 0:ntok], Vv_[:, mh, hd * 256 + dc * 128: hd * 256 + (dc + 1) * 128], pmTv[:, mh, 0:ntok], mh == 0, mh == 1, [Vb, pmT], [po])
                    X("vector", "tensor_tensor", [po, rd], [omT], out=omTv[:, hd * 2 + dc, cs], in0=po.t[:, 0:ntok], in1=rd.t[:, 0:ntok], op=ALU.mult)

        def mem_attn(l, ntiles, gname):
            mem_q_proj(l, ntiles, gname)
            mem_attn_core(0, ntiles * 128, memKT, memKTv, memV, memVv)
            residual_proj(omT, omTv, w_mem_o[l], 8, ntiles)

        def ffn(l, ntiles, gname):
            ntok = ntiles * 128
            fence([qmT, omT, pmT], [actT])
            rmsnorm_to_aT(ntiles, GCOL[gname])
            for pc in range(6):
                ncols = 512 if pc < 5 else 256
                wg, wgv = load_w(w_ffn_gate[l], 0, 8, pc * 512, ncols)
                wu, wuv = load_w(w_ffn_up[l], 0, 8, pc * 512, ncols)
                for cc in range(ncols // 128):
                    fc = pc * 4 + cc
                    pg = next_ps()
                    proj_fm(wg, wgv, cc, ntok, pg)
                    pu = PS[4 + fc % 2]
                    for c in range(8):
                        MM(pu.t[:, 0:ntok], wuv[:, c, cc * 128:(cc + 1) * 128], aTv[:, c, 0:ntok], c == 0, c == 7, [wu, aT], [pu])
                    sg = next_stg()
                    S.op("scalar", lambda e, pg=pg, sg=sg: e.activation(out=sg.t[:, 0:ntok], in_=pg.t[:, 0:ntok], func=AF.Silu), [pg], [sg])
                    X("vector", "tensor_tensor", [pu, sg], [actT], out=actTv[:, fc, 0:ntok], in0=pu.t[:, 0:ntok], in1=sg.t[:, 0:ntok], op=ALU.mult)
            residual_proj(actT, actTv, w_ffn_down[l], NFC, ntiles)

        def prompt_mem_kv(l):
            DMA("sync", hv[:, 0:2, :], memp.rearrange("(t p) d -> p t d", p=128), [], [hb])
            for mh in range(2):
                at = atok[mh % 2]
                S.op("scalar", lambda e, at=at, mh=mh: e.activation(out=at.t, in_=hv[:, mh, :], func=AF.Copy), [hb], [at])
                if KM < 2:
                    continue
                pst = PS[2]
                pstv = pst.t[:, 0:512].bitcast(BF16).rearrange("p (c t) -> p c t", c=8)
                for c in range(8):
                    TR(pstv[:, c, :], at.t[:, c * 128:(c + 1) * 128], ident_b.t, [at, ident_b], [pst])
                if KM < 3:
                    continue
                X("vector", "tensor_copy", [pst], [aT], out=aTv[:, :, mh * 128:(mh + 1) * 128], in_=pstv)
            if KM < 4:
                DMA("sync", memkp[0, 0:128, 0:512], atok[0].t.bitcast(F32), [atok[0], aT, PS[2]], [], is_out=True)
                return
            for ch in range(2):
                wbuf, wv = load_w(w_mem_k[l], 0, 8, ch * 512, 512)
                if KM < 5:
                    DMA("sync", memkp[0, 0:128, 0:512], atok[0].t.bitcast(F32), [atok[0], aT, PS[2], wbuf], [], is_out=True)
                    continue
                for mh in range(2):
                    ps = next_ps()
                    proj_tm(wbuf, wv, mh, 512, ps)
                    st = next_stg()
                    S.op("scalar", lambda e, ps=ps, st=st: e.activation(out=st.t, in_=ps.t, func=AF.Copy), [ps], [st])
                    DMA("sync", memkp[l, mh * 128:(mh + 1) * 128, ch * 512:(ch + 1) * 512], st.t, [st], [], is_out=True)
                if KM < 6:
                    continue
                for cc in range(4):
                    ps = next_ps()
                    proj_fm(wbuf, wv, cc, 256, ps)
                    X("vector", "tensor_copy", [ps], [memKT], out=memKTv[:, ch * 4 + cc, :], in_=ps.t[:, 0:256])
            if KM < 7:
                return
            for ch in range(2):
                wbuf, wv = load_w(w_mem_v[l], 0, 8, ch * 512, 512)
                for mh in range(2):
                    ps = next_ps()
                    proj_tm(wbuf, wv, mh, 512, ps)
                    st = next_stg()
                    S.op("scalar", lambda e, ps=ps, st=st: e.activation(out=st.t, in_=ps.t, func=AF.Copy), [ps], [st])
                    X("vector", "tensor_copy", [ps], [memV], out=memVv[:, mh, ch * 512:(ch + 1) * 512], in_=ps.t)
                    DMA("sync", memvp[l, mh * 128:(mh + 1) * 128, ch * 512:(ch + 1) * 512], st.t, [st], [], is_out=True)

        def sb_attention(gi):
            nkb = 4 * gi + 4
            for hd in range(8):
                pj, hh = hd // 2, hd % 2
                prt = slice(hh * 64, (hh + 1) * 64)
                po = PS[6 + hd % 2]
                first = True
                for it, kb in enumerate(reversed(range(nkb))):
                    j = kb - 4 * gi
                    diag = j >= 0
                    ps = next_ps()
                    MM(ps.t, KTv[prt, pj, kb * 128:(kb + 1) * 128], QTv[prt, pj, :], True, True, [KT, QT], [ps])
                    E = Eb[it % 2]
                    SP = SPb[it % 2]
                    TM = TMb[it % 2]
                    Wt = Wb[it % 2]
                    S.op("scalar", lambda e, ps=ps, E=E: e.activation(out=E.t, in_=ps.t, func=AF.Exp, scale=0.125), [ps], [E])
                    S.op("scalar", lambda e, E=E, SP=SP: e.activation(out=SP.t, in_=E.t, func=AF.Ln, bias=1.0), [E], [SP])
                    X("vector", "scalar_tensor_tensor", [ps, SP], [TM], out=TM.t, in0=ps.t, scalar=0.125, in1=SP.t, op0=ALU.mult, op1=ALU.subtract)
                    if diag:
                        X("gpsimd", "tensor_tensor", [SP, sbmask], [SPm], out=SPm.t, in0=SP.t, in1=sbm[:, j, :], op=ALU.mult)
                        spm = SPm
                    else:
                        spm = SP
                    pa = PS[4 + it % 2]
                    MM(pa.t, negtri.t, spm.t, True, first, [negtri, spm], [pa])
                    if not first:
                        MM(pa.t, negones.t, SPacc.t, False, True, [negones, SPacc], [pa])
                    X("vector", "tensor_tensor", [TM, pa], [TM], out=TM.t, in0=TM.t, in1=pa.t, op=ALU.add)
                    S.op("scalar", lambda e, TM=TM, Wt=Wt: e.activation(out=Wt.t, in_=TM.t, func=AF.Exp), [TM], [Wt])
                    if diag:
                        X("gpsimd", "tensor_tensor", [Wt, sbmask], [Wt], out=Wt.t, in0=Wt.t, in1=sbm[:, j, :], op=ALU.mult)
                    if first:
                        X("gpsimd", "tensor_copy", [spm], [SPacc], out=SPacc.t, in_=spm.t)
                    elif it < nkb - 1:
                        X("gpsimd", "tensor_tensor", [spm, SPacc], [SPacc], out=SPacc.t, in0=SPacc.t, in1=spm.t, op=ALU.add)
                    MM(po.t[prt, :], Vv[:, kb, hd * 64:(hd + 1) * 64], Wt.t, first, it == nkb - 1, [Vr, Wt], [po])
                    first = False
                S.op("scalar", lambda e, po=po, prt=prt, pj=pj: e.activation(out=oTv[prt, pj, :], in_=po.t[prt, :], func=AF.Copy), [po], [oT])

        def conv_module(gi):
            for cc in range(4):
                eng = "vector"
                X(eng, "tensor_scalar", [hp, cw, vec], [acc], out=accv[:, cc, :], in0=hpv[:, cc, 0:512], scalar1=cwv[:, cc, 0:1], scalar2=vec.t[:, 56 + cc:57 + cc], op0=ALU.mult, op1=ALU.add)
                for jj in range(1, 31):
                    X(eng, "scalar_tensor_tensor", [hp, cw, acc], [acc], out=accv[:, cc, :], in0=hpv[:, cc, jj:jj + 512], scalar=cwv[:, cc, jj:jj + 1], in1=accv[:, cc, :], op0=ALU.mult, op1=ALU.add)
            pm = PS[3]
            pq = PS[4]
            sq = Eb[0]
            for cc in range(4):
                MM(pm.t, div512.t, accv[:, cc, :], cc == 0, cc == 3, [div512, acc], [pm])
            for cc in range(4):
                S.op("scalar", lambda e, cc=cc: e.activation(out=sq.t, in_=accv[:, cc, :], func=AF.Square), [acc], [sq])
                MM(pq.t, div512.t, sq.t, cc == 0, cc == 3, [div512, sq], [pq])
            mean = SPb[0]
            var = SPb[1]
            rstd = TMb[0]
            S.op("scalar", lambda e: e.activation(out=mean.t, in_=pm.t, func=AF.Copy), [pm], [mean])
            X("gpsimd", "tensor_tensor", [mean], [var], out=var.t, in0=mean.t, in1=mean.t, op=ALU.mult)
            X("vector", "tensor_tensor", [pq, var], [var], out=var.t, in0=pq.t, in1=var.t, op=ALU.subtract)
            X("vector", "tensor_scalar", [var], [var], out=var.t, in0=var.t, scalar1=EPS, scalar2=None, op0=ALU.add)
            S.op("scalar", lambda e: e.activation(out=var.t, in_=var.t, func=AF.Sqrt), [var], [var])
            X("vector", "reciprocal", [var], [rstd], out=rstd.t, in_=var.t)
            for cc in range(4):
                tm = TMb[1]
                X("vector", "tensor_tensor", [acc, mean], [tm], out=tm.t, in0=accv[:, cc, :], in1=mean.t, op=ALU.subtract)
                X("vector", "tensor_tensor", [tm, rstd], [tm], out=tm.t, in0=tm.t, in1=rstd.t, op=ALU.mult)
                S.op("scalar", lambda e, cc=cc, tm=tm: e.activation(out=ycTv[:, cc, :], in_=tm.t, func=AF.Silu, scale=vec.t[:, 60 + cc:61 + cc], bias=vec.t[:, 64 + cc:65 + cc]), [tm, vec], [ycT])

        def prompt_l0_group(gi):
            r0 = gi * GT
            fence([actT, qmT, omT, pmT], [QT, oT, ycT, acc])
            DMA("sync", hv, xp[r0:r0 + GT, :].rearrange("(t p) d -> p t d", p=128), [], [hb])
            rmsnorm_to_aT(4, GCOL["mix0"])
            wval, wvalv = load_w(w_in_even, 0, 8, 0, 512)
            for cc in range(4):
                ps = next_ps()
                proj_fm(wval, wvalv, cc, 512, ps)
                S.op("scalar", lambda e, ps=ps, cc=cc: e.activation(out=accv[:, cc, :], in_=ps.t, func=AF.Copy), [ps], [acc])
            wgt, wgtv = load_w(w_in_even, 0, 8, 512, 512)
            for cc in range(4):
                ps = next_ps()
                proj_fm(wgt, wgtv, cc, 512, ps)
                sg = next_stg()
                S.op("scalar", lambda e, ps=ps, sg=sg: e.activation(out=sg.t, in_=ps.t, func=AF.Sigmoid), [ps], [sg])
                X("vector", "tensor_tensor", [acc, sg], [hp], out=hpv[:, cc, 30:542], in0=accv[:, cc, :], in1=sg.t, op=ALU.mult)
            if KSUB < 2:
                DMA("sync", yp[r0:r0 + GT, :].rearrange("(t p) d -> p t d", p=128), hv, [hb], [], is_out=True)
                return
            wq, wqv = load_w(w_in_even, 0, 8, 1024, 512)
            for pj in range(4):
                ps = next_ps()
                proj_fm(wq, wqv, pj, 512, ps)
                S.op("scalar", lambda e, ps=ps, pj=pj: e.activation(out=QTv[:, pj, :], in_=ps.t, func=AF.Copy), [ps], [QT])
            wk, wkv = load_w(w_in_even, 0, 8, 1536, 512)
            for pj in range(4):
                ps = next_ps()
                proj_fm(wk, wkv, pj, 512, ps)
                S.op("scalar", lambda e, ps=ps, pj=pj: e.activation(out=KTv[:, pj, r0:r0 + GT], in_=ps.t, func=AF.Copy), [ps], [KT])
            for ti in range(4):
                ps = next_ps()
                proj_tm(wk, wkv, ti, 512, ps)
                st = next_stg()
                S.op("scalar", lambda e, ps=ps, st=st: e.activation(out=st.t, in_=ps.t, func=AF.Copy), [ps], [st])
                DMA("sync", sbkp[r0 + ti * 128:r0 + (ti + 1) * 128, :], st.t, [st], [], is_out=True)
            wv_, wvv = load_w(w_in_even, 0, 8, 2048, 512)
            for ti in range(4):
                ps = next_ps()
                proj_tm(wv_, wvv, ti, 512, ps)
                st = next_stg()
                S.op("scalar", lambda e, ps=ps, st=st: e.activation(out=st.t, in_=ps.t, func=AF.Copy), [ps], [st])
                X("vector", "tensor_copy", [ps], [Vr], out=Vv[:, gi * 4 + ti, :], in_=ps.t)
                DMA("sync", sbvp[r0 + ti * 128:r0 + (ti + 1) * 128, :], st.t, [st], [], is_out=True)
            if KSUB < 3:
                DMA("sync", yp[r0:r0 + GT, :].rearrange("(t p) d -> p t d", p=128), hv, [hb], [], is_out=True)
                return
            conv_module(gi)
            if gi == NG - 1 and os.environ.get('KNOCP', '0') == '1':
                pass
            elif gi == NG - 1:
                pt_ = PS[3]
                for cc in range(4):
                    MM(pt_.t[0:30, cc * 128:(cc + 1) * 128], hpv[:, cc, 512:542], ident_f.t, cc == 0, cc == 3, [hp, ident_f], [pt_])
                st = next_stg()
                S.op("scalar", lambda e, st=st: e.activation(out=st.t[0:30, :], in_=pt_.t[0:30, :], func=AF.Copy), [pt_], [st])
                DMA("sync", convp, st.t[0:30, :], [st], [], is_out=True)
            else:
                X("gpsimd", "tensor_copy", [hp], [hp], out=hpv[:, :, 0:30], in_=hpv[:, :, 512:542])
            if KSUB < 4:
                DMA("sync", yp[r0:r0 + GT, :].rearrange("(t p) d -> p t d", p=128), hv, [hb], [], is_out=True)
                return
            sb_attention(gi)
            if KSUB < 5:
                DMA("sync", yp[r0:r0 + GT, :].rearrange("(t p) d -> p t d", p=128), hv, [hb], [], is_out=True)
                return
            for ch in range(2):
                wo, wov = load_w(w_mix_out[0], 0, 8, ch * 512, 512)
                for ti in range(4):
                    ps = next_ps()
                    for c in range(8):
                        if c < 4:
                            MM(ps.t, ycTv[:, c, ti * 128:(ti + 1) * 128], wov[:, c, :], c == 0, False, [wo, ycT], [ps])
                        else:
                            MM(ps.t, oTv[:, c - 4, ti * 128:(ti + 1) * 128], wov[:, c, :], False, c == 7, [wo, oT], [ps])
                    X("vector", "tensor_tensor", [hb, ps], [hb], out=hv[:, ti, ch * 512:(ch + 1) * 512], in0=hv[:, ti, ch * 512:(ch + 1) * 512], in1=ps.t, op=ALU.add)
            if KSUB < 6:
                DMA("sync", yp[r0:r0 + GT, :].rearrange("(t p) d -> p t d", p=128), hv, [hb], [], is_out=True)
                return
            mem_attn(0, 4, "mem0")
            if KSUB < 7:
                DMA("sync", yp[r0:r0 + GT, :].rearrange("(t p) d -> p t d", p=128), hv, [hb], [], is_out=True)
                return
            ffn(0, 4, "ffn0")
            if KSUB < 8:
                DMA("sync", yp[r0:r0 + GT, :].rearrange("(t p) d -> p t d", p=128), hv, [hb], [], is_out=True)
                return
            DMA("sync", h1[r0:r0 + GT, :].rearrange("(t p) d -> p t d", p=128), hv, [hb], [])
            if STAGE <= 2:
                DMA("sync", yp[r0:r0 + GT, :].rearrange("(t p) d -> p t d", p=128), hv, [hb], [], is_out=True)


        AX = mybir.AxisListType
        L = {}

        def bc(ap, axis, shape):
            return ap.unsqueeze(axis).to_broadcast(list(shape))

        def l1_alloc():
            cur[0] = l0_base
            L["SKT"] = alloc16("SKT", 4 * T, parts=64)
            L["SV"] = alloc16("SV", 32 * 260)
            L["CKVT"] = alloc16("CKVT", 4 * T)
            ckvt_base = cur[0] - 2 * T
            L["pe"] = alloc32("pe", 64)
            L["ropeC"] = alloc32("ropeC", 128)
            L["ropeS"] = alloc32("ropeS", 128)
            L["ropeb"] = alloc32("ropeb", 64, parts=64)
            L["m3"] = alloc32("m3", 256)
            L["selx"] = alloc16("selx", 128)
            L["KCT"] = alloc16("KCT", 256, parts=64)
            L["VC"] = alloc16("VC", 256, parts=64)
            L["w2"] = alloc16("w2", 2 * 2 * 64)
            L["G"] = alloc32("G", 4 * 48)
            L["tb"] = [alloc32("tb%d" % i, 192) for i in range(2)]
            L["Ec"] = alloc32("Ec", 256)
            L["Pn"] = alloc32("Pn", 256)
            L["Pb"] = alloc16("Pb", 256)
            L["PT"] = alloc16("PT", 512, parts=64)
            L["sc"] = alloc32("sc", 64)
            L["sc2"] = alloc32("sc2", 64)
            L["wk16"] = alloc32("wk16", 64)
            L["m8"] = alloc32("m8", 8)
            L["selb"] = alloc16("selb", 64)
            L["selT"] = alloc16("selT", 128, parts=64)
            L["sm"] = alloc32("sm", 32)
            L["Es"] = [alloc16("Es%d" % i, 512) for i in range(2)]
            L["Pms"] = [alloc16("Pms%d" % i, 512) for i in range(2)]
            L["Of"] = alloc32("Of", 1024)
            L["Ot"] = alloc32("Ot", 256)
            L["Otok"] = alloc16("Otok", 1024)
            L["skb"] = alloc16("skb", 256)
            L["hid"] = alloc16("hid", 512)
            L["kcb"] = alloc16("kcb", 256, parts=64)
            print("L1 arena words used", cur[0])
            end = cur[0]
            cur[0] = ckvt_base
            L["QT"] = [alloc16("QT1_%d" % i, 16 * 128, parts=64) for i in range(2)]
            L["qtok"] = alloc16("qtok", 4 * 1024)
            L["WKT"] = alloc16("WKT", 4 * 1024, parts=64)
            L["WV"] = alloc16("WV", 8 * 260)
            assert cur[0] <= ckvt_base + 2 * T, cur[0] - ckvt_base
            cur[0] = end

        def rope(xin, xout, H, cs, sn, R, W, np_=128):
            cb = bc(cs, 1, [np_, H, 32])
            sb_ = bc(sn, 1, [np_, H, 32])
            tA = L["Of"].t[0:np_, 0:H * 32].rearrange("p (h d) -> p h d", h=H)
            tB = L["Of"].t[0:np_, 512:512 + H * 32].rearrange("p (h d) -> p h d", h=H)
            x1 = xin[:, :, 0:32]
            x2 = xin[:, :, 32:64]
            rA = rB = L["Of"]
            X("vector", "tensor_tensor", R, [rA], out=tA, in0=x1, in1=cb, op=ALU.mult)
            X("vector", "tensor_tensor", R, [rB], out=tB, in0=x2, in1=sb_, op=ALU.mult)
            X("vector", "tensor_tensor", [rA], W, out=xout[:, :, 0:32], in0=tA, in1=tB, op=ALU.subtract)
            X("vector", "tensor_tensor", R, [rA], out=tA, in0=x2, in1=cb, op=ALU.mult)
            X("vector", "tensor_tensor", R, [rB], out=tB, in0=x1, in1=sb_, op=ALU.mult)
            X("vector", "tensor_tensor", [rA], W, out=xout[:, :, 32:64], in0=tA, in1=tB, op=ALU.add)

        def load_rope(gi):
            r0 = gi * GT
            DMA("sync", L["ropeC"].t.rearrange("p (t d) -> p t d", t=4), ropec[r0:r0 + GT, :].rearrange("(t p) d -> p t d", p=128), [], [L["ropeC"]])
            DMA("sync", L["ropeS"].t.rearrange("p (t d) -> p t d", t=4), ropes[r0:r0 + GT, :].rearrange("(t p) d -> p t d", p=128), [], [L["ropeS"]])

        def l1_consts():
            DMA("sync", L["pe"].t, peT, [], [L["pe"]])
            DMA("sync", L["ropeb"].t, ropeb, [], [L["ropeb"]])
            DMA("sync", L["m3"].t.rearrange("p (j q) -> p j q", j=2), c_m3, [], [L["m3"]])
            DMA("gpsimd", L["w2"].t.rearrange("p (k c e) -> p k c e", k=2, c=2), w2kv.rearrange("k (c p) e -> p k c e", p=128), [], [L["w2"]])
            SVv = L["SV"].t.rearrange("p (t g e) -> p t g e", t=32, g=4)
            X("vector", "memset", [], [L["SV"]], ap=SVv[:, :, :, 64:65], constant=1.0)

        def kv_tile_to_residents(ps, tile, ti, kT, kTslot, vR, vslot, outk, outv, orow):
            st = next_stg()
            cs = L["ropeC"].t[:, ti * 32:(ti + 1) * 32]
            sn = L["ropeS"].t[:, ti * 32:(ti + 1) * 32]
            rope(ps.t[:, 0:256].rearrange("p (h d) -> p h d", h=4), st.t[:, 0:256].rearrange("p (h d) -> p h d", h=4), 4, cs, sn,
                 [ps, L["ropeC"], L["ropeS"]], [st])
            X("vector", "tensor_copy", [ps], [st], out=st.t[:, 256:512], in_=ps.t[:, 256:512])
            if orow is not None:
                DMA("sync", outk[orow:orow + 128, :], st.t[:, 0:256], [st], [], is_out=True)
                DMA("sync", outv[orow:orow + 128, :], st.t[:, 256:512], [st], [], is_out=True)
            skb = L["skb"]
            X("gpsimd", "tensor_copy", [st], [skb], out=skb.t, in_=st.t[:, 0:256])
            vv = vR.t.rearrange("p (t g e) -> p t g e", g=4, e=65)
            X("gpsimd", "tensor_copy", [st], [vR], out=vv[:, vslot, :, 0:64], in_=st.t[:, 256:512].rearrange("p (g e) -> p g e", g=4))
            pst = PS[2]
            p64 = pst.t[0:64, 0:256].bitcast(BF16).rearrange("p (g t) -> p g t", g=4)
            for g in range(4):
                TR(p64[:, g, :], skb.t[:, g * 64:(g + 1) * 64], ident_b.t, [skb, ident_b], [pst])
            kv_ = kT.t.rearrange("p (g t) -> p g t", g=4)
            S.op("scalar", lambda e: e.activation(out=kv_[:, :, kTslot * 128:(kTslot + 1) * 128], in_=p64, func=AF.Copy), [pst], [kT])

        def l1_sweepA(gi):
            r0 = gi * GT
            fence([actT, qmT, omT, pmT], [QT, oT, ycT, acc])
            DMA("sync", hv, h1[r0:r0 + GT, :].rearrange("(t p) d -> p t d", p=128), [], [hb])
            rmsnorm_to_aT(4, GCOL["mix1"])
            load_rope(gi)
            CKVT = L["CKVT"]
            CKVTv = CKVT.t.rearrange("p (g t) -> p g t", g=4)
            wc, wcv = load_w(w_in_odd, 0, 8, 1024, 512)
            for ti in range(4):
                ps = next_ps()
                proj_tm(wc, wcv, ti, 512, ps)
                st = next_stg()
                S.op("scalar", lambda e, ps=ps, st=st: e.activation(out=st.t, in_=ps.t, func=AF.Copy), [ps], [st])
                DMA("sync", cmpkp[r0 + ti * 128:r0 + (ti + 1) * 128, :], st.t[:, 0:256], [st], [], is_out=True)
                DMA("sync", cmpvp[r0 + ti * 128:r0 + (ti + 1) * 128, :], st.t[:, 256:512], [st], [], is_out=True)
            for g in range(4):
                ps = next_ps()
                for c in range(8):
                    MM(ps.t[0:64, :], wcv[:, c, g * 64:(g + 1) * 64], aTv[:, c, :], c == 0, c == 7, [wc, aT], [ps])
                for c in range(8):
                    MM(ps.t[64:128, :], wcv[:, c, 256 + g * 64:256 + (g + 1) * 64], aTv[:, c, :], c == 0, c == 7, [wc, aT], [ps])
                X("vector", "tensor_tensor", [ps, L["pe"]], [CKVT], out=CKVTv[:, g, r0:r0 + GT].rearrange("p (n l) -> p n l", l=64),
                  in0=ps.t.rearrange("p (n l) -> p n l", l=64), in1=bc(L["pe"].t, 1, [128, 8, 64]), op=ALU.add)
            ws, wsv = load_w(w_in_odd, 0, 8, 1536, 512)
            for ti in range(4):
                tile = gi * 4 + ti
                ps = next_ps()
                proj_tm(ws, wsv, ti, 512, ps)
                kv_tile_to_residents(ps, tile, ti, L["SKT"], tile, L["SV"], tile, selkp, selvp, r0 + ti * 128)

        def gelu_to(ps, out_bf):
            a = L["Of"]
            at_ = a.t[:, 0:512]
            bt_ = a.t[:, 512:1024]
            S.op("scalar", lambda e: e.activation(out=at_, in_=ps.t, func=AF.Square), [ps], [a])
            X("vector", "tensor_scalar", [a], [a], out=at_, in0=at_, scalar1=0.044715, scalar2=1.0, op0=ALU.mult, op1=ALU.add)
            X("vector", "tensor_tensor", [a, ps], [a], out=at_, in0=at_, in1=ps.t, op=ALU.mult)
            S.op("scalar", lambda e: e.activation(out=bt_, in_=at_, func=AF.Sigmoid, scale=1.5957691216057308), [a], [a])
            X("vector", "tensor_tensor", [a, ps], [out_bf], out=out_bf.t, in0=bt_, in1=ps.t, op=ALU.mult)

        def l1_compress():
            CKVT = L["CKVT"]
            CKVTv = CKVT.t.rearrange("p (g n l) -> p g n l", g=4, l=64)
            hps = [PS[4], PS[5]]
            for pc in range(4):
                i = wbi[0] % NWB
                wbi[0] += 1
                wp = wb[i]
                wpv = wp.t.rearrange("p (l h) -> p l h", l=16)
                DMA("gpsimd", wpv, w1kv[:, pc * 16:(pc + 1) * 16, :], [], [wp])
                for kvi in range(2):
                    prt = slice(kvi * 64, (kvi + 1) * 64)
                    for g in range(4):
                        for hc in range(2):
                            col = (g * 2 + hc) * 64
                            for ll in range(16):
                                l = pc * 16 + ll
                                first = (pc == 0 and g == 0 and hc == 0 and ll == 0)
                                last = (pc == 3 and g == 3 and hc == 1 and ll == 15)
                                MM(hps[kvi].t[:, col:col + 64], wpv[prt, ll, hc * 128:(hc + 1) * 128], CKVTv[prt, g, :, l], first, last, [wp, CKVT], [hps[kvi]])
            w2v = L["w2"].t.rearrange("p (k c e) -> p k c e", k=2, c=2)
            for kvi in range(2):
                hid = L["hid"]
                gelu_to(hps[kvi], hid)
                pk = PS[3]
                n = 0
                for g in range(4):
                    for hc in range(2):
                        col = (g * 2 + hc) * 64
                        MM(pk.t[0:64, g * 64:(g + 1) * 64], hid.t[:, col:col + 64], w2v[:, kvi, hc, :], n == 0, n == 7, [hid, L["w2"]], [pk])
                        n += 1
                if kvi == 0:
                    kcb = L["kcb"]
                    rope(pk.t[0:64, 0:256].rearrange("p (h d) -> p h d", h=4), kcb.t.rearrange("p (h d) -> p h d", h=4), 4,
                         L["ropeb"].t[:, 0:32], L["ropeb"].t[:, 32:64], [pk, L["ropeb"]], [kcb], np_=64)
                    pst = PS[2]
                    p64 = pst.t[0:64, 0:128].bitcast(BF16).rearrange("p (g t) -> p g t", g=4)
                    for g in range(4):
                        TR(p64[:, g, :], kcb.t[:, g * 64:(g + 1) * 64], ident_b.t[0:64, 0:64], [kcb, ident_b], [pst])
                    X("vector", "tensor_copy", [pst], [L["KCT"]], out=L["KCT"].t.rearrange("p (g t) -> p g t", g=4), in_=p64)
                else:
                    X("vector", "tensor_copy", [pk], [L["VC"]], out=L["VC"].t, in_=pk.t[0:64, 0:256])

        def attn_branch(i, g, QTv, KTres, Vres, kblocks, masks, pso, sel_ex=None):
            KTv_ = KTres.t.rearrange("p (g t) -> p g t", g=4)
            Vv_ = Vres.t.rearrange("p (t g e) -> p t g e", g=4, e=65)
            nb = len(kblocks)
            for bi, (slot, mk) in enumerate(kblocks):
                pss = next_ps()
                MM(pss.t, KTv_[:, g, slot * 128:(slot + 1) * 128], QTv[:, 4 * g:4 * g + 4, :], True, True, [KTres, L["QTcur"]], [pss])
                Es = L["Es"][bi % 2]
                S.op("scalar", lambda e, pss=pss, Es=Es: e.activation(out=Es.t, in_=pss.t, func=AF.Exp, scale=0.125), [pss], [Es])
                src = Es
                if mk is not None:
                    Pm = L["Pms"][bi % 2]
                    if mk == "sel":
                        psm = PS[5]
                        kb = sel_ex[bi]
                        selx = L["selx"]
                        X("gpsimd", "tensor_copy", [L["selb"]], [selx], out=selx.t.rearrange("p (n l) -> p n l", n=2),
                          in_=L["selb"].t[:, 2 * kb:2 * kb + 2].unsqueeze(2).to_broadcast([128, 2, 64]))
                        pmb = psm.t[:, 0:64].bitcast(BF16)
                        TR(pmb, selx.t, ident_b.t, [selx, ident_b], [psm])
                        map_, mR = pmb, [psm]
                    else:
                        map_, mR = L["m3"].t[:, mk * 128:(mk + 1) * 128], [L["m3"]]
                    X("vector", "tensor_tensor", [Es] + mR, [Pm], out=Pm.t.rearrange("p (r t) -> p r t", r=4),
                      in0=Es.t.rearrange("p (r t) -> p r t", r=4), in1=bc(map_, 1, [128, 4, 128]), op=ALU.mult)
                    src = Pm
                for r in range(4):
                    MM(pso.t[:, r * 65:(r + 1) * 65], src.t[:, r * 128:(r + 1) * 128], Vv_[:, slot, g, :], (bi == 0 and r == 0), (bi == nb - 1 and r == 3), [src, Vres], [pso])

        def add_branch(pso, g, ti, br):
            Gv = L["G"].t[:, ti * 48 + g * 12: ti * 48 + (g + 1) * 12].rearrange("p (r b) -> p r b", b=3)
            pv = pso.t[:, 0:260].rearrange("p (r e) -> p r e", e=65)
            sm = L["sm"]
            cf = sm.t[:, 8:12]
            X("vector", "reciprocal", [pso], [sm], out=cf.unsqueeze(2), in_=pv[:, :, 64:65])
            X("vector", "tensor_tensor", [sm, L["G"]], [sm], out=cf.unsqueeze(2), in0=cf.unsqueeze(2), in1=Gv[:, :, br:br + 1], op=ALU.mult)
            Ot = L["Ot"]
            Otv = Ot.t.rearrange("p (r d) -> p r d", r=4)
            Ofv = L["Of"].t[:, g * 256:(g + 1) * 256].rearrange("p (r d) -> p r d", r=4)
            X("vector", "tensor_tensor", [pso, sm], [Ot], out=Otv, in0=pv[:, :, 0:64], in1=cf.unsqueeze(2).to_broadcast([128, 4, 64]), op=ALU.mult)
            X("gpsimd", "tensor_tensor", [Ot, L["Of"]], [L["Of"]], out=Ofv, in0=Ofv, in1=Otv, op=ALU.add)

        def l1_attn_tile(gi, ti):
            i = gi * 4 + ti
            QTb = L["QT"][i % 2]
            L["QTcur"] = QTb
            QTv = QTb.t.rearrange("p (h t) -> p h t", h=16)
            qtokv = L["qtok"].t.rearrange("p (t n) -> p t n", t=4)
            pst = PS[2]
            p64 = pst.t[0:64, 0:512].bitcast(BF16).rearrange("p (h t) -> p h t", h=8)
            for b8 in range(2):
                for hh in range(8):
                    hd = b8 * 8 + hh
                    TR(p64[:, hh, :], qtokv[:, ti, hd * 64:(hd + 1) * 64], ident_b.t, [L["qtok"], ident_b], [pst])
                S.op("scalar", lambda e, b8=b8: e.activation(out=QTv[:, b8 * 8:(b8 + 1) * 8, :], in_=p64, func=AF.Copy), [pst], [QTb])
            tb = L["tb"][i % 2]
            tbv = tb.t.rearrange("p (j n) -> p j n", j=3)
            DMA("sync", tbv, tabs[i * 128:(i + 1) * 128, :, :], [], [tb])
            KCTv = L["KCT"].t.rearrange("p (g n) -> p g n", g=4)
            VCv = L["VC"].t.rearrange("p (g e) -> p g e", g=4)
            Of = L["Of"]
            for g in range(4):
                psc = PS[3]
                for r in range(4):
                    MM(psc.t[:, r * 64:(r + 1) * 64], QTv[:, 4 * g + r, :], KCTv[:, g, :], True, True, [QTb, L["KCT"]], [psc])
                sm = L["sm"]
                pscv = psc.t[:, 0:256].rearrange("p (r n) -> p r n", r=4)
                X("vector", "tensor_reduce", [psc], [sm], out=sm.t[:, 0:4], in_=pscv, axis=AX.X, op=ALU.max)
                X("vector", "tensor_scalar", [sm], [sm], out=sm.t[:, 0:4], in0=sm.t[:, 0:4], scalar1=-0.125, scalar2=None, op0=ALU.mult)
                Ec, Pn, Pb = L["Ec"], L["Pn"], L["Pb"]
                for r in range(4):
                    S.op("scalar", lambda e, r=r: e.activation(out=Ec.t[:, r * 64:(r + 1) * 64], in_=psc.t[:, r * 64:(r + 1) * 64], func=AF.Exp, scale=0.125, bias=sm.t[:, r:r + 1]), [psc, sm], [Ec])
                Ecv = Ec.t.rearrange("p (r n) -> p r n", r=4)
                Pnv = Pn.t.rearrange("p (r n) -> p r n", r=4)
                X("vector", "tensor_tensor", [Ec, tb], [Ec], out=Ecv, in0=Ecv, in1=bc(tbv[:, 0, :], 1, [128, 4, 64]), op=ALU.mult)
                X("vector", "tensor_reduce", [Ec], [sm], out=sm.t[:, 4:8], in_=Ecv, axis=AX.X, op=ALU.add)
                X("vector", "tensor_scalar", [sm], [sm], out=sm.t[:, 4:8], in0=sm.t[:, 4:8], scalar1=1e-30, scalar2=None, op0=ALU.max)
                X("vector", "reciprocal", [sm], [sm], out=sm.t[:, 4:8], in_=sm.t[:, 4:8])
                X("vector", "tensor_tensor", [Ec, sm], [Pn], out=Pnv, in0=Ecv, in1=sm.t[:, 4:8].unsqueeze(2).to_broadcast([128, 4, 64]), op=ALU.mult)
                sc, sc2, wk16, m8, selb, selT = L["sc"], L["sc2"], L["wk16"], L["m8"], L["selb"], L["selT"]
                X("vector", "tensor_reduce", [Pn], [sc], out=sc.t, in_=Pn.t.rearrange("p (r n) -> p n r", r=4), axis=AX.X, op=ALU.add)
                X("gpsimd", "tensor_copy", [Pn], [Pb], out=Pb.t, in_=Pn.t)
                pst = PS[2]
                p64c = pst.t[0:64, 0:256].bitcast(BF16).rearrange("p (r t) -> p r t", r=4)
                for r in range(4):
                    TR(p64c[:, r, :], Pb.t[:, r * 64:(r + 1) * 64], ident_b.t, [Pb, ident_b], [pst])
                PT = L["PT"]
                PTv = PT.t.rearrange("p (r t) -> p r t", r=4)
                S.op("scalar", lambda e: e.activation(out=PTv, in_=p64c, func=AF.Copy), [pst], [PT])
                poc = PS[4]
                for r in range(4):
                    MM(poc.t[:, r * 64:(r + 1) * 64], PTv[:, r, :], VCv[:, g, :], True, True, [PT, L["VC"]], [poc])
                Gv = L["G"].t[:, ti * 48 + g * 12: ti * 48 + (g + 1) * 12].rearrange("p (r b) -> p r b", b=3)
                Ofv = Of.t[:, g * 256:(g + 1) * 256].rearrange("p (r d) -> p r d", r=4)
                X("vector", "tensor_tensor", [poc, L["G"]], [Of], out=Ofv, in0=poc.t[:, 0:256].rearrange("p (r d) -> p r d", r=4),
                  in1=Gv[:, :, 0:1].to_broadcast([128, 4, 64]), op=ALU.mult)
                X("vector", "tensor_tensor", [sc, tb], [sc2], out=sc2.t, in0=sc.t, in1=tbv[:, 1, :], op=ALU.mult)
                X("vector", "tensor_tensor", [sc2, tb], [sc2], out=sc2.t, in0=sc2.t, in1=tbv[:, 2, :], op=ALU.add)
                S.op("vector", lambda e: e.max(out=m8.t, in_=sc2.t), [sc2], [m8])
                S.op("vector", lambda e: e.match_replace(out=wk16.t, in_to_replace=m8.t, in_values=sc2.t, imm_value=-2.0), [sc2, m8], [wk16])
                S.op("vector", lambda e: e.max(out=m8.t, in_=wk16.t), [wk16], [m8])
                S.op("vector", lambda e: e.match_replace(out=wk16.t, in_to_replace=m8.t, in_values=wk16.t, imm_value=-2.0), [wk16, m8], [wk16])
                X("vector", "tensor_tensor", [sc2, wk16], [wk16], out=wk16.t, in0=sc2.t, in1=wk16.t, op=ALU.subtract)
                X("vector", "tensor_scalar", [wk16], [selb], out=selb.t, in0=wk16.t, scalar1=1.0, scalar2=None, op0=ALU.min)
                kbl = [(kb, "sel") for kb in range(i)] + [(i, 0)]
                pos = PS[6]
                attn_branch(i, g, QTv, L["SKT"], L["SV"], kbl, None, pos, sel_ex=list(range(i + 1)))
                add_branch(pos, g, ti, 1)
                wbl = []
                for kt in range(max(0, i - 4), i + 1):
                    dlt = i - kt
                    wbl.append((kt % 8, 0 if dlt == 0 else (1 if dlt == 4 else None)))
                pow_ = PS[7]
                attn_branch(i, g, QTv, L["WKT"], L["WV"], wbl, None, pow_)
                add_branch(pow_, g, ti, 2)
            Otok = L["Otok"]
            S.op("scalar", lambda e: e.activation(out=Otok.t, in_=Of.t, func=AF.Copy), [Of], [Otok])
            pstv = pst.t[:, 0:512].bitcast(BF16).rearrange("p (c t) -> p c t", c=8)
            for c in range(8):
                TR(pstv[:, c, :], Otok.t[:, c * 128:(c + 1) * 128], ident_b.t, [Otok, ident_b], [pst])
            X("vector", "tensor_copy", [pst], [qmT], out=qmTv[:, :, ti * 128:(ti + 1) * 128], in_=pstv)

        def l1_sweepB(gi):
            r0 = gi * GT
            fence([actT, omT, pmT, QT, oT, ycT, acc], [qmT])
            DMA("sync", hv, h1[r0:r0 + GT, :].rearrange("(t p) d -> p t d", p=128), [], [hb])
            rmsnorm_to_aT(4, GCOL["mix1"])
            load_rope(gi)
            qtokv = L["qtok"].t.rearrange("p (t n) -> p t n", t=4)
            for half in range(2):
                wq, wqv = load_w(w_in_odd, 0, 8, half * 512, 512)
                for ti in range(4):
                    ps = next_ps()
                    proj_tm(wq, wqv, ti, 512, ps)
                    rope(ps.t.rearrange("p (h d) -> p h d", h=8), qtokv[:, ti, half * 512:(half + 1) * 512].rearrange("p (h d) -> p h d", h=8), 8,
                         L["ropeC"].t[:, ti * 32:(ti + 1) * 32], L["ropeS"].t[:, ti * 32:(ti + 1) * 32], [ps, L["ropeC"], L["ropeS"]], [L["qtok"]])
            ww, wwv = load_w(w_in_odd, 0, 8, 2048, 512)
            for ti in range(4):
                tile = gi * 4 + ti
                ps = next_ps()
                proj_tm(ww, wwv, ti, 512, ps)
                kv_tile_to_residents(ps, tile, ti, L["WKT"], tile % 8, L["WV"], tile % 8, winkp, winvp, (tile - 28) * 128 if tile >= 28 else None)
            wg_, wgv_ = load_w(w_in_odd, 0, 8, 2560, 48)
            for ti in range(4):
                ps = next_ps()
                proj_tm(wg_, wgv_, ti, 48, ps)
                S.op("scalar", lambda e, ps=ps, ti=ti, Gt=L["G"].t: e.activation(out=Gt[:, ti * 48:(ti + 1) * 48], in_=ps.t[:, 0:48], func=AF.Sigmoid), [ps], [L["G"]])
            for ti in range(4):
                l1_attn_tile(gi, ti)
            residual_proj(qmT, qmTv, w_mix_out[1], 8, 4)
            mem_attn(1, 4, "mem1")
            ffn(1, 4, "ffn1")
            DMA("sync", L["Of"].t, gfin.to_broadcast([128, D]), [], [L["Of"]])
            for ti in range(4):
                k = smi[0] % 8
                smi[0] += 1
                ss = small.t[:, k * 4:k * 4 + 1]
                sd = small.t[:, k * 4 + 1:k * 4 + 2]
                rs = small.t[:, k * 4 + 2:k * 4 + 3]
                at = atok[ti % 2]
                S.op("scalar", lambda e, at=at, ti=ti, ss=ss: e.activation(out=at.t, in_=hv[:, ti, :], func=AF.Square, accum_out=ss), [hb], [at, small])
                X("vector", "tensor_scalar", [small], [small], out=sd, in0=ss, scalar1=1.0 / D, scalar2=EPS, op0=ALU.mult, op1=ALU.add)
                S.op("scalar", lambda e, sd=sd: e.activation(out=sd, in_=sd, func=AF.Sqrt), [small], [small])
                X("vector", "reciprocal", [small], [small], out=rs, in_=sd)
                X("vector", "scalar_tensor_tensor", [hb, small, L["Of"]], [hb], out=hv[:, ti, :], in0=hv[:, ti, :], scalar=rs, in1=L["Of"].t, op0=ALU.mult, op1=ALU.mult)
            DMA("sync", yp[r0:r0 + GT, :].rearrange("(t p) d -> p t d", p=128), hv, [hb], [], is_out=True)


        Z = {}

        def sample_alloc():
            cur[0] = l0_base
            Z["ptb"] = Buf(AR[:, cur[0]:cur[0] + 256].bitcast(I32), "ptb"); cur[0] += 256
            Z["idx"] = Buf(AR[:, cur[0]:cur[0] + 256].bitcast(I32), "idx"); cur[0] += 256
            Z["ptf"] = alloc32("ptf", 256)
            Z["iota"] = alloc32("iota", 1)
            Z["Kst"] = [alloc16("Kst%d" % i, 2048) for i in range(1)]
            Z["Vst"] = [alloc16("Vst%d" % i, 2048) for i in range(1)]
            Z["KTm"] = [alloc16("KTm%d" % i, 2048) for i in range(1)]
            Z["base"] = cur[0]
            Z["hps"] = alloc32("hps", 4 * 16 * 38)
            Z["glu"] = alloc32("glu", 512)
            Z["accs"] = alloc32("accs", 512)
            Z["lnA"] = alloc32("lnA", 128)
            Z["lnB"] = alloc32("lnB", 128)
            Z["lnC"] = alloc32("lnC", 128)
            Z["lnD"] = alloc32("lnD", 128)
            Z["mixT"] = alloc16("mixT", 8 * 128)
            Z["QTs"] = alloc16("QTs", 4 * 128)
            Z["KTs"] = alloc16("KTs", 4 * 128)
            Z["Qbd"] = alloc16("Qbd", 64)
            Z["vtok"] = alloc16("vtok", 512)
            Z["Vn"] = alloc16("Vn", 512)
            Z["Kp"] = [alloc16("Kp%d" % i, 512) for i in range(2)]
            Z["Vp"] = [alloc16("Vp%d" % i, 512) for i in range(2)]
            Z["KpT"] = [alloc16("KpT%d" % i, 512) for i in range(2)]
            Z["E"] = [alloc32("sE%d" % i, 64) for i in range(2)]
            Z["SP"] = [alloc32("sSP%d" % i, 64) for i in range(2)]
            Z["TM"] = [alloc32("sTM%d" % i, 64) for i in range(2)]
            Z["W"] = [alloc16("sW%d" % i, 64) for i in range(2)]
            Z["SPacc"] = alloc32("sSPacc", 64)
            Z["mnew"] = alloc32("mnew", 64)
            print("sample arena words used", cur[0])

        def sample_setup():
            DMA("sync", Z["ptb"].t, ptab.to_broadcast([128, 256]), [], [Z["ptb"]])
            DMA("sync", Z["iota"].t, iota, [], [Z["iota"]])
            DMA("sync", Z["mnew"].t[0:8, :], c_masknew, [], [Z["mnew"]])
            X("vector", "tensor_copy", [Z["ptb"]], [Z["ptf"]], out=Z["ptf"].t, in_=Z["ptb"].t)
            X("vector", "tensor_scalar", [Z["ptf"], Z["iota"]], [Z["idx"]], out=Z["idx"].t, in0=Z["ptf"].t, scalar1=128.0, scalar2=Z["iota"].t[:, 0:1], op0=ALU.mult, op1=ALU.add)

        def gather(dst, rows_ap, j):
            idxb = Z["idx"]
            return S.dma("gpsimd", lambda e: e.indirect_dma_start(out=dst.t, out_offset=None, in_=rows_ap,
                                                                   in_offset=bass.IndirectOffsetOnAxis(ap=idxb.t[:, j:j + 1], axis=0)), [idxb], [dst])

        def sample_l0():
            fence([actT, qmT, omT, pmT], [QT, oT, ycT, acc])
            DMA("sync", hv[:, 0, :], xs, [], [hb])
            rmsnorm_to_aT(1, GCOL["mix0"])
            hps, glu, accs = Z["hps"], Z["glu"], Z["accs"]
            hpsv = hps.t.rearrange("p (c s t) -> p c s t", c=4, s=16)
            gluv = glu.t.rearrange("p (c t) -> p c t", c=4)
            accsv = accs.t.rearrange("p (c t) -> p c t", c=4)
            sc2d = sconv.rearrange("s t c -> (s t) c")
            for rt in range(4):
                st = next_stg()
                DMA("sync", st.t[0:120, :], sc2d[rt * 120:(rt + 1) * 120, :], [], [st])
                for cc in range(4):
                    ps = next_ps()
                    MM(ps.t[:, 0:120], st.t[0:120, cc * 128:(cc + 1) * 128], ident_f.t[0:120, 0:120], True, True, [st, ident_f], [ps])
                    S.op("scalar", lambda e, ps=ps, cc=cc, rt=rt: e.activation(out=hpsv[:, cc, rt * 4:(rt + 1) * 4, 0:30], in_=ps.t[:, 0:120].rearrange("p (s t) -> p s t", s=4), func=AF.Copy), [ps], [hps])
            DMA("sync", convs[:, 0:22, :], sconv[:, 8:30, :], [], [], is_out=True)
            wval, wvalv = load_w(w_in_even, 0, 8, 0, 512)
            for cc in range(4):
                ps = next_ps()
                proj_fm(wval, wvalv, cc, 128, ps)
                S.op("scalar", lambda e, ps=ps, cc=cc: e.activation(out=accsv[:, cc, :], in_=ps.t[:, 0:128], func=AF.Copy), [ps], [accs])
            wgt, wgtv = load_w(w_in_even, 0, 8, 512, 512)
            for cc in range(4):
                ps = next_ps()
                proj_fm(wgt, wgtv, cc, 128, ps)
                sg = next_stg()
                S.op("scalar", lambda e, ps=ps, sg=sg: e.activation(out=sg.t[:, 0:128], in_=ps.t[:, 0:128], func=AF.Sigmoid), [ps], [sg])
                X("vector", "tensor_tensor", [accs, sg], [glu], out=gluv[:, cc, :], in0=accsv[:, cc, :], in1=sg.t[:, 0:128], op=ALU.mult)
                X("vector", "tensor_copy", [glu], [hps], out=hpsv[:, cc, :, 30:38], in_=gluv[:, cc, :].rearrange("p (s t) -> p s t", s=16))
            pt_ = PS[3]
            for cc in range(4):
                MM(pt_.t[:, cc * 128:(cc + 1) * 128], gluv[:, cc, :], ident_f.t, True, True, [glu, ident_f], [pt_])
            stc = next_stg()
            S.op("scalar", lambda e: e.activation(out=stc.t, in_=pt_.t, func=AF.Copy), [pt_], [stc])
            for sq in range(16):
                DMA("sync", convs[sq, 22:30, :], stc.t[sq * 8:(sq + 1) * 8, :], [stc], [], is_out=True)
            QTs, KTs = Z["QTs"], Z["KTs"]
            QTsv = QTs.t.rearrange("p (c t) -> p c t", c=4)
            KTsv = KTs.t.rearrange("p (c t) -> p c t", c=4)
            wq, wqv = load_w(w_in_even, 0, 8, 1024, 512)
            for pj in range(4):
                ps = next_ps()
                proj_fm(wq, wqv, pj, 128, ps)
                S.op("scalar", lambda e, ps=ps, pj=pj: e.activation(out=QTsv[:, pj, :], in_=ps.t[:, 0:128], func=AF.Copy), [ps], [QTs])
            wk, wkv = load_w(w_in_even, 0, 8, 1536, 512)
            for pj in range(4):
                ps = next_ps()
                proj_fm(wk, wkv, pj, 128, ps)
                S.op("scalar", lambda e, ps=ps, pj=pj: e.activation(out=KTsv[:, pj, :], in_=ps.t[:, 0:128], func=AF.Copy), [ps], [KTs])
            ps = next_ps()
            proj_tm(wk, wkv, 0, 512, ps)
            st = next_stg()
            S.op("scalar", lambda e, ps=ps, st=st: e.activation(out=st.t, in_=ps.t, func=AF.Copy), [ps], [st])
            DMA("sync", sbks, st.t, [st], [], is_out=True)
            wv_, wvv = load_w(w_in_even, 0, 8, 2048, 512)
            ps = next_ps()
            proj_tm(wv_, wvv, 0, 512, ps)
            st = next_stg()
            S.op("scalar", lambda e, ps=ps, st=st: e.activation(out=st.t, in_=ps.t, func=AF.Copy), [ps], [st])
            X("vector", "tensor_copy", [ps], [Z["vtok"]], out=Z["vtok"].t, in_=ps.t)
            DMA("sync", sbvs, st.t, [st], [], is_out=True)
            for cc in range(4):
                av = accsv[:, cc, :].rearrange("p (s t) -> p s t", s=16)
                X("vector", "tensor_scalar", [hps, cw, vec], [accs], out=av, in0=hpsv[:, cc, :, 0:8], scalar1=cwv[:, cc, 0:1], scalar2=vec.t[:, 56 + cc:57 + cc], op0=ALU.mult, op1=ALU.add)
                for jj in range(1, 31):
                    X("vector", "scalar_tensor_tensor", [hps, cw, accs], [accs], out=av, in0=hpsv[:, cc, :, jj:jj + 8], scalar=cwv[:, cc, jj:jj + 1], in1=av, op0=ALU.mult, op1=ALU.add)
            pm, pq = PS[3], PS[4]
            sq_ = Z["lnA"]
            for cc in range(4):
                MM(pm.t[:, 0:128], div512.t, accsv[:, cc, :], cc == 0, cc == 3, [div512, accs], [pm])
            for cc in range(4):
                S.op("scalar", lambda e, cc=cc: e.activation(out=sq_.t, in_=accsv[:, cc, :], func=AF.Square), [accs], [sq_])
                MM(pq.t[:, 0:128], div512.t, sq_.t, cc == 0, cc == 3, [div512, sq_], [pq])
            mean, var, rstd, tm = Z["lnB"], Z["lnC"], Z["lnD"], Z["lnA"]
            S.op("scalar", lambda e: e.activation(out=mean.t, in_=pm.t[:, 0:128], func=AF.Copy), [pm], [mean])
            X("gpsimd", "tensor_tensor", [mean], [var], out=var.t, in0=mean.t, in1=mean.t, op=ALU.mult)
            X("vector", "tensor_tensor", [pq, var], [var], out=var.t, in0=pq.t[:, 0:128], in1=var.t, op=ALU.subtract)
            X("vector", "tensor_scalar", [var], [var], out=var.t, in0=var.t, scalar1=EPS, scalar2=None, op0=ALU.add)
            S.op("scalar", lambda e: e.activation(out=var.t, in_=var.t, func=AF.Sqrt), [var], [var])
            X("vector", "reciprocal", [var], [rstd], out=rstd.t, in_=var.t)
            mixT = Z["mixT"]
            mixTv = mixT.t.rearrange("p (c t) -> p c t", c=8)
            for cc in range(4):
                X("vector", "tensor_tensor", [accs, mean], [tm], out=tm.t, in0=accsv[:, cc, :], in1=mean.t, op=ALU.subtract)
                X("vector", "tensor_tensor", [tm, rstd], [tm], out=tm.t, in0=tm.t, in1=rstd.t, op=ALU.mult)
                S.op("scalar", lambda e, cc=cc: e.activation(out=mixTv[:, cc, :], in_=tm.t, func=AF.Silu, scale=vec.t[:, 60 + cc:61 + cc], bias=vec.t[:, 64 + cc:65 + cc]), [tm, vec], [mixT])
            Qbd = Z["Qbd"]
            Qbdv = Qbd.t.rearrange("p (c q) -> p c q", c=4)
            X("vector", "memset", [], [Qbd], ap=Qbd.t, constant=0.0)
            Vn = Z["Vn"]
            SPacc = Z["SPacc"]
            mnew = Z["mnew"]
            for b in range(16):
                bs = slice(b * 8, (b + 1) * 8)
                X("vector", "tensor_copy", [QTs], [Qbd], out=Qbdv[0:64, :, 0:8], in_=QTsv[0:64, :, bs])
                X("vector", "tensor_copy", [QTs], [Qbd], out=Qbdv[64:128, :, 8:16], in_=QTsv[64:128, :, bs])
                DMA("sync", Vn.t[0:8, :], Z["vtok"].t[bs, :], [Z["vtok"]], [Vn])
                po = PS[6 + b % 2]
                X("vector", "memset", [], [SPacc], ap=SPacc.t, constant=0.0)
                for it, blk in enumerate([16] + list(reversed(range(16)))):
                    new = blk == 16
                    nk = 8 if new else 128
                    E, SP, TM, Wt = Z["E"][it % 2], Z["SP"][it % 2], Z["TM"][it % 2], Z["W"][it % 2]
                    ps = next_ps()
                    if new:
                        for pj in range(4):
                            MM(ps.t[0:8, pj * 16:(pj + 1) * 16], KTsv[:, pj, bs], Qbdv[:, pj, :], True, True, [KTs, Qbd], [ps])
                        Vsrc, Vb_ = Vn.t, Vn
                    else:
                        Kp, Vp, KpT = Z["Kp"][it % 2], Z["Vp"][it % 2], Z["KpT"][it % 2]
                        gather(Kp, sbk_rows, b * 16 + blk)
                        gather(Vp, sbv_rows, b * 16 + blk)
                        pst = PS[2]
                        pstv = pst.t[:, 0:256].bitcast(BF16).rearrange("p (c t) -> p c t", c=4)
                        for pj in range(4):
                            TR(pstv[:, pj, :], Kp.t[:, pj * 128:(pj + 1) * 128], ident_b.t, [Kp, ident_b], [pst])
                        KpTv = KpT.t.rearrange("p (c t) -> p c t", c=4)
                        S.op("scalar", lambda e, KpTv=KpTv, pstv=pstv: e.activation(out=KpTv, in_=pstv, func=AF.Copy), [pst], [KpT])
                        for pj in range(4):
                            MM(ps.t[:, pj * 16:(pj + 1) * 16], KpTv[:, pj, :], Qbdv[:, pj, :], True, True, [KpT, Qbd], [ps])
                        Vsrc, Vb_ = Vp.t, Vp
                    kp = slice(0, nk)
                    S.op("scalar", lambda e, ps=ps, E=E, kp=kp: e.activation(out=E.t[kp, :], in_=ps.t[kp, 0:64], func=AF.Exp, scale=0.125), [ps], [E])
                    S.op("scalar", lambda e, E=E, SP=SP, kp=kp: e.activation(out=SP.t[kp, :], in_=E.t[kp, :], func=AF.Ln, bias=1.0), [E], [SP])
                    X("vector", "scalar_tensor_tensor", [ps, SP], [TM], out=TM.t[kp, :], in0=ps.t[kp, 0:64], scalar=0.125, in1=SP.t[kp, :], op0=ALU.mult, op1=ALU.subtract)
                    pa = PS[4 + it % 2]
                    if new:
                        X("vector", "tensor_tensor", [SP, mnew], [SP], out=SP.t[kp, :], in0=SP.t[kp, :], in1=mnew.t[kp, :], op=ALU.mult)
                        MM(pa.t[kp, 0:64], negtri.t[0:8, 0:8], SP.t[kp, :], True, True, [negtri, SP], [pa])
                    else:
                        MM(pa.t[:, 0:64], negtri.t, SP.t, True, False, [negtri, SP], [pa])
                        MM(pa.t[:, 0:64], negones.t, SPacc.t, False, True, [negones, SPacc], [pa])
                    X("vector", "tensor_tensor", [TM, pa], [TM], out=TM.t[kp, :], in0=TM.t[kp, :], in1=pa.t[kp, 0:64], op=ALU.add)
                    S.op("scalar", lambda e, TM=TM, Wt=Wt, kp=kp: e.activation(out=Wt.t[kp, :], in_=TM.t[kp, :], func=AF.Exp), [TM], [Wt])
                    if new:
                        X("vector", "tensor_tensor", [Wt, mnew], [Wt], out=Wt.t[kp, :], in0=Wt.t[kp, :], in1=mnew.t[kp, :], op=ALU.mult)
                        X("gpsimd", "tensor_copy", [SP], [SPacc], out=SPacc.t[kp, :], in_=SP.t[kp, :])
                    elif blk > 0:
                        X("gpsimd", "tensor_tensor", [SP, SPacc], [SPacc], out=SPacc.t, in0=SPacc.t, in1=SP.t, op=ALU.add)
                    for hd in range(8):
                        pj, hh = hd // 2, hd % 2
                        MM(po.t[hh * 64:(hh + 1) * 64, pj * 8:(pj + 1) * 8], Vsrc[kp, hd * 64:(hd + 1) * 64], Wt.t[kp, hd * 8:(hd + 1) * 8],
                           (new and pj == 0), (blk == 0 and pj == 3), [Vb_, Wt], [po])
                S.op("scalar", lambda e, po=po, bs=bs: e.activation(out=mixTv[:, 4:8, bs], in_=po.t[:, 0:32].rearrange("p (c q) -> p c q", c=4), func=AF.Copy), [po], [mixT])
            residual_proj(mixT, mixTv, w_mix_out[0], 8, 1)

        def sample_mem_attn(l, gname):
            mem_q_proj(l, 1, gname)
            for b in range(16):
                Kst, Vst, KTm = Z["Kst"][0], Z["Vst"][0], Z["KTm"][0]
                Kstv = Kst.t.rearrange("p (h d) -> p h d", h=2)
                Vstv = Vst.t.rearrange("p (h d) -> p h d", h=2)
                KTmv = KTm.t.rearrange("p (c m) -> p c m", c=8)
                DMA("gpsimd", Kstv, cmk[l, b].rearrange("(h p) d -> p h d", p=128), [], [Kst])
                DMA("gpsimd", Vstv, cmv[l, b].rearrange("(h p) d -> p h d", p=128), [], [Vst])
                for mh in range(2):
                    pst = PS[2]
                    pstv = pst.t[:, 0:512].bitcast(BF16).rearrange("p (c t) -> p c t", c=8)
                    for c in range(8):
                        TR(pstv[:, c, :], Kstv[:, mh, c * 128:(c + 1) * 128], ident_b.t, [Kst, ident_b], [pst])
                    X("vector", "tensor_copy", [pst], [KTm], out=KTmv[:, :, mh * 128:(mh + 1) * 128], in_=pstv)
                mem_attn_core(b * 8, 8, KTm, KTmv, Vst, Vstv)
            residual_proj(omT, omTv, w_mem_o[l], 8, 1)


        def sample_alloc1():
            cur[0] = Z["base"]
            for k_, n_, p_ in [("pe", 64, 128), ("ropeb", 64, 64), ("Of", 1024, 128), ("sm", 32, 128), ("Ec", 256, 128), ("Pn", 256, 128),
                               ("sc", 65, 128), ("sc2", 65, 128), ("wk16", 65, 128), ("m8", 8, 128), ("Ot", 256, 128), ("G", 16 * 48, 128), ("m3", 256, 128)]:
                L[k_] = alloc32(k_ + "_s", n_, parts=p_)
            for k_, n_, p_ in [("CKVT", 4 * T, 128), ("KCT", 256, 64), ("VC", 256, 64), ("w2", 256, 128), ("hid", 512, 128), ("kcb", 256, 64),
                               ("Pb", 256, 128), ("PT", 64, 64), ("selb", 66, 128), ("selx", 128, 128), ("Otok", 1024, 128)]:
                L[k_] = alloc16(k_ + "_s", n_, parts=p_)
            Z["rope"] = alloc32("rope_s", 64)
            Z["tbs"] = alloc32("tbs", 2 * 3 * 65)
            Z["qtok"] = alloc16("qtok_s", 1024)
            Z["QT1"] = alloc16("QT1s", 16 * 128, parts=64)
            Z["SKTn"] = alloc16("SKTn", 4 * 128, parts=64)
            Z["WKTn"] = alloc16("WKTn", 4 * 128, parts=64)
            Z["svtok"] = alloc16("svtok", 260)
            Z["wvtok"] = alloc16("wvtok", 260)
            Z["SVn"] = alloc16("SVn", 260)
            Z["WVn"] = alloc16("WVn", 260)
            Z["CKp"] = [alloc16("CKp%d" % i, 512) for i in range(2)]
            Z["SKp"] = [alloc16("SKp%d" % i, 256) for i in range(2)]
            Z["SKTb"] = alloc16("SKTb", 4 * 2048, parts=64)
            Z["SVp"] = alloc16("SVp", 16 * 260)
            Z["WKst"] = alloc16("WKst", 4 * 256)
            Z["WKTb"] = alloc16("WKTb", 4 * 512, parts=64)
            Z["WVp"] = alloc16("WVp", 4 * 260)
            Z["graw"] = [alloc16("graw%d" % i, 256) for i in range(3)]
            Z["Es"] = [alloc16("sEs%d" % i, 32) for i in range(2)]
            Z["Pms"] = [alloc16("sPms%d" % i, 32) for i in range(2)]
            print("sample L1 arena words used", cur[0])

        def s_branch(g, bs, blocks, pso):
            QT1v = Z["QT1"].t.rearrange("p (h t) -> p h t", h=16)
            nb = len(blocks)
            for bi, (kt_ap, ktb, v_ap, vb, nk, mk, mkb, pre) in enumerate(blocks):
                kp = slice(0, nk)
                pss = next_ps()
                MM(pss.t[kp, 0:32], kt_ap, QT1v[:, 4 * g:4 * g + 4, bs], True, True, [ktb, Z["QT1"]], [pss])
                Es = Z["Es"][bi % 2]
                S.op("scalar", lambda e, pss=pss, Es=Es, kp=kp: e.activation(out=Es.t[kp, :], in_=pss.t[kp, 0:32], func=AF.Exp, scale=0.125), [pss], [Es])
                src = Es
                if mk is not None or pre is not None:
                    if pre is not None:
                        mk, mkb = pre()
                    Pm = Z["Pms"][bi % 2]
                    X("vector", "tensor_tensor", [Es] + mkb, [Pm], out=Pm.t[kp, :].rearrange("p (r t) -> p r t", r=4),
                      in0=Es.t[kp, :].rearrange("p (r t) -> p r t", r=4), in1=bc(mk, 1, [nk, 4, 8]), op=ALU.mult)
                    src = Pm
                for r in range(4):
                    MM(pso.t[0:8, r * 65:(r + 1) * 65], src.t[kp, r * 8:(r + 1) * 8], v_ap, (bi == 0 and r == 0), (bi == nb - 1 and r == 3), [src, vb], [pso])

        def s_add(pso, g, b, br):
            Gv = L["G"].t[0:8, b * 48 + g * 12: b * 48 + (g + 1) * 12].rearrange("p (r b) -> p r b", b=3)
            pv = pso.t[0:8, 0:260].rearrange("p (r e) -> p r e", e=65)
            sm = L["sm"]
            cf = sm.t[0:8, 8:12]
            X("vector", "reciprocal", [pso], [sm], out=cf.unsqueeze(2), in_=pv[:, :, 64:65])
            X("vector", "tensor_tensor", [sm, L["G"]], [sm], out=cf.unsqueeze(2), in0=cf.unsqueeze(2), in1=Gv[:, :, br:br + 1], op=ALU.mult)
            Ot = L["Ot"]
            Otv = Ot.t[0:8, :].rearrange("p (r d) -> p r d", r=4)
            Ofv = L["Of"].t[0:8, g * 256:(g + 1) * 256].rearrange("p (r d) -> p r d", r=4)
            X("vector", "tensor_tensor", [pso, sm], [Ot], out=Otv, in0=pv[:, :, 0:64], in1=cf.unsqueeze(2).to_broadcast([8, 4, 64]), op=ALU.mult)
            X("vector", "tensor_tensor", [Ot, L["Of"]], [L["Of"]], out=Ofv, in0=Ofv, in1=Otv, op=ALU.add)

        def sample_l1():
            fence([actT, omT, pmT, QT, oT, ycT, acc], [qmT])
            rmsnorm_to_aT(1, GCOL["mix1"])
            DMA("sync", L["pe"].t, peT, [], [L["pe"]])
            DMA("sync", L["ropeb"].t, ropeb_s, [], [L["ropeb"]])
            DMA("sync", L["m3"].t.rearrange("p (j q) -> p j q", j=2), c_m3, [], [L["m3"]])
            DMA("gpsimd", L["w2"].t.rearrange("p (k c e) -> p k c e", k=2, c=2), w2kv.rearrange("k (c p) e -> p k c e", p=128), [], [L["w2"]])
            DMA("sync", Z["rope"].t, ropecs, [], [Z["rope"]])
            DMA("sync", Z["tbs"].t[0:8, :], tabs_s, [], [Z["tbs"]])
            cs, sn = Z["rope"].t[:, 0:32], Z["rope"].t[:, 32:64]
            qtok = Z["qtok"]
            for half in range(2):
                wq, wqv = load_w(w_in_odd, 0, 8, half * 512, 512)
                ps = next_ps()
                proj_tm(wq, wqv, 0, 512, ps)
                rope(ps.t.rearrange("p (h d) -> p h d", h=8), qtok.t[:, half * 512:(half + 1) * 512].rearrange("p (h d) -> p h d", h=8), 8, cs, sn, [ps, Z["rope"]], [qtok])
            QT1 = Z["QT1"]
            QT1v = QT1.t.rearrange("p (h t) -> p h t", h=16)
            pst = PS[2]
            p64 = pst.t[0:64, 0:512].bitcast(BF16).rearrange("p (h t) -> p h t", h=8)
            for b8 in range(2):
                for hh in range(8):
                    hd = b8 * 8 + hh
                    TR(p64[:, hh, :], qtok.t[:, hd * 64:(hd + 1) * 64], ident_b.t, [qtok, ident_b], [pst])
                S.op("scalar", lambda e, b8=b8: e.activation(out=QT1v[:, b8 * 8:(b8 + 1) * 8, :], in_=p64, func=AF.Copy), [pst], [QT1])
            wc, wcv = load_w(w_in_odd, 0, 8, 1024, 512)
            ps = next_ps()
            proj_tm(wc, wcv, 0, 512, ps)
            st = next_stg()
            S.op("scalar", lambda e, ps=ps, st=st: e.activation(out=st.t, in_=ps.t, func=AF.Copy), [ps], [st])
            DMA("sync", cmpks, st.t[:, 0:256], [st], [], is_out=True)
            DMA("sync", cmpvs, st.t[:, 256:512], [st], [], is_out=True)

            def new_kv(c0, KTn, vtok, outk, outv, winout):
                wbuf, wv = load_w(w_in_odd, 0, 8, c0, 512)
                ps = next_ps()
                proj_tm(wbuf, wv, 0, 512, ps)
                st = next_stg()
                rope(ps.t[:, 0:256].rearrange("p (h d) -> p h d", h=4), st.t[:, 0:256].rearrange("p (h d) -> p h d", h=4), 4, cs, sn, [ps, Z["rope"]], [st])
                X("vector", "tensor_copy", [ps], [st], out=st.t[:, 256:512], in_=ps.t[:, 256:512])
                if not winout:
                    DMA("sync", outk, st.t[:, 0:256], [st], [], is_out=True)
                    DMA("sync", outv, st.t[:, 256:512], [st], [], is_out=True)
                else:
                    for sq in range(16):
                        DMA("sync", outk[sq, 504:512, :], st.t[sq * 8:(sq + 1) * 8, 0:256], [st], [], is_out=True)
                        DMA("sync", outv[sq, 504:512, :], st.t[sq * 8:(sq + 1) * 8, 256:512], [st], [], is_out=True)
                skb = L["Pb"]
                X("gpsimd", "tensor_copy", [st], [skb], out=skb.t, in_=st.t[:, 0:256])
                vv = vtok.t.rearrange("p (g e) -> p g e", g=4)
                X("vector", "memset", [], [vtok], ap=vv[:, :, 64:65], constant=1.0)
                X("gpsimd", "tensor_copy", [st], [vtok], out=vv[:, :, 0:64], in_=st.t[:, 256:512].rearrange("p (g e) -> p g e", g=4))
                pq = pst.t[0:64, 0:256].bitcast(BF16).rearrange("p (g t) -> p g t", g=4)
                for g in range(4):
                    TR(pq[:, g, :], skb.t[:, g * 64:(g + 1) * 64], ident_b.t, [skb, ident_b], [pst])
                S.op("scalar", lambda e: e.activation(out=KTn.t.rearrange("p (g t) -> p g t", g=4), in_=pq, func=AF.Copy), [pst], [KTn])

            new_kv(1536, Z["SKTn"], Z["svtok"], selks, selvs, False)
            new_kv(2048, Z["WKTn"], Z["wvtok"], winks, winvs, True)
            DMA("sync", winks[:, 0:504, :], cwk_d[:, 8:512, :], [], [], is_out=True)
            DMA("sync", winvs[:, 0:504, :], cwv_d[:, 8:512, :], [], [], is_out=True)
            wg_, wgv_ = load_w(w_in_odd, 0, 8, 2560, 48)
            for b in range(16):
                ps = next_ps()
                for c in range(8):
                    MM(ps.t[0:8, 0:48], aTv[:, c, b * 8:(b + 1) * 8], wgv_[:, c, 0:48], c == 0, c == 7, [wg_, aT], [ps])
                S.op("scalar", lambda e, ps=ps, b=b, Gt=L["G"].t: e.activation(out=Gt[0:8, b * 48:(b + 1) * 48], in_=ps.t[0:8, 0:48], func=AF.Sigmoid), [ps], [L["G"]])
            SVpv = Z["SVp"].t.rearrange("p (t g e) -> p t g e", t=16, g=4)
            WVpv = Z["WVp"].t.rearrange("p (t g e) -> p t g e", t=4, g=4)
            X("vector", "memset", [], [Z["SVp"]], ap=SVpv[:, :, :, 64:65], constant=1.0)
            X("vector", "memset", [], [Z["WVp"]], ap=WVpv[:, :, :, 64:65], constant=1.0)
            CKVT = L["CKVT"]
            CKVTv = CKVT.t.rearrange("p (g t) -> p g t", g=4)
            SKTb, WKTb = Z["SKTb"], Z["WKTb"]
            SKTbv = SKTb.t.rearrange("p (g t) -> p g t", g=4)
            WKTbv = WKTb.t.rearrange("p (g t) -> p g t", g=4)
            SKTnv = Z["SKTn"].t.rearrange("p (g t) -> p g t", g=4)
            WKTnv = Z["WKTn"].t.rearrange("p (g t) -> p g t", g=4)
            tbs = Z["tbs"]
            for pair in range(8):
                for s2 in range(2):
                    b = pair * 2 + s2
                    for pg in range(16):
                        CKp = Z["CKp"][pg % 2]
                        CKpv = CKp.t.rearrange("p (g e) -> p g e", g=4)
                        j = b * 16 + pg
                        idxb = Z["idx"]
                        rk, rv = Z["graw"][0], Z["graw"][1]
                        gather(rk, cck_rows, j)
                        gather(rv, ccv_rows, j)
                        X("gpsimd", "tensor_copy", [rk], [CKp], out=CKpv[:, :, 0:64], in_=rk.t.rearrange("p (g e) -> p g e", g=4))
                        X("vector", "tensor_copy", [rv], [CKp], out=CKpv[:, :, 64:128], in_=rv.t.rearrange("p (g e) -> p g e", g=4))
                        pstv = pst.t[:, 0:256].bitcast(BF16).rearrange("p (g t) -> p g t", g=4)
                        for g in range(4):
                            TR(pstv[:, g, :], CKpv[:, g, :], ident_b.t, [CKp, ident_b], [pst])
                        t0 = s2 * 2048 + pg * 128
                        for g in range(4):
                            X("vector", "tensor_tensor", [pst, L["pe"]], [CKVT], out=CKVTv[:, g, t0:t0 + 128].rearrange("p (n l) -> p n l", l=64),
                              in0=pstv[:, g, :].rearrange("p (n l) -> p n l", l=64), in1=bc(L["pe"].t, 1, [128, 2, 64]), op=ALU.add)
                l1_compress()
                KCTv = L["KCT"].t.rearrange("p (g n) -> p g n", g=4)
                VCv = L["VC"].t.rearrange("p (g e) -> p g e", g=4)
                for s2 in range(2):
                    b = pair * 2 + s2
                    bs = slice(b * 8, (b + 1) * 8)
                    tbv = tbs.t[0:8, s2 * 195:(s2 + 1) * 195].rearrange("p (j n) -> p j n", j=3)
                    for pg in range(16):
                        SKp = Z["SKp"][pg % 2]
                        gather(SKp, csk_rows, b * 16 + pg)
                        idxb = Z["idx"]
                        j = b * 16 + pg
                        rs_ = Z["graw"][2]
                        gather(rs_, csv_rows, j)
                        X("gpsimd", "tensor_copy", [rs_], [Z["SVp"]], out=SVpv[:, pg, :, 0:64], in_=rs_.t.rearrange("p (g e) -> p g e", g=4))
                        pq = pst.t[0:64, 0:256].bitcast(BF16).rearrange("p (g t) -> p g t", g=4)
                        for g in range(4):
                            TR(pq[:, g, :], SKp.t[:, g * 64:(g + 1) * 64], ident_b.t, [SKp, ident_b], [pst])
                        S.op("scalar", lambda e, pg=pg, pq=pq: e.activation(out=SKTbv[:, :, pg * 128:(pg + 1) * 128], in_=pq, func=AF.Copy), [pst], [SKTb])
                    WKst = Z["WKst"]
                    WKstv = WKst.t.rearrange("p (t n) -> p t n", t=4)
                    DMA("gpsimd", WKstv, cwk_d[b].rearrange("(t p) n -> p t n", p=128), [], [WKst])
                    for wt in range(4):
                        DMA("gpsimd", WVpv[:, wt, :, 0:64], cwv_d[b, wt * 128:(wt + 1) * 128, :].rearrange("p (g e) -> p g e", g=4), [], [Z["WVp"]])
                    for wt in range(4):
                        pq = pst.t[0:64, 0:256].bitcast(BF16).rearrange("p (g t) -> p g t", g=4)
                        for g in range(4):
                            TR(pq[:, g, :], WKstv[:, wt, g * 64:(g + 1) * 64], ident_b.t, [WKst, ident_b], [pst])
                        S.op("scalar", lambda e, wt=wt, pq=pq: e.activation(out=WKTbv[:, :, wt * 128:(wt + 1) * 128], in_=pq, func=AF.Copy), [pst], [WKTb])
                    DMA("sync", Z["SVn"].t[0:8, :], Z["svtok"].t[bs, :], [Z["svtok"]], [Z["SVn"]])
                    DMA("sync", Z["WVn"].t[0:8, :], Z["wvtok"].t[bs, :], [Z["wvtok"]], [Z["WVn"]])
                    SVnv = Z["SVn"].t.rearrange("p (g e) -> p g e", g=4)
                    WVnv = Z["WVn"].t.rearrange("p (g e) -> p g e", g=4)
                    Of = L["Of"]
                    for g in range(4):
                        psc = PS[3]
                        for r in range(4):
                            MM(psc.t[0:8, r * 64:(r + 1) * 64], QT1v[:, 4 * g + r, bs], KCTv[:, g, :], True, True, [QT1, L["KCT"]], [psc])
                        sm = L["sm"]
                        pscv = psc.t[0:8, 0:256].rearrange("p (r n) -> p r n", r=4)
                        X("vector", "tensor_reduce", [psc], [sm], out=sm.t[0:8, 0:4], in_=pscv, axis=AX.X, op=ALU.max)
                        X("vector", "tensor_scalar", [sm], [sm], out=sm.t[0:8, 0:4], in0=sm.t[0:8, 0:4], scalar1=-0.125, scalar2=None, op0=ALU.mult)
                        Ec, Pn, Pb = L["Ec"], L["Pn"], L["Pb"]
                        for r in range(4):
                            S.op("scalar", lambda e, r=r, psc=psc, Ec=Ec, sm=sm: e.activation(out=Ec.t[0:8, r * 64:(r + 1) * 64], in_=psc.t[0:8, r * 64:(r + 1) * 64], func=AF.Exp, scale=0.125, bias=sm.t[0:8, r:r + 1]), [psc, sm], [Ec])
                        Ecv = Ec.t[0:8, :].rearrange("p (r n) -> p r n", r=4)
                        Pnv = Pn.t[0:8, :].rearrange("p (r n) -> p r n", r=4)
                        X("vector", "tensor_tensor", [Ec, tbs], [Ec], out=Ecv, in0=Ecv, in1=bc(tbv[:, 0, 0:64], 1, [8, 4, 64]), op=ALU.mult)
                        X("vector", "tensor_reduce", [Ec], [sm], out=sm.t[0:8, 4:8], in_=Ecv, axis=AX.X, op=ALU.add)
                        X("vector", "tensor_scalar", [sm], [sm], out=sm.t[0:8, 4:8], in0=sm.t[0:8, 4:8], scalar1=1e-30, scalar2=None, op0=ALU.max)
                        X("vector", "reciprocal", [sm], [sm], out=sm.t[0:8, 4:8], in_=sm.t[0:8, 4:8])
                        X("vector", "tensor_tensor", [Ec, sm], [Pn], out=Pnv, in0=Ecv, in1=sm.t[0:8, 4:8].unsqueeze(2).to_broadcast([8, 4, 64]), op=ALU.mult)
                        sc, sc2, wk16, m8, selb = L["sc"], L["sc2"], L["wk16"], L["m8"], L["selb"]
                        X("vector", "memset", [], [sc], ap=sc.t[0:8, 64:65], constant=0.0)
                        X("vector", "tensor_reduce", [Pn], [sc], out=sc.t[0:8, 0:64], in_=Pn.t[0:8, :].rearrange("p (r n) -> p n r", r=4), axis=AX.X, op=ALU.add)
                        X("gpsimd", "tensor_copy", [Pn], [Pb], out=Pb.t[0:8, :], in_=Pn.t[0:8, :])
                        p64c = pst.t[0:64, 0:16].bitcast(BF16).rearrange("p (r t) -> p r t", r=4)
                        for r in range(4):
                            TR(p64c[:, r, :], Pb.t[0:8, r * 64:(r + 1) * 64], ident_b.t[0:8, 0:8], [Pb, ident_b], [pst])
                        PT = L["PT"]
                        PTv = PT.t[:, 0:32].rearrange("p (r t) -> p r t", r=4)
                        S.op("scalar", lambda e, PTv=PTv, p64c=p64c: e.activation(out=PTv, in_=p64c, func=AF.Copy), [pst], [PT])
                        poc = PS[4]
                        for r in range(4):
                            MM(poc.t[0:8, r * 64:(r + 1) * 64], PTv[:, r, :], VCv[:, g, :], True, True, [PT, L["VC"]], [poc])
                        Gv = L["G"].t[0:8, b * 48 + g * 12: b * 48 + (g + 1) * 12].rearrange("p (r b) -> p r b", b=3)
                        Ofv = Of.t[0:8, g * 256:(g + 1) * 256].rearrange("p (r d) -> p r d", r=4)
                        X("vector", "tensor_tensor", [poc, L["G"]], [Of], out=Ofv, in0=poc.t[0:8, 0:256].rearrange("p (r d) -> p r d", r=4),
                          in1=Gv[:, :, 0:1].to_broadcast([8, 4, 64]), op=ALU.mult)
                        X("vector", "tensor_tensor", [sc, tbs], [sc2], out=sc2.t[0:8, :], in0=sc.t[0:8, :], in1=tbv[:, 1, :], op=ALU.mult)
                        X("vector", "tensor_tensor", [sc2, tbs], [sc2], out=sc2.t[0:8, :], in0=sc2.t[0:8, :], in1=tbv[:, 2, :], op=ALU.add)
                        S.op("vector", lambda e, m8=m8, sc2=sc2: e.max(out=m8.t[0:8, :], in_=sc2.t[0:8, :]), [sc2], [m8])
                        S.op("vector", lambda e, m8=m8, sc2=sc2, wk16=wk16: e.match_replace(out=wk16.t[0:8, :], in_to_replace=m8.t[0:8, :], in_values=sc2.t[0:8, :], imm_value=-2.0), [sc2, m8], [wk16])
                        S.op("vector", lambda e, m8=m8, wk16=wk16: e.max(out=m8.t[0:8, :], in_=wk16.t[0:8, :]), [wk16], [m8])
                        S.op("vector", lambda e, m8=m8, wk16=wk16: e.match_replace(out=wk16.t[0:8, :], in_to_replace=m8.t[0:8, :], in_values=wk16.t[0:8, :], imm_value=-2.0), [wk16, m8], [wk16])
                        X("vector", "tensor_tensor", [sc2, wk16], [wk16], out=wk16.t[0:8, :], in0=sc2.t[0:8, :], in1=wk16.t[0:8, :], op=ALU.subtract)
                        X("vector", "tensor_scalar", [wk16], [selb], out=selb.t[0:8, 0:65], in0=wk16.t[0:8, :], scalar1=1.0, scalar2=None, op0=ALU.min)
                        blocks = [(SKTnv[:, g, bs], Z["SKTn"], SVnv[0:8, g, :], Z["SVn"], 8, L["m3"].t[0:8, 0:8], [L["m3"]], None)]
                        for pg in range(16):
                            def pre(pg=pg, s2=s2):
                                selx = L["selx"]
                                c0 = s2 * 32 + 2 * pg
                                X("gpsimd", "tensor_copy", [L["selb"]], [selx], out=selx.t[0:8, :].rearrange("p (n l) -> p n l", n=2),
                                  in_=L["selb"].t[0:8, c0:c0 + 2].unsqueeze(2).to_broadcast([8, 2, 64]))
                                psm = PS[5]
                                pmb = psm.t[:, 0:4].bitcast(BF16)
                                TR(pmb, selx.t[0:8, :], ident_b.t[0:8, 0:8], [selx, ident_b], [psm])
                                return pmb, [psm]
                            blocks.append((SKTbv[:, g, pg * 128:(pg + 1) * 128], SKTb, SVpv[:, pg, g, :], Z["SVp"], 128, None, None, pre))
                        pos = PS[6]
                        s_branch(g, bs, blocks, pos)
                        s_add(pos, g, b, 1)
                        blocks = [(WKTnv[:, g, bs], Z["WKTn"], WVnv[0:8, g, :], Z["WVn"], 8, L["m3"].t[0:8, 0:8], [L["m3"]], None)]
                        for wt in range(4):
                            mk = L["m3"].t[:, 128:136] if wt == 0 else None
                            blocks.append((WKTbv[:, g, wt * 128:(wt + 1) * 128], WKTb, WVpv[:, wt, g, :], Z["WVp"], 128, mk, [L["m3"]] if wt == 0 else None, None))
                        pow_ = PS[7]
                        s_branch(g, bs, blocks, pow_)
                        s_add(pow_, g, b, 2)
                    Otok = L["Otok"]
                    S.op("scalar", lambda e, Otok=Otok, Of=Of: e.activation(out=Otok.t[0:8, :], in_=Of.t[0:8, :], func=AF.Copy), [Of], [Otok])
                    pstv8 = pst.t[:, 0:32].bitcast(BF16).rearrange("p (c t) -> p c t", c=8)
                    for c in range(8):
                        TR(pstv8[:, c, :], Otok.t[0:8, c * 128:(c + 1) * 128], ident_b.t[0:8, 0:8], [Otok, ident_b], [pst])
                    X("vector", "tensor_copy", [pst], [qmT], out=qmTv[:, :, bs], in_=pstv8)
            residual_proj(qmT, qmTv, w_mix_out[1], 8, 1)

        def sample_final():
            DMA("sync", L["Of"].t, gfin.to_broadcast([128, D]), [], [L["Of"]])
            k = smi[0] % 8
            smi[0] += 1
            ss = small.t[:, k * 4:k * 4 + 1]
            sd = small.t[:, k * 4 + 1:k * 4 + 2]
            rs = small.t[:, k * 4 + 2:k * 4 + 3]
            at = atok[0]
            S.op("scalar", lambda e: e.activation(out=at.t, in_=hv[:, 0, :], func=AF.Square, accum_out=ss), [hb], [at, small])
            X("vector", "tensor_scalar", [small], [small], out=sd, in0=ss, scalar1=1.0 / D, scalar2=EPS, op0=ALU.mult, op1=ALU.add)
            S.op("scalar", lambda e: e.activation(out=sd, in_=sd, func=AF.Sqrt), [small], [small])
            X("vector", "reciprocal", [small], [small], out=rs, in_=sd)
            X("vector", "scalar_tensor_tensor", [hb, small, L["Of"]], [hb], out=hv[:, 0, :], in0=hv[:, 0, :], scalar=rs, in1=L["Of"].t, op0=ALU.mult, op1=ALU.mult)
            DMA("sync", ys, hv[:, 0, :], [hb], [], is_out=True)

        if KSUB >= -1:
            X("vector", "memset", [], [hp], ap=hp.t, constant=0.0)
        if KSUB >= 0:
            prompt_mem_kv(0)
        else:
            DMA("sync", memkp[0, 0:128, 0:128], ident_f.t, [ident_f], [], is_out=True)
        ngroups = NG if STAGE >= 2 else 1
        if os.environ.get("KONLY", "") == "sample":
            ngroups = 0
        ngroups = int(os.environ.get('KNG', ngroups))
        if KSUB < 1:
            ngroups = 0
        for gi in range(ngroups):
            prompt_l0_group(gi)
        if STAGE >= 3 and os.environ.get("KONLY", "") != "sample":
            S.barrier()
            l1_alloc()
            l1_consts()
            prompt_mem_kv(1)
            for gi in range(ngroups):
                l1_sweepA(gi)
            l1_compress()
            S.barrier()
            WVv = L["WV"].t.rearrange("p (t g e) -> p t g e", t=8, g=4)
            X("vector", "memset", [], [L["WV"]], ap=WVv[:, :, :, 64:65], constant=1.0)
            nb_ = int(os.environ.get("KNB", ngroups))
            for gi in range(nb_):
                l1_sweepB(gi)
        if STAGE >= 4:
            S.barrier()
            sample_alloc()
            sample_setup()
            sample_l0()
            sample_mem_attn(0, "mem0")
            ffn(0, 1, "ffn0")
            if STAGE == 4:
                DMA("sync", ys, hv[:, 0, :], [hb], [], is_out=True)
            if STAGE >= 5:
                S.barrier()
                sample_alloc1()
                sample_l1()
                sample_mem_attn(1, "mem1")
                ffn(1, 1, "ffn1")
                sample_final()
        S.finish()
        print("op counts", S.counts)
    return P


_PROG = None
OUT_IDX = [0, 2, 3, 4, 5, 6, 7, 8, 9, 10, 11, 12, 1, 13, 14, 15, 16, 17, 18, 19, 20, 21]


def _tabs_s():
    out = np.zeros((8, 2, 3, 65), np.float32)
    for s2 in range(2):
        own = np.zeros(65, bool)
        own[s2 * 32:(s2 + 1) * 32] = True
        forced = np.zeros(65, bool)
        forced[[s2 * 32, s2 * 32 + 31, 64]] = True
        out[:, s2, 0, :] = own
        out[:, s2, 1, :] = own & ~forced
        out[:, s2, 2, :] = np.where(forced, 1.0e4, np.where(own, 0.0, -1.0))
    return np.ascontiguousarray(out.reshape(8, 390))


def _consts():
    k = np.arange(128)[:, None]
    q = np.arange(512)[None, :]
    sbmask = np.stack([(j * 128 + k < q).astype(np.float32) for j in range(4)])
    kk = np.arange(128)
    negtri = -(kk[:, None] > kk[None, :]).astype(np.float32)
    t = np.arange(T)
    inv = (10000.0 ** (-np.arange(32, dtype=np.float32) / 32)).astype(np.float32)
    ang = t[:, None].astype(np.float32) * inv[None, :]
    bpos = (np.arange(64) * 64 + 63).astype(np.float32)
    angb = bpos[:, None] * inv[None, :]
    n = np.arange(64)[None, :]
    curb = (t // 64)[:, None]
    valid = (n * 64 + 63 <= t[:, None])
    forced = (n == 0) | (n == curb) | (n == curb - 1)
    invalid = n > curb
    tabs = np.stack([valid.astype(np.float32), (~(forced | invalid)).astype(np.float32),
                     np.where(invalid, -1.0, np.where(forced, 1.0e4, 0.0)).astype(np.float32)], axis=1)
    kk2 = np.arange(128)
    m3 = np.stack([(kk2[:, None] <= kk2[None, :]).astype(np.float32), (kk2[:, None] > kk2[None, :]).astype(np.float32)], axis=1)
    ex = (np.arange(T)[None, :] // 64 == np.arange(64)[:, None]).astype(np.float32)
    return {
        "ropec": np.cos(ang).astype(np.float32), "ropes": np.sin(ang).astype(np.float32),
        "ropeb": np.concatenate([np.cos(angb), np.sin(angb)], axis=1).astype(np.float32),
        "tabs": np.ascontiguousarray(tabs), "c_m3": np.ascontiguousarray(m3), "c_ex": ex,
        "iota": np.arange(128, dtype=np.float32).reshape(128, 1),
        "c_masknew": np.tile((np.arange(8)[:, None] < np.arange(8)[None, :]).astype(np.float32), (1, 8)),
        "ropecs": np.concatenate([np.cos((2048 + (np.arange(128) % 8))[:, None].astype(np.float32) * inv[None, :]),
                                  np.sin((2048 + (np.arange(128) % 8))[:, None].astype(np.float32) * inv[None, :])], axis=1).astype(np.float32),
        "ropeb_s": np.concatenate([np.cos(angb[np.arange(64) % 32]), np.sin(angb[np.arange(64) % 32])], axis=1).astype(np.float32),
        "tabs_s": _tabs_s(),
        "c_ident": np.eye(128, dtype=np.float32),
        "c_ones": np.ones((128, 128), np.float32),
        "c_negtri": negtri,
        "c_negones": -np.ones((128, 128), np.float32),
        "c_div512": np.full((128, 128), 1.0 / 512, np.float32),
        "c_sbmask": sbmask,
    }


def kernel(**inp):
    global _PROG
    if _PROG is None:
        _PROG = build_program()
    P = _PROG
    f = lambda k: np.ascontiguousarray(np.asarray(inp[k], dtype=np.float32))
    cst = _consts()

    def col(v):
        v = np.asarray(v, np.float32)
        return np.ascontiguousarray(v.reshape(-1, 128).T)

    vecs = np.zeros((128, 80), np.float32)
    vecs[:, 0:8] = col(inp["norm_mix"][0])
    vecs[:, 8:16] = col(inp["norm_mix"][1])
    vecs[:, 16:24] = col(inp["norm_mem"][0])
    vecs[:, 24:32] = col(inp["norm_mem"][1])
    vecs[:, 32:40] = col(inp["norm_ffn"][0])
    vecs[:, 40:48] = col(inp["norm_ffn"][1])
    vecs[:, 48:56] = col(inp["final_norm"])
    vecs[:, 56:60] = col(inp["conv_b"][0])
    vecs[:, 60:64] = col(inp["conv_ln_g"][0])
    vecs[:, 64:68] = col(inp["conv_ln_b"][0])
    shared = {
        "vecs": vecs,
        "cwT": np.ascontiguousarray(np.asarray(inp["conv_w"][0], np.float32).T),
        "w_in_even": f("w_in_even")[0],
        "w_in_odd": f("w_in_odd")[0],
        "w1kv": np.ascontiguousarray(np.concatenate([f("cmp_w1_k")[0].transpose(1, 0, 2), f("cmp_w1_v")[0].transpose(1, 0, 2)], axis=0)),
        "w2kv": np.ascontiguousarray(np.stack([f("cmp_w2_k")[0], f("cmp_w2_v")[0]])),
        "peT": np.ascontiguousarray(np.concatenate([f("cmp_pe_k")[0].T, f("cmp_pe_v")[0].T], axis=0)),
        "gfin": f("final_norm").reshape(1, D),
        "w_mix_out": f("w_mix_out"),
        "w_mem_q": f("w_mem_q"), "w_mem_k": f("w_mem_k"), "w_mem_v": f("w_mem_v"), "w_mem_o": f("w_mem_o"),
        "w_ffn_gate": f("w_ffn_gate"), "w_ffn_up": f("w_ffn_up"), "w_ffn_down": f("w_ffn_down"),
    }
    shared.update(cst)
    xpr = f("x_prompt")
    mpr = f("mem_prompt")
    shared["sbk_rows"] = f("cache_sb_k").reshape(2560 * 128, 512)
    shared["sbv_rows"] = f("cache_sb_v").reshape(2560 * 128, 512)
    for nm, key in [("cck_rows", "cache_nsa_cmp_k"), ("ccv_rows", "cache_nsa_cmp_v"), ("csk_rows", "cache_nsa_sel_k"), ("csv_rows", "cache_nsa_sel_v")]:
        shared[nm] = f(key).reshape(2560 * 128, 256)
    cwkr = f("cache_nsa_win_k")[0].reshape(NCORES, 16, 512, 256)
    cwvr = f("cache_nsa_win_v")[0].reshape(NCORES, 16, 512, 256)
    xsr = f("x_sample").reshape(NCORES, 128, D)
    ptr = np.ascontiguousarray(np.asarray(inp["page_table"], np.int32).reshape(NCORES, 1, 256))
    scv = f("state_conv")[0].reshape(NCORES, 16, 30, 512)
    cmkr = f("cache_mem_k").reshape(2, NCORES, 16, 256, D)
    cmvr = f("cache_mem_v").reshape(2, NCORES, 16, 256, D)
    in_maps = []
    for c in range(NCORES):
        b = c // 2
        m = dict(shared)
        m["xp"] = xpr[b]
        m["memp"] = mpr[b]
        m["xs"] = xsr[c]
        m["ptab"] = ptr[c]
        m["sconv"] = scv[c]
        m["cmk"] = np.ascontiguousarray(cmkr[:, c])
        m["cmv"] = np.ascontiguousarray(cmvr[:, c])
        m["cwk"] = cwkr[c]
        m["cwv"] = cwvr[c]
        in_maps.append({k: m[k] for k in P.din})
    res = run_bass_kernel_spmd(P.nc, in_maps, core_ids=list(range(NCORES)))
    R = res.results
    ev = [R[2 * b] for b in range(4)]
    y_prompt = np.stack([r["yp"] for r in ev])
    sb_k_p = np.stack([r["sbkp"] for r in ev]).reshape(1, 4, T, 8, 64)
    sb_v_p = np.stack([r["sbvp"] for r in ev]).reshape(1, 4, T, 8, 64)
    conv_p = np.stack([r["convp"] for r in ev]).reshape(1, 4, 30, 512)
    mem_k_p = np.stack([r["memkp"] for r in ev], axis=1).reshape(2, 4, 256, 4, 256)
    mem_v_p = np.stack([r["memvp"] for r in ev], axis=1).reshape(2, 4, 256, 4, 256)
    st4 = lambda k, n: np.stack([r[k] for r in ev]).reshape(1, 4, n, 4, 64)
    outs = (y_prompt, sb_k_p, sb_v_p, conv_p, st4("cmpkp", T), st4("cmpvp", T), st4("selkp", T), st4("selvp", T),
            st4("winkp", 512), st4("winvp", 512), mem_k_p, mem_v_p,
            np.concatenate([r["ys"] for r in R]).reshape(128, 8, D),
            np.concatenate([r["sbks"] for r in R]).reshape(1, 128, 8, 8, 64),
            np.concatenate([r["sbvs"] for r in R]).reshape(1, 128, 8, 8, 64),
            np.concatenate([r["convs"] for r in R]).reshape(1, 128, 30, 512),
            np.concatenate([r["cmpks"] for r in R]).reshape(1, 128, 8, 4, 64),
            np.concatenate([r["cmpvs"] for r in R]).reshape(1, 128, 8, 4, 64),
            np.concatenate([r["selks"] for r in R]).reshape(1, 128, 8, 4, 64),
            np.concatenate([r["selvs"] for r in R]).reshape(1, 128, 8, 4, 64),
            np.concatenate([r["winks"] for r in R]).reshape(1, 128, 512, 4, 64),
            np.concatenate([r["winvs"] for r in R]).reshape(1, 128, 512, 4, 64))
    full = [None] * 22
    for o, i in zip(outs, OUT_IDX):
        full[i] = np.ascontiguousarray(o, dtype=np.float32)
    return tuple(full)
```
